# Optimizing a Trainium2 kernel written in Bass

```python
import math
import jax, jax.numpy as jnp
from jax import lax
import numpy as np

D_MODEL = 1024
BATCH = 32
SEQ = 2048
DEPTH = 1

FOX_HEADS = 8
FOX_HEAD_DIM = 128
FOX_WIDTH = FOX_HEADS * FOX_HEAD_DIM
FOX_BLOCK = 128
GDN_HEADS = 8
GDN_HEAD_K = 128
GDN_HEAD_V = 128
GDN_K_WIDTH = GDN_HEADS * GDN_HEAD_K
GDN_V_WIDTH = GDN_HEADS * GDN_HEAD_V
GDN_CONV = 4
GDN_CONV_CH = 2 * GDN_K_WIDTH + GDN_V_WIDTH
GDN_CHUNK = 64
D_FF = 2816
FFN_CONV = 3
EPS = 1e-6

IN_SIZES = (FOX_WIDTH, FOX_WIDTH, FOX_WIDTH, FOX_HEADS,
            GDN_K_WIDTH, GDN_K_WIDTH, GDN_V_WIDTH, GDN_HEADS, GDN_HEADS, GDN_V_WIDTH,
            D_MODEL, D_MODEL)
D_IN = sum(IN_SIZES)

kernel_name = "fox_gdn_parallel_hybrid_convffn"


def rmsnorm(x, g):
    xf = x.astype(jnp.float32)
    xf = xf * lax.rsqrt(jnp.mean(xf * xf, axis=-1, keepdims=True) + EPS)
    return (xf * g.astype(jnp.float32)).astype(x.dtype)


def l2norm(x):
    xf = x.astype(jnp.float32)
    return xf * lax.rsqrt(jnp.sum(xf * xf, axis=-1, keepdims=True) + EPS)


def causal_dwconv(x, w):
    K = w.shape[0]
    S = x.shape[1]
    xp = jnp.pad(x, ((0, 0), (K - 1, 0), (0, 0)))
    return sum(xp[:, i:i + S, :] * w[i] for i in range(K))


def split_in(h):
    idx = np.cumsum(np.array(IN_SIZES))[:-1].tolist()
    return jnp.split(h, idx, axis=-1)


def fox_attention(q, k, v, log_f):
    B, S, H, Dh = q.shape
    nb = S // FOX_BLOCK
    scale = Dh ** -0.5
    c = jnp.cumsum(log_f.astype(jnp.float32), axis=1).transpose(0, 2, 1)
    kh = k.transpose(0, 2, 1, 3)
    vh = v.transpose(0, 2, 1, 3)
    qb = q.reshape(B, nb, FOX_BLOCK, H, Dh).transpose(1, 0, 3, 2, 4)
    cb = c.reshape(B, H, nb, FOX_BLOCK).transpose(2, 0, 1, 3)
    key_pos = jnp.arange(S)

    def block(args):
        qi, ci, i = args
        s = jnp.einsum('bhqd,bhkd->bhqk', qi, kh).astype(jnp.float32) * scale
        s = s + (ci[..., :, None] - c[..., None, :])
        q_pos = i * FOX_BLOCK + jnp.arange(FOX_BLOCK)
        causal = key_pos[None, :] <= q_pos[:, None]
        p = jax.nn.softmax(jnp.where(causal, s, -jnp.inf), axis=-1)
        return jnp.einsum('bhqk,bhkd->bhqd', p.astype(vh.dtype), vh)

    o = lax.map(block, (qb, cb, jnp.arange(nb)))
    return o.transpose(1, 0, 3, 2, 4).reshape(B, S, H * Dh)


def gated_delta_rule(q, k, v, g, beta):
    B, S, H, dk = q.shape
    dv = v.shape[-1]
    C = GDN_CHUNK
    N = S // C
    f32 = jnp.float32

    def chunk4(t):
        return t.astype(f32).reshape(B, N, C, H, t.shape[-1]).transpose(0, 3, 1, 2, 4)

    def chunk3(t):
        return t.astype(f32).reshape(B, N, C, H).transpose(0, 3, 1, 2)

    qc = chunk4(q) * (dk ** -0.5)
    kc = chunk4(k)
    vc = chunk4(v)
    bc = chunk3(beta)
    gc = jnp.cumsum(chunk3(g), axis=-1)

    tri_incl = jnp.tril(jnp.ones((C, C), dtype=bool))
    tri_strict = jnp.tril(jnp.ones((C, C), dtype=bool), k=-1)
    diff = gc[..., :, None] - gc[..., None, :]
    decay = jnp.exp(jnp.where(tri_incl, diff, -jnp.inf))

    kb = kc * bc[..., None]
    A = jnp.where(tri_strict, jnp.einsum('bhncd,bhnmd->bhncm', kb, kc) * decay, 0.0)
    lhs = A + jnp.eye(C, dtype=f32)
    rhs = jnp.concatenate([vc * bc[..., None], kb * jnp.exp(gc)[..., None]], axis=-1)
    sol = lax.linalg.triangular_solve(lhs, rhs, left_side=True, lower=True, unit_diagonal=True)
    u_hat, w = sol[..., :dv], sol[..., dv:]

    attn = jnp.einsum('bhncd,bhnmd->bhncm', qc, kc) * decay
    q_dec = qc * jnp.exp(gc)[..., None]
    k_dec = kc * jnp.exp(gc[..., -1:] - gc)[..., None]
    g_last = jnp.exp(gc[..., -1])

    xs = tuple(jnp.moveaxis(t, 2, 0) for t in (u_hat, w, attn, q_dec, k_dec, g_last))

    def step(state, inp):
        u_hat_i, w_i, attn_i, q_dec_i, k_dec_i, gl_i = inp
        u = u_hat_i - jnp.einsum('bhcd,bhdv->bhcv', w_i, state)
        o = jnp.einsum('bhcd,bhdv->bhcv', q_dec_i, state) + jnp.einsum('bhcm,bhmv->bhcv', attn_i, u)
        state = state * gl_i[..., None, None] + jnp.einsum('bhcd,bhcv->bhdv', k_dec_i, u)
        return state, o

    s0 = jnp.zeros((B, H, dk, dv), f32)
    _, o = lax.scan(step, s0, xs)
    return o.transpose(1, 0, 3, 2, 4).reshape(B, S, H, dv).astype(v.dtype)


def setup_inputs(seed: int = 0) -> dict:
    key = jax.random.key(seed)
    ks = jax.random.split(key, 20)
    L, D = DEPTH, D_MODEL
    nrm = jax.random.normal
    x = nrm(ks[0], (BATCH, SEQ, D), jnp.float32)
    norm_mix = 1.0 + 0.05 * nrm(ks[1], (L, D), jnp.float32)
    w_in = nrm(ks[2], (L, D, D_IN), jnp.float32) * D ** -0.5
    fox_f_bias = 3.0 + 0.5 * nrm(ks[3], (L, FOX_HEADS), jnp.float32)
    gdn_conv_w = nrm(ks[4], (L, GDN_CONV, GDN_CONV_CH), jnp.float32) * GDN_CONV ** -0.5
    gdn_a_log = jnp.log(jax.random.uniform(ks[5], (L, GDN_HEADS), jnp.float32, 1.0, 16.0))
    dt = jnp.exp(jax.random.uniform(ks[6], (L, GDN_HEADS), jnp.float32, math.log(1e-3), math.log(1e-1)))
    gdn_dt_bias = dt + jnp.log(-jnp.expm1(-dt))
    gdn_norm = 1.0 + 0.05 * nrm(ks[7], (L, GDN_HEAD_V), jnp.float32)
    w_branch_fox = nrm(ks[8], (L, FOX_WIDTH, D), jnp.float32) * FOX_WIDTH ** -0.5
    w_branch_gdn = nrm(ks[9], (L, GDN_V_WIDTH, D), jnp.float32) * GDN_V_WIDTH ** -0.5
    w_out = nrm(ks[10], (L, D, D), jnp.float32) * D ** -0.5
    norm_ffn = 1.0 + 0.05 * nrm(ks[11], (L, D), jnp.float32)
    w_up = nrm(ks[12], (L, D, 2 * D_FF), jnp.float32) * D ** -0.5
    ffn_conv_w = nrm(ks[13], (L, FFN_CONV, 2 * D_FF), jnp.float32) * FFN_CONV ** -0.5
    w_down = nrm(ks[14], (L, D_FF, D), jnp.float32) * D_FF ** -0.5
    norm_final = 1.0 + 0.05 * nrm(ks[15], (D,), jnp.float32)
    return {"x": x, "norm_mix": norm_mix, "w_in": w_in, "fox_f_bias": fox_f_bias,
            "gdn_conv_w": gdn_conv_w, "gdn_a_log": gdn_a_log, "gdn_dt_bias": gdn_dt_bias,
            "gdn_norm": gdn_norm, "w_branch_fox": w_branch_fox, "w_branch_gdn": w_branch_gdn,
            "w_out": w_out, "norm_ffn": norm_ffn, "w_up": w_up, "ffn_conv_w": ffn_conv_w,
            "w_down": w_down, "norm_final": norm_final}


def reference(x, norm_mix, w_in, fox_f_bias, gdn_conv_w, gdn_a_log, gdn_dt_bias, gdn_norm,
              w_branch_fox, w_branch_gdn, w_out, norm_ffn, w_up, ffn_conv_w, w_down, norm_final):
    B, S, D = x.shape
    h = x
    for l in range(DEPTH):
        hn = rmsnorm(h, norm_mix[l])
        proj = jnp.einsum('bsd,de->bse', hn, w_in[l])
        (fq, fk, fv, ff, gq, gk, gv, ga, gb, gz, gate_fox, gate_gdn) = split_in(proj)

        log_f = jax.nn.log_sigmoid(ff.astype(jnp.float32) + fox_f_bias[l].astype(jnp.float32))
        y_fox = fox_attention(fq.reshape(B, S, FOX_HEADS, FOX_HEAD_DIM),
                              fk.reshape(B, S, FOX_HEADS, FOX_HEAD_DIM),
                              fv.reshape(B, S, FOX_HEADS, FOX_HEAD_DIM), log_f)

        qkv = jax.nn.silu(causal_dwconv(jnp.concatenate([gq, gk, gv], axis=-1), gdn_conv_w[l]))
        cq, ck, cv = jnp.split(qkv, [GDN_K_WIDTH, 2 * GDN_K_WIDTH], axis=-1)
        q = l2norm(cq.reshape(B, S, GDN_HEADS, GDN_HEAD_K))
        k = l2norm(ck.reshape(B, S, GDN_HEADS, GDN_HEAD_K))
        v = cv.reshape(B, S, GDN_HEADS, GDN_HEAD_V)
        g = -jnp.exp(gdn_a_log[l].astype(jnp.float32)) * jax.nn.softplus(
            ga.astype(jnp.float32) + gdn_dt_bias[l].astype(jnp.float32))
        beta = jax.nn.sigmoid(gb.astype(jnp.float32))
        o = gated_delta_rule(q, k, v, g, beta)
        o = rmsnorm(o, gdn_norm[l]) * jax.nn.silu(gz.reshape(B, S, GDN_HEADS, GDN_HEAD_V))
        y_gdn = o.reshape(B, S, GDN_V_WIDTH)

        y = (jax.nn.sigmoid(gate_fox) * jnp.einsum('bse,ed->bsd', y_fox, w_branch_fox[l])
             + jax.nn.sigmoid(gate_gdn) * jnp.einsum('bse,ed->bsd', y_gdn, w_branch_gdn[l]))
        h = h + jnp.einsum('bsd,de->bse', y, w_out[l])

        hn = rmsnorm(h, norm_ffn[l])
        up = causal_dwconv(jnp.einsum('bsd,df->bsf', hn, w_up[l]), ffn_conv_w[l])
        u_gate, u_val = jnp.split(up, 2, axis=-1)
        h = h + jnp.einsum('bsf,fd->bsd', jax.nn.silu(u_gate) * u_val, w_down[l])
    return rmsnorm(h, norm_final)
```

```python
import numpy as np
import concourse.bass as bass
import concourse.mybir as mybir
from concourse.bass_utils import run_bass_kernel_spmd

F32 = mybir.dt.float32
BF16 = mybir.dt.bfloat16
AF = mybir.ActivationFunctionType
ALU = mybir.AluOpType
AX = mybir.AxisListType

D = 1024
H = 8
DH = 128
DFF = 2816
NF = 22
OFF_FQ, OFF_FK, OFF_FV, OFF_FF = 0, 1024, 2048, 3072
OFF_GQ, OFF_GK, OFF_GV, OFF_GA, OFF_GB, OFF_GZ = 3080, 4104, 5128, 6152, 6160, 6168
OFF_GF, OFF_GG = 7192, 8216
DIN = 9240
EPS = 1e-6
NEG = -30000.0
NCB = 18
import os
SCHED_DEBUG = bool(os.environ.get("SCHED_DEBUG"))
ATTACH_WAITS = False
N_SW_SEMS = 2


class TT:
    __slots__ = ("ap", "w", "r", "name", "pg")

    def __init__(self, ap, name=""):
        self.ap = ap
        self.w = {}
        self.r = []
        self.pg = []
        self.name = name

    def __getitem__(self, idx):
        return self.ap[idx]


class TTV:
    __slots__ = ("p", "ap", "name")

    def __init__(self, p, ap):
        self.p = p
        self.ap = ap
        self.name = p.name + "_v"

    def __getitem__(self, idx):
        return self.ap[idx]

    w = property(lambda self: self.p.w, lambda self, v: setattr(self.p, "w", v))
    r = property(lambda self: self.p.r, lambda self, v: setattr(self.p, "r", v))
    pg = property(lambda self: self.p.pg, lambda self, v: setattr(self.p, "pg", v))


class _Op:
    __slots__ = ("id", "eng", "fn", "deps", "dur", "lat", "is_dma", "multi", "seg", "users", "nun", "start", "fin",
                 "needs_inc", "incval", "dsem", "dcount", "dprev", "waits", "qpos")


def _free_n(ap):
    n = 1
    for d in ap.shape[1:]:
        n *= d
    return n


class Prog:
    ENGS = ("pe", "act", "dve", "pool", "sp")
    NDS = 32
    WIN = 48
    XLAT = 60.0

    def __init__(self, nc, same_sync=True):
        self.nc = nc
        self.sems = []
        self._ctx = []
        self.esem = {n: self._newsem("e_" + n) for n in self.ENGS}
        self.dsems_ = [self._newsem("dq%d" % i) for i in range(self.NDS)]
        self.ops = []
        self.seg = 0
        self.nsegs = 1
        self.seginfo = []

    def _newsem(self, name):
        cm = self.nc.semaphore(name)
        h = cm.__enter__()
        self._ctx.append(cm)
        self.sems.append(h)
        return len(self.sems) - 1

    def dsem(self, name=None):
        return True

    def sbuf(self, name, shape, dtype):
        cm = self.nc.sbuf_tensor(name, shape, dtype)
        t = cm.__enter__()
        self._ctx.append(cm)
        return t

    def psum(self, name, shape, dtype):
        cm = self.nc.psum_tensor(name, shape, dtype)
        t = cm.__enter__()
        self._ctx.append(cm)
        return t

    def st(self, name, shape, dtype):
        return TT(self.sbuf(name, shape, dtype)[:], name)

    def barrier(self):
        self.seg += 1
        self.nsegs = self.seg + 1

    def op(self, eng, fn, reads=(), writes=(), inc=True, dsem=None, key=None, dur=100.0, lat=None, multi=False):
        o = _Op()
        o.id = len(self.ops)
        o.eng = eng
        o.fn = fn
        o.dur = dur
        o.is_dma = dsem is not None
        o.lat = lat if lat is not None else dur
        o.multi = multi
        o.seg = self.seg
        deps = {}

        def add(i, raw):
            if i is None:
                return
            if raw or i not in deps:
                deps[i] = raw or deps.get(i, False)

        for t in reads:
            if t is None:
                continue
            for i in t.w.values():
                add(i, True)
        for t in writes:
            if t is None:
                continue
            if key is None:
                for i in t.w.values():
                    add(i, False)
                for i in t.r:
                    add(i, False)
                t.w = {None: o.id}
                t.r = []
                t.pg = []
            else:
                if t.r:
                    t.pg = list(t.r) + list(t.w.values())
                    for i in t.pg:
                        add(i, False)
                    t.w = {key: o.id}
                    t.r = []
                else:
                    add(t.w.get(key), False)
                    add(t.w.get(None), False)
                    for i in t.pg:
                        add(i, False)
                    t.w[key] = o.id
        for t in reads:
            if t is None or t in writes:
                continue
            t.r.append(o.id)
        o.deps = deps
        self.ops.append(o)
        return o.id

    def mm(self, ot, out, lhsT, rhs, reads, start=True, stop=True, inc=True, skip=False, key=None):
        n = _free_n(rhs)
        f = 4.0 if rhs.dtype == F32 else 1.0
        self.op("pe", lambda e: e.matmul(out, lhsT=lhsT, rhs=rhs, start=start, stop=stop, skip_group_check=skip),
                reads=reads, writes=[ot], key=key, dur=f * max(n, 64) / 2.4 + 6.0)

    def tr(self, ot, out, in_, ident, reads, inc=True, key=None):
        f = 4.0 if in_.dtype == F32 else 1.0
        self.op("pe", lambda e: e.transpose(out=out, in_=in_, identity=ident), reads=reads, writes=[ot], key=key, dur=f * 64 / 2.4 + 6.0)

    def act(self, ot, out, in_, func, reads, bias=None, scale=None, accum=None, extra_w=(), key=None):
        kw = {}
        d = 224.0 + _free_n(in_) / 1.2
        if bias is not None:
            kw["bias"] = bias
            if not isinstance(bias, float):
                d += 93
        if scale is not None:
            kw["scale"] = scale
        if accum is not None:
            kw["accum_out"] = accum
            d += 93
        self.op("act", lambda e: e.activation(out=out, in_=in_, func=func, **kw), reads=reads, writes=[ot] + list(extra_w),
                key=key, dur=d, multi=(accum is not None))

    def _vd(self, eng, ap):
        n = _free_n(ap)
        return (60.0 + n / 0.96) if eng == "dve" else (130.0 + n / 0.5)

    def tsc(self, eng, ot, out, in0, s1, s2, op0, op1, reads, key=None):
        if op1 is None:
            self.op(eng, lambda e: e.tensor_scalar(out=out, in0=in0, scalar1=s1, scalar2=None, op0=op0), reads=reads, writes=[ot], key=key, dur=self._vd(eng, in0))
        else:
            self.op(eng, lambda e: e.tensor_scalar(out=out, in0=in0, scalar1=s1, scalar2=s2, op0=op0, op1=op1), reads=reads, writes=[ot], key=key, dur=self._vd(eng, in0))

    def stt(self, ot, out, in0, scalar, in1, op0, op1, reads, key=None):
        self.op("dve", lambda e: e.scalar_tensor_tensor(out=out, in0=in0, scalar=scalar, in1=in1, op0=op0, op1=op1), reads=reads, writes=[ot], key=key, dur=self._vd("dve", in0))

    def tt(self, eng, ot, out, in0, in1, op, reads, key=None):
        self.op(eng, lambda e: e.tensor_tensor(out=out, in0=in0, in1=in1, op=op), reads=reads, writes=[ot], key=key, dur=self._vd(eng, out))

    def cp(self, eng, ot, out, in_, reads, key=None):
        self.op(eng, lambda e: e.tensor_copy(out=out, in_=in_), reads=reads, writes=[ot], key=key, dur=self._vd(eng, in_))

    def dma(self, eng, ot, out, in_, reads, dsem, write=True, key=None):
        n = 1
        for d in out.shape:
            n *= d
        nbytes = n * (4 if out.dtype == F32 else 2)
        self.op(eng, lambda e: e.dma_start(out=out, in_=in_), reads=reads, writes=([ot] if write else []), dsem=True, key=key,
                dur=60.0, lat=2200.0 + nbytes / 60.0)

    def _schedule_segment(self, ids, t0):
        ops = self.ops
        inseg = set(ids)
        q = {n: [] for n in self.ENGS}
        for i in ids:
            o = ops[i]
            o.deps = {d: r for d, r in o.deps.items() if d in inseg}
            o.nun = len(o.deps)
            o.users = []
            o.start = None
            q[o.eng].append(i)
        for i in ids:
            for d in ops[i].deps:
                ops[d].users.append(i)
        head = {n: 0 for n in self.ENGS}
        free = {n: t0 for n in self.ENGS}
        order = {n: [] for n in self.ENGS}
        cand = {n: None for n in self.ENGS}
        dirty = set(self.ENGS)
        remaining = len(ids)
        WIN = self.WIN
        XL = self.XLAT

        def rescan(n):
            lst = q[n]
            h = head[n]
            while h < len(lst) and ops[lst[h]].start is not None:
                h += 1
            head[n] = h
            best = None
            cnt = 0
            j = h
            fr = free[n]
            while j < len(lst) and cnt < WIN:
                o = ops[lst[j]]
                j += 1
                if o.start is not None:
                    continue
                cnt += 1
                if o.nun:
                    continue
                rd = fr
                for d in o.deps:
                    od = ops[d]
                    f = od.fin + (XL if od.eng != n or od.is_dma else 0.0)
                    if f > rd:
                        rd = f
                if best is None or rd < best[0]:
                    best = (rd, o.id)
                    if rd <= fr:
                        break
            cand[n] = best

        while remaining:
            for n in dirty:
                rescan(n)
            dirty = set()
            bn = None
            for n in self.ENGS:
                c = cand[n]
                if c is not None and (bn is None or c < cand[bn]):
                    bn = n
            if bn is None:
                raise RuntimeError("scheduler stuck: dependency cycle / window too small")
            st, i = cand[bn]
            o = ops[i]
            o.start = st
            o.fin = st + o.lat
            free[bn] = st + o.dur
            o.qpos = len(order[bn])
            order[bn].append(i)
            remaining -= 1
            dirty.add(bn)
            for u in o.users:
                ou = ops[u]
                ou.nun -= 1
                if ou.nun == 0:
                    dirty.add(ou.eng)
        tend = max([free[n] for n in self.ENGS] + [ops[i].fin for i in ids]) if ids else t0
        return order, tend

    def emit(self):
        nc = self.nc
        ops = self.ops
        segs = [[] for _ in range(self.nsegs)]
        for o in ops:
            segs[o.seg].append(o.id)
        ecount = {n: 0 for n in self.ENGS}
        dcount = [0] * self.NDS
        dnext = 0
        dnext_sw = 0
        known = {n: {} for n in self.ENGS}
        streams = {n: [] for n in self.ENGS}
        t0 = 0.0
        for sids in segs:
            tprev = t0
            order, t0 = self._schedule_segment(sids, t0)
            if SCHED_DEBUG:
                busy = {n: sum(ops[i].dur for i in order[n]) for n in self.ENGS}
                print("seg %3d n=%5d len_us=%8.1f  " % (len(self.seginfo), len(sids), (t0 - tprev) / 1e3) + " ".join("%s=%.0f" % (n, busy[n] / 1e3) for n in self.ENGS))
            self.seginfo.append(t0 - tprev)
            for i in sids:
                o = ops[i]
                o.needs_inc = False
            for i in sids:
                o = ops[i]
                for d, raw in o.deps.items():
                    od = ops[d]
                    if od.is_dma:
                        continue
                    if od.eng != o.eng or raw or o.eng != "pe":
                        od.needs_inc = True
            for n in self.ENGS:
                for i in reversed(order[n]):
                    if not ops[i].is_dma:
                        ops[i].needs_inc = True
                        break
            for n in self.ENGS:
                for i in order[n]:
                    o = ops[i]
                    if o.is_dma:
                        if n == "pool":
                            k = self.NDS - 8 + (dnext_sw % N_SW_SEMS)
                            dnext_sw += 1
                        else:
                            k = dnext % 16
                            dnext += 1
                        o.dsem = k
                        o.dprev = dcount[k]
                        dcount[k] += 16
                        o.dcount = dcount[k]
                    elif o.needs_inc:
                        ecount[n] += 1
                        o.incval = ecount[n]
            for n in self.ENGS:
                kn = known[n]
                for i in order[n]:
                    o = ops[i]
                    w = {}
                    for d, raw in o.deps.items():
                        od = ops[d]
                        if od.is_dma:
                            s, v = self.dsems_[od.dsem], od.dcount
                        elif od.eng != n or raw or n != "pe":
                            s, v = self.esem[od.eng], od.incval
                        else:
                            continue
                        if w.get(s, 0) < v:
                            w[s] = v
                    if o.is_dma and o.dprev:
                        s = self.dsems_[o.dsem]
                        if w.get(s, 0) < o.dprev:
                            w[s] = o.dprev
                    wl = []
                    for s, v in w.items():
                        if kn.get(s, 0) < v:
                            kn[s] = v
                            wl.append((s, v))
                    if o.is_dma:
                        incspec = (self.dsems_[o.dsem], 16)
                    elif o.needs_inc:
                        incspec = (self.esem[n], 1)
                    else:
                        incspec = None
                    streams[n].append((wl, o.fn, incspec, ATTACH_WAITS and not o.multi and not o.is_dma))
            toks = [(self.esem[n], ecount[n]) for n in self.ENGS if ecount[n]]
            toks += [(self.dsems_[k], dcount[k]) for k in range(self.NDS) if dcount[k]]
            for n in self.ENGS:
                wl = []
                for s, v in toks:
                    if s != self.esem[n] and known[n].get(s, 0) < v:
                        known[n][s] = v
                        wl.append((s, v))
                if wl:
                    streams[n].append((wl, None, None, False))
        self.sim_end = t0
        with nc.Block() as block:
            def mk(n):
                def body(e):
                    for wl, fn, incspec, attach in streams[n]:
                        if fn is None:
                            for s, v in wl:
                                e.wait_ge(self.sems[s], v)
                            continue
                        rest = wl
                        first = None
                        if attach and wl:
                            first = wl[0]
                            rest = wl[1:]
                        for s, v in rest:
                            e.wait_ge(self.sems[s], v)
                        ins = fn(e)
                        if first is not None:
                            ins._wait_ge(self.sems[first[0]], first[1])
                        if incspec is not None:
                            ins.then_inc(self.sems[incspec[0]], incspec[1])
                return body
            for n, f in (("sp", block.sync), ("pe", block.tensor), ("act", block.scalar), ("dve", block.vector), ("pool", block.gpsimd)):
                if streams[n]:
                    f(mk(n))

    def close(self):
        for cm in reversed(self._ctx):
            cm.__exit__(None, None, None)
        self._ctx = []


def host_consts():
    p = np.arange(128)[:, None]
    c = np.arange(128)[None, :]
    cb = np.zeros((128, NCB, 128), np.float32)
    cb[:, 0, :] = np.eye(128)
    cb[:, 1, :] = np.where(p > c, NEG, 0.0)
    cb[:, 2, :] = np.where(p >= c, NEG, 0.0)
    cb[:, 3, :] = np.where(c >= p, NEG, 0.0)
    for l in range(7):
        b = 1 << l
        m = ((p // (2 * b)) == (c // (2 * b))) & ((p % (2 * b)) >= b) & ((c % (2 * b)) < b)
        cb[:, 4 + l, :] = m
        cb[:, 11 + l, :] = m.T
    sel_last = np.zeros((128, 128), np.float32)
    sel_last[127, :] = 1.0
    sb = np.zeros((128, 2, 8, 6), np.float32)
    for h in range(8):
        for pc in range(3):
            sb[32 * pc + h, 0, h, pc] = -1.0
            sb[32 * pc + h, 1, h, 3 + pc] = 1.0
        sb[96, 0, h, 3:6] = 1.0
        sb[96, 1, h, 0:3] = 1.0
    return cb, sel_last, sb.reshape(128, 96)


class Arena:
    def __init__(self, P, nbytes):
        self.t = P.sbuf("arena", [128, nbytes // 2], BF16)
        self.n = nbytes
        self.off = 0

    def reset(self):
        self.off = 0

    def alloc(self, name, shape, dtype):
        esz = 4 if dtype == F32 else 2
        n = 1
        for d in shape[1:]:
            n *= d
        nb = (n * esz + 63) // 64 * 64
        if self.off + nb > self.n:
            raise RuntimeError("arena overflow at %s: %d + %d > %d" % (name, self.off, nb, self.n))
        a = self.off // 2
        ap = self.t[0:shape[0], a:a + n * esz // 2]
        if dtype == F32:
            ap = ap.bitcast(F32)
        if len(shape) == 3:
            ap = ap.rearrange("p (a b) -> p a b", a=shape[1])
        elif len(shape) == 4:
            ap = ap.rearrange("p (a b c) -> p a b c", a=shape[1], b=shape[2])
        self.off += nb
        return TT(ap, name)


def bc_last(ap2, n):
    return ap2.rearrange("p (h o) -> p h o", o=1).broadcast_to([ap2.shape[0], ap2.shape[1], n])


def bc_mid(ap2, h):
    return ap2.rearrange("p (o n) -> p o n", o=1).broadcast_to([ap2.shape[0], h, ap2.shape[1]])


def build(T, NSEQ, debug=False, arena_bytes=100 * 1024):
    NT = T // 128
    BW = min(512, T)
    NB = T // BW
    QPB = BW // 128
    nc = bass.Bass("TRN2", target_bir_lowering=False)

    def din(name, shape, dt=F32):
        return nc.dram_tensor(name, shape, dt, kind="ExternalInput").ap()

    x_d = din("x", [NSEQ, T, D])
    win_d = din("w_in", [D, DIN])
    wbf_d = din("w_bf", [D, D])
    wbg_d = din("w_bg", [D, D])
    wout_d = din("w_out", [D, D])
    wup_d = din("w_up", [D, 2 * DFF])
    wdn_d = din("w_down", [DFF, D])
    nrm_d = din("norms", [3, D])
    gnorm_d = din("gdn_norm", [1, DH])
    cwg_d = din("cwg", [128, 24, 4])
    cwf_d = din("cwf", [128, 44, 3])
    hv_d = din("hv", [8, 3])
    cb_d = din("cb", [128, NCB, 128])
    sl_d = din("sel_last", [128, 128])
    sb_d = din("selbig", [128, 96])
    out_d = nc.dram_tensor("out", [NSEQ, T, D], F32, kind="ExternalOutput").ap()
    if debug:
        dbg_fox = nc.dram_tensor("dbg_fox", [128, H, T], BF16, kind="ExternalOutput").ap()
        dbg_gdn = nc.dram_tensor("dbg_gdn", [128, H, T], BF16, kind="ExternalOutput").ap()

    def scratch(name, shape):
        return nc.dram_tensor(name, shape, BF16, kind="Internal").ap()

    wb_in = scratch("wb_in", [D, DIN])
    wb_bf = scratch("wb_bf", [D, D])
    wb_bg = scratch("wb_bg", [D, D])
    wb_out = scratch("wb_out", [D, D])
    wb_up = scratch("wb_up", [D, 2 * DFF])
    wb_dn = scratch("wb_dn", [DFF, D])

    P = Prog(nc)
    S = P.st
    cbb = S("cbb", [128, NCB, 128], BF16)
    mkw = S("mkw", [128, 512], BF16)
    ones_b = S("ones_b", [128, 128], BF16)
    idf8 = S("idf8", [8, 8], F32)
    sl_f = S("sl_f", [128, 128], F32)
    sb_b = S("sb_b", [128, 96], BF16)
    gnorm = S("gnorm", [128, DH], F32)
    cwg = S("cwg_s", [128, 24, 4], F32)
    cwf = S("cwf_s", [128, 44, 3], F32)
    hv = S("hv_s", [8, 8], F32)
    wsm = S("wsm", [128, 8, 24], BF16)
    onesr = S("onesr", [8, 128], F32)
    hnT = S("hnT", [128, 8, T], BF16)
    yfoxT = S("yfoxT", [128, 8, T], BF16)
    ygdnT = S("ygdnT", [128, 8, T], BF16)

    def alloc_pk():
        AR.reset()
        pk_ = {n: AR.alloc("PK_" + n, [128, T], BF16) for n in ("c", "L", "LB")}
        cols_ = AR.alloc("cols", [128, NT, 5, 8], F32)
        return pk_, cols_
    AR = Arena(P, arena_bytes)

    pf = [TT(P.psum("pf%d" % i, [128, 512], F32)[:], "pf%d" % i) for i in range(8)]
    pb = [TTV(pf[6 + i], pf[6 + i][:].bitcast(BF16)) for i in range(2)]
    ident_b = cbb[:, 0, :]

    dW = P.dsem("dW")
    wtt = {}
    for name, src, dst, rows_ in (("in", win_d, wb_in, D), ("bf", wbf_d, wb_bf, D), ("bg", wbg_d, wb_bg, D),
                                  ("out", wout_d, wb_out, D), ("up", wup_d, wb_up, D), ("dn", wdn_d, wb_dn, DFF)):
        t = TT(dst, "wb_" + name)
        wtt[name] = t
        for r0 in range(0, rows_, 128):
            P.dma("pool", t, dst[r0:r0 + 128, :], src[r0:r0 + 128, :], [], dW, key=r0)
    dC = P.dsem("dC")
    cbf = AR.alloc("cbf", [128, NCB, 128], F32)
    sb_f = AR.alloc("sb_f", [128, 96], F32)
    P.dma("sp", cbf, cbf[:], cb_d, [], dC)
    P.dma("sp", sb_f, sb_f[:], sb_d, [], dC)
    P.dma("sp", sl_f, sl_f[:], sl_d, [], dC)
    P.dma("sp", idf8, idf8[:], cb_d[0:8, 0, 0:8], [], dC)
    P.dma("sp", gnorm, gnorm[:], gnorm_d.partition_broadcast(128), [], dC)
    P.dma("sp", cwg, cwg[:], cwg_d, [], dC)
    P.dma("sp", cwf, cwf[:], cwf_d, [], dC)
    P.dma("sp", hv, hv[:, 0:3], hv_d, [], dC)
    P.cp("dve", cbb, cbb[:], cbf[:], [cbf])
    P.cp("dve", sb_b, sb_b[:], sb_f[:], [sb_f])
    P.op("dve", lambda e: e.memset(mkw[:], 0.0), writes=[mkw])
    P.cp("dve", mkw, mkw[:, 0:128], cbb[:, 1, :], [cbb])
    P.op("dve", lambda e: e.memset(ones_b[:], 1.0), writes=[ones_b])
    P.op("dve", lambda e: e.memset(onesr[:], 1.0), writes=[onesr])
    P.tsc("dve", hv, hv[:, 3:4], hv[:, 0:1], -1.0, None, ALU.mult, None, [hv])
    P.act(hv, hv[:, 4:5], hv[:, 1:2], AF.Exp, [hv])
    dS = P.dsem("dS")
    for i, off in enumerate((OFF_FF, OFF_GA, OFF_GB)):
        P.dma("sp", wsm, wsm[:, :, i * 8:(i + 1) * 8], wb_in[:, off:off + 8].rearrange("(c p) n -> p c n", p=128), [wtt["in"]], dS, key=i)

    def selbig(kind, h):
        o = (kind * 8 + h) * 6
        return sb_b[:, o:o + 6]

    dX = [P.dsem("dX%d" % i) for i in range(2)]
    dWh = [P.dsem("dWh%d" % i) for i in range(8)]
    dPK = P.dsem("dPK")
    dOut = P.dsem("dOut")
    dG = P.dsem("dG")
    dTail = [P.dsem("dTl%d" % i) for i in range(12)]

    def wslice(wb, c0, n):
        return wb[:, c0:c0 + n].rearrange("(c p) n -> p c n", p=128)

    def rstd_from_ss(stt_, cin, cout, n, scale):
        P.tsc("dve", stt_, stt_[:, cout:cout + n], stt_[:, cin:cin + n], scale, EPS, ALU.mult, ALU.add, [stt_])
        P.act(stt_, stt_[:, cout:cout + n], stt_[:, cout:cout + n], AF.Ln, [stt_])
        P.act(stt_, stt_[:, cout:cout + n], stt_[:, cout:cout + n], AF.Exp, [stt_], scale=-0.5)

    def rmsnorm_tile(xin, xin_tt, gam_ap, gam_tt, stt_, junk, outb):
        P.act(junk, junk[:], xin, AF.Square, [xin_tt], accum=stt_[:, 0:1], extra_w=[stt_])
        rstd_from_ss(stt_, 0, 1, 1, 1.0 / D)
        P.stt(outb, outb[:], xin, stt_[:, 1:2], gam_ap, ALU.mult, ALU.mult, [xin_tt, stt_, gam_tt])

    xcnt = [0]
    P.barrier()

    for s in range(NSEQ):
        AR.reset()
        gam1 = AR.alloc("gam1", [128, D], F32)
        xt = [AR.alloc("xt%d" % i, [128, D], F32) for i in range(2)]
        sqj = AR.alloc("sqj", [128, D], F32)
        hnb = [AR.alloc("hnb%d" % i, [128, D], BF16) for i in range(2)]
        st4 = [AR.alloc("st4_%d" % i, [128, 8], F32) for i in range(2)]
        P.dma("sp", gam1, gam1[:], nrm_d[0:1, :].partition_broadcast(128), [], dG)
        for tt_ in range(NT):
            i = xcnt[0] % 2
            xcnt[0] += 1
            P.dma("sp", xt[i], xt[i][:], x_d[s, tt_ * 128:(tt_ + 1) * 128, :], [], dX[i])
            rmsnorm_tile(xt[i][:], xt[i], gam1[:], gam1, st4[i], sqj, hnb[i])
            for half in range(2):
                pbt = pb[half]
                for c in range(4):
                    cc = half * 4 + c
                    P.tr(pbt, pbt[:, c * 128:(c + 1) * 128], hnb[i][:, cc * 128:(cc + 1) * 128], ident_b, [hnb[i], cbb], inc=(c == 3), key=c)
                P.act(hnT, hnT[:, half * 4:half * 4 + 4, tt_ * 128:(tt_ + 1) * 128],
                      pbt[:, 0:512].rearrange("p (c n) -> p c n", c=4), AF.Copy, [pbt], key=(tt_, half))
        P.barrier()

        PK, cols = alloc_pk()
        for n in PK:
            P.op("pool", lambda e, n=n: e.memset(PK[n][:], 1.0), writes=[PK[n]], dur=4000.0)
        rows = {n: AR.alloc("row_" + n, [8, T], F32) for n in ("nlf", "ncum", "negg", "nlb", "L", "LB", "r1", "r2")}
        rows["tmp"] = rows["r1"]
        pcs = [AR.alloc("pc%d" % i, [8, T], BF16) for i in range(3)] * 3
        nlf, ncum, negg, nlb, Lr, LBr, tmp = (rows[n] for n in ("nlf", "ncum", "negg", "nlb", "L", "LB", "tmp"))
        Lc = AR.alloc("Lc", [128, 8], F32)
        nlc = AR.alloc("nlc", [128, 8], F32)
        t8 = AR.alloc("t8", [128, 8], F32)
        for b in range(NB):
            cs = slice(b * BW, (b + 1) * BW)
            for i in range(3):
                pst = pf[i]
                for c in range(8):
                    P.mm(pst, pst[0:8, 0:BW], wsm[:, c, i * 8:(i + 1) * 8], hnT[:, c, cs], [wsm, hnT], start=(c == 0), stop=(c == 7), inc=(c == 7))
            P.act(tmp, tmp[:, cs], pf[0][0:8, 0:BW], AF.Exp, [pf[0], hv], bias=hv[:, 3:4], scale=-1.0)
            P.act(nlf, nlf[:, cs], tmp[:, cs], AF.Ln, [tmp], bias=1.0)
            P.act(tmp, tmp[:, cs], pf[1][0:8, 0:BW], AF.Exp, [pf[1], hv], bias=hv[:, 2:3], scale=1.0)
            P.act(negg, negg[:, cs], tmp[:, cs], AF.Ln, [tmp], bias=1.0)
            P.tsc("dve", negg, negg[:, cs], negg[:, cs], hv[:, 4:5], None, ALU.mult, None, [negg, hv])
            P.act(tmp, tmp[:, cs], pf[2][0:8, 0:BW], AF.Exp, [pf[2]], scale=-1.0)
            P.act(nlb, nlb[:, cs], tmp[:, cs], AF.Ln, [tmp], bias=1.0)
        for tt_ in range(NT):
            cs = slice(tt_ * 128, (tt_ + 1) * 128)
            init = 0.0 if tt_ == 0 else ncum[:, tt_ * 128 - 1:tt_ * 128]
            P.op("dve", lambda e, cs=cs, init=init: e.tensor_tensor_scan(out=ncum[:, cs], data0=onesr[:], data1=nlf[:, cs], initial=init, op0=ALU.mult, op1=ALU.add),
                 reads=[onesr, nlf, ncum], writes=[ncum])
            P.op("dve", lambda e, cs=cs: e.tensor_tensor_scan(out=Lr[:, cs], data0=onesr[:], data1=negg[:, cs], initial=0.0, op0=ALU.mult, op1=ALU.add),
                 reads=[onesr, negg], writes=[Lr])
        P.tt("dve", LBr, LBr[:], Lr[:], nlb[:], ALU.add, [Lr, nlb])
        r1, r2 = rows["r1"], rows["r2"]
        for k, (src, pk) in enumerate(((ncum, PK["c"]), (Lr, PK["L"]), (LBr, PK["LB"]))):
            pc = pcs[3 * k:3 * k + 3]
            P.cp("dve", pc[0], pc[0][:], src[:], [src])
            P.tt("dve", r1, r1[:], src[:], pc[0][:], ALU.subtract, [src, pc[0]])
            P.cp("dve", pc[1], pc[1][:], r1[:], [r1])
            P.tt("dve", r2, r2[:], r1[:], pc[1][:], ALU.subtract, [r1, pc[1]])
            P.cp("dve", pc[2], pc[2][:], r2[:], [r2])
            for j in range(3):
                P.dma("sp", pk, pk[32 * j:32 * j + 8, :], pc[j][:], [pc[j]], dPK, key=j)
        for tt_ in range(NT):
            cs = slice(tt_ * 128, (tt_ + 1) * 128)
            pst = pf[3 + tt_ % 2]
            P.tr(pst, pst[:, 0:8], Lr[:, cs], idf8[:], [Lr, idf8], inc=False)
            P.tr(pst, pst[:, 8:16], nlb[:, cs], idf8[:], [nlb, idf8], inc=True)
            P.cp("dve", Lc, Lc[:], pst[:, 0:8], [pst])
            P.cp("dve", nlc, nlc[:], pst[:, 8:16], [pst])
            P.mm(pst, pst[:, 16:24], sl_f[:], Lc[:], [sl_f, Lc])
            P.tt("dve", t8, t8[:], Lc[:], nlc[:], ALU.add, [Lc, nlc])
            P.act(cols, cols[:, tt_, 0, :], t8[:], AF.Exp, [t8], scale=-1.0, key=(tt_, 0))
            P.act(cols, cols[:, tt_, 1, :], nlc[:], AF.Exp, [nlc], scale=-1.0, key=(tt_, 1))
            P.tt("dve", t8, t8[:], Lc[:], pst[:, 16:24], ALU.subtract, [Lc, pst])
            P.act(cols, cols[:, tt_, 2, :], t8[:], AF.Exp, [t8], key=(tt_, 2))
            P.act(cols, cols[:, tt_, 3, :], pst[:, 16:24], AF.Exp, [pst], scale=-1.0, key=(tt_, 3))
            P.act(cols, cols[:, tt_, 4, :], Lc[:], AF.Exp, [Lc], scale=-1.0, key=(tt_, 4))
        P.barrier()

        PK, cols = alloc_pk()
        wh = [[AR.alloc("wh%d_%d" % (i, j), [128, 8, 128], BF16) for j in range(3)] for i in range(2)]
        qT = AR.alloc("qT", [128, T], BF16)
        kT = AR.alloc("kT", [128, T], BF16)
        vtok = AR.alloc("vtok", [128, NT, 128], BF16)
        QA = AR.alloc("QA", [6, T], BF16)
        KA = AR.alloc("KA", [6, T], BF16)
        pT = [AR.alloc("pT%d" % i, [128, BW], BF16) for i in range(3)]
        rsb = AR.alloc("rsb", [128, BW], F32)
        for h in range(H):
            hs = h % 2
            for j, off in enumerate((OFF_FQ, OFF_FK, OFF_FV)):
                P.dma("sp", wh[hs][j], wh[hs][j][:], wslice(wb_in, off + h * 128, 128), [wtt["in"]], dWh[hs * 3 + j])
            for b in range(NB):
                cs = slice(b * BW, (b + 1) * BW)
                for j, dst in ((0, qT), (1, kT)):
                    pst = pf[j]
                    for c in range(8):
                        P.mm(pst, pst[:, 0:BW], wh[hs][j][:, c, :], hnT[:, c, cs], [wh[hs][j], hnT], start=(c == 0), stop=(c == 7), inc=(c == 7))
                    if j == 0:
                        P.act(dst, dst[:, cs], pst[:, 0:BW], AF.Copy, [pst], scale=DH ** -0.5)
                    else:
                        P.cp("dve", dst, dst[:, cs], pst[:, 0:BW], [pst])
                for kind, dst in ((0, QA), (1, KA)):
                    pst = pf[3]
                    P.mm(pst, pst[0:6, 0:BW], selbig(kind, h), PK["c"][:, cs], [sb_b, PK["c"]])
                    P.cp("dve", dst, dst[:, cs], pst[0:6, 0:BW], [pst])
            for t0 in range(0, NT, 4):
                n4 = min(4, NT - t0)
                pst = pf[2]
                for ti in range(n4):
                    tcs = slice((t0 + ti) * 128, (t0 + ti + 1) * 128)
                    for c in range(8):
                        P.mm(pst, pst[:, ti * 128:(ti + 1) * 128], hnT[:, c, tcs], wh[hs][2][:, c, :], [wh[hs][2], hnT], start=(c == 0), stop=(c == 7), inc=(c == 7 and ti == n4 - 1), key=ti)
                P.act(vtok, vtok[:, t0:t0 + n4, :], pst[:, 0:n4 * 128].rearrange("p (t n) -> p t n", t=n4), AF.Copy, [pst], key=t0)
            pcount = 0
            for qb in range(NB):
                ot, sm = pf[4], pf[5]
                nk = QPB * qb + QPB
                for kt in range(nk):
                    j = max(0, kt - QPB * qb)
                    c0 = 128 * j
                    ncol = BW - c0
                    qcs = slice(qb * BW + c0, (qb + 1) * BW)
                    kcs = slice(kt * 128, (kt + 1) * 128)
                    diag = kt >= QPB * qb
                    sps = pf[pcount % 2]
                    ptile = pT[pcount % 3]
                    pcount += 1
                    P.mm(sps, sps[:, c0:BW], kT[:, kcs], qT[:, qcs], [kT, qT], start=True, stop=False, inc=False)
                    P.mm(sps, sps[:, c0:BW], KA[:, kcs], QA[:, qcs], [KA, QA], start=False, stop=(not diag), inc=(not diag))
                    if diag:
                        P.mm(sps, sps[:, c0:BW], ident_b, mkw[:, 0:ncol], [cbb, mkw], start=False, stop=True)
                    P.act(ptile, ptile[:, c0:BW], sps[:, c0:BW], AF.Exp, [sps])
                    last = kt == nk - 1
                    P.mm(ot, ot[:, c0:BW], vtok[:, kt, :], ptile[:, c0:BW], [vtok, ptile], start=(kt == 0), stop=last, inc=False, skip=True)
                    P.mm(sm, sm[:, c0:BW], ones_b[:], ptile[:, c0:BW], [ones_b, ptile], start=(kt == 0), stop=last, inc=True, skip=True)
                P.op("dve", lambda e, sm=sm: e.reciprocal(out=rsb[:, 0:BW], in_=sm[:, 0:BW]), reads=[sm], writes=[rsb])
                P.tt("dve", yfoxT, yfoxT[:, h, qb * BW:(qb + 1) * BW], ot[:, 0:BW], rsb[:, 0:BW], ALU.mult, [ot, rsb], key=(h, qb))
        if debug and s == 0:
            P.dma("sp", None, dbg_fox, yfoxT[:], [yfoxT], dOut, write=False)
        P.barrier()

        for g in range(2):
            gdn_half(P, AR, locals(), s, g)
            P.barrier()
        if debug and s == 0:
            P.dma("sp", None, dbg_gdn, ygdnT[:], [ygdnT], dOut, write=False)

        tail_phase(P, AR, locals(), s)
        P.barrier()

    P.emit()
    P.close()
    return nc
def gdn_half(P, AR, L, s, g):
    T, NT = L["T"], L["NT"]
    hnT, ygdnT, cbb, ident_b = L["hnT"], L["ygdnT"], L["cbb"], L["ident_b"]
    pf, pb, wb_in, wtt, dWh, cwg, gnorm, sb_b = L["pf"], L["pb"], L["wb_in"], L["wtt"], L["dWh"], L["cwg"], L["gnorm"], L["sb_b"]
    selbig, wslice = L["selbig"], L["wslice"]
    G = 4
    PK, cols = L["alloc_pk"]()
    A = AR.alloc
    wg = [A("wg%d" % qi, [128, G, 8, 128], BF16) for qi in range(4)]
    halo = A("halo", [128, 12, 3], F32)
    stage = [A("stage%d" % i, [128, 131], F32) for i in range(2)]
    acc = [A("acc%d" % i, [128, 128], F32) for i in range(2)]
    cT = A("cT", [128, 12, 128], BF16)
    QA1 = A("QA1", [6, G, 128], BF16)
    QA2 = A("QA2", [6, G, 128], BF16)
    KAg = A("KAg", [6, G, 128], BF16)
    tks = [{n: A("tk%d_" % p + n, [128, G, 128], BF16) for n in ("kn", "Rw", "kd", "qs", "qd", "Rv")} for p in range(2)]
    knTs = [A("knT%d" % p, [128, G, 128], BF16) for p in range(2)]
    qsTs = [A("qsT%d" % p, [128, G, 128], BF16) for p in range(2)]
    qdTs = [A("qdT%d" % p, [128, G, 128], BF16) for p in range(2)]
    E1Ts = [A("E1T%d" % p, [128, G, 128], BF16) for p in range(2)]
    E1s = [A("E1_%d" % p, [128, G, 128], BF16) for p in range(2)]
    E2Ts = [A("E2T%d" % p, [128, G, 128], BF16) for p in range(2)]
    Eb = [A("Eb%d" % i, [128, G, 128], BF16) for i in range(2)]
    EbT = [A("EbT%d" % i, [128, G, 128], BF16) for i in range(2)]
    Tm = [A("Tm%d" % i, [128, G, 128], BF16) for i in range(2)]
    TTm = [A("TTm%d" % i, [128, G, 128], BF16) for i in range(2)]
    Y1s = A("Y1s", [128, G, 128], BF16)
    Y2s = A("Y2s", [128, G, 128], BF16)
    negwT = A("negwT", [128, G, 128], BF16)
    u_bf = A("u_bf", [128, G, 128], BF16)
    S_f = A("S_f", [128, G, 128], F32)
    S_b = A("S_b", [128, G, 128], BF16)
    gzs = A("gzs", [128, G, 128], BF16)
    o_f = A("o_f", [128, G, 128], F32)
    on_b = A("on_b", [128, G, 128], BF16)
    junk = A("junk", [128, 128], BF16)
    scs = [A("sc%d" % p, [128, 40], F32) for p in range(2)]
    if SCHED_DEBUG:
        print("gdn arena used", AR.off, "of", AR.n)

    offs = (OFF_GQ, OFF_GK, OFF_GV, OFF_GZ)
    for qi in range(4):
        for i in range(G):
            hh = g * G + i
            P.dma("sp", wg[qi], wg[qi][:, i, :, :], wslice(wb_in, offs[qi] + hh * 128, 128), [wtt["in"]], dWh[6], key=i)
    P.op("dve", lambda e: e.memset(S_f[:], 0.0), writes=[S_f])
    P.op("dve", lambda e: e.memset(S_b[:], 0.0), writes=[S_b])
    hsl = slice(g * G, (g + 1) * G)
    scale = DH ** -0.5

    def v4(t_, i):
        return t_[:, i, :]

    for t in range(NT):
        cs = slice(t * 128, (t + 1) * 128)
        tk, knT, qsT, qdT, sc = tks[t % 2], knTs[t % 2], qsTs[t % 2], qdTs[t % 2], scs[t % 2]
        E1T, E1, E2T = E1Ts[t % 2], E1s[t % 2], E2Ts[t % 2]
        AT, Am, attnT = E1T, E1, E2T
        k = 0
        for qi in range(3):
            for i in range(G):
                hh = g * G + i
                ch = qi * 8 + hh
                pst = pf[k % 2]
                stg = stage[k % 2]
                ac = acc[k % 2]
                k += 1
                for c in range(8):
                    P.mm(pst, pst[:, 0:128], wg[qi][:, i, c, :], hnT[:, c, cs], [wg[qi], hnT], start=(c == 0), stop=(c == 7), inc=(c == 7))
                P.act(stg, stg[:, 3:131], pst[:, 0:128], AF.Copy, [pst], key='x')
                hi = qi * G + i
                if t == 0:
                    P.op("pool", lambda e, stg=stg: e.memset(stg[:, 0:3], 0.0), writes=[stg], key='h')
                else:
                    P.cp("pool", stg, stg[:, 0:3], halo[:, hi, :], [halo], key='h')
                P.cp("pool", halo, halo[:, hi, :], stg[:, 128:131], [stg], key=hi)
                P.tsc("dve", ac, ac[:], stg[:, 3:131], cwg[:, ch, 3:4], None, ALU.mult, None, [stg, cwg])
                for tap in (2, 1, 0):
                    P.stt(ac, ac[:], stg[:, tap:tap + 128], cwg[:, ch, tap:tap + 1], ac[:], ALU.mult, ALU.add, [stg, cwg, ac])
                P.act(cT, cT[:, hi, :], ac[:], AF.Silu, [ac], key=hi)
        for i in range(G):
            for c in range(8):
                P.mm(pf[2], pf[2][:, i * 128:(i + 1) * 128], hnT[:, c, cs], wg[3][:, i, c, :], [wg[3], hnT], start=(c == 0), stop=(c == 7), inc=(c == 7 and i == G - 1))
        P.act(gzs, gzs[:], pf[2][:].rearrange("p (a b) -> p a b", a=G), AF.Silu, [pf[2]])
        for dst, pk, kind, pst in ((QA1, PK["LB"], 0, pf[3]), (QA2, PK["L"], 0, pf[4]), (KAg, PK["L"], 1, pf[5])):
            for i in range(G):
                P.mm(pst, pst[0:6, i * 128:(i + 1) * 128], selbig(kind, g * G + i), pk[:, cs], [sb_b, pk], inc=(i == G - 1))
            P.cp("dve", dst, dst[:], pst[0:6, :].rearrange("p (a b) -> p a b", a=G), [pst])
        for i in range(G):
            P.tr(pb[0], pb[0][:, i * 128:(i + 1) * 128], cT[:, 1 * G + i, :], ident_b, [cT, cbb], inc=False)
        for i in range(G):
            P.tr(pb[0], pb[0][:, 512 + i * 128:512 + (i + 1) * 128], cT[:, 0 * G + i, :], ident_b, [cT, cbb], inc=(i == G - 1))
        for i in range(G):
            P.tr(pb[1], pb[1][:, i * 128:(i + 1) * 128], cT[:, 2 * G + i, :], ident_b, [cT, cbb], inc=(i == G - 1))
        for i in range(G):
            P.act(junk, junk[:], pb[0][:, i * 128:(i + 1) * 128], AF.Square, [pb[0]], accum=sc[:, i:i + 1], extra_w=[sc])
            P.act(junk, junk[:], pb[0][:, 512 + i * 128:512 + (i + 1) * 128], AF.Square, [pb[0]], accum=sc[:, 4 + i:5 + i], extra_w=[sc])
        P.tsc("dve", sc, sc[:, 8:16], sc[:, 0:8], 1.0, EPS, ALU.mult, ALU.add, [sc])
        P.act(sc, sc[:, 8:16], sc[:, 8:16], AF.Ln, [sc])
        P.act(sc, sc[:, 8:16], sc[:, 8:16], AF.Exp, [sc], scale=-0.5)
        P.tt("dve", sc, sc[:, 16:20], sc[:, 8:12], cols[:, t, 0, hsl], ALU.mult, [sc, cols])
        P.tt("dve", sc, sc[:, 20:24], sc[:, 8:12], cols[:, t, 2, hsl], ALU.mult, [sc, cols])
        P.tsc("dve", sc, sc[:, 24:28], sc[:, 12:16], scale, None, ALU.mult, None, [sc])
        P.tt("dve", sc, sc[:, 28:32], sc[:, 24:28], cols[:, t, 4, hsl], ALU.mult, [sc, cols])
        k4 = pb[0][:, 0:512].rearrange("p (a b) -> p a b", a=G)
        q4 = pb[0][:, 512:1024].rearrange("p (a b) -> p a b", a=G)
        v4_ = pb[1][:, 0:512].rearrange("p (a b) -> p a b", a=G)
        for name, src, scl, srct in (("kn", k4, sc[:, 8:12], pb[0]), ("Rw", k4, sc[:, 16:20], pb[0]), ("kd", k4, sc[:, 20:24], pb[0]),
                                     ("qs", q4, sc[:, 24:28], pb[0]), ("qd", q4, sc[:, 28:32], pb[0])):
            P.tt("dve", tk[name], tk[name][:], src, bc_last(scl, 128), ALU.mult, [srct, sc])
        P.tt("dve", tk["Rv"], tk["Rv"][:], v4_, bc_last(cols[:, t, 1, hsl], 128), ALU.mult, [pb[1], cols])
        for i in range(G):
            P.tr(pb[1], pb[1][:, 512 + i * 128:512 + (i + 1) * 128], tk["kn"][:, i, :], ident_b, [tk["kn"], cbb], inc=(i == G - 1))
        P.act(knT, knT[:], pb[1][:, 512:1024].rearrange("p (a b) -> p a b", a=G), AF.Copy, [pb[1]])
        for i in range(G):
            P.tr(pb[0], pb[0][:, i * 128:(i + 1) * 128], tk["qs"][:, i, :], ident_b, [tk["qs"], cbb], inc=False)
        for i in range(G):
            P.tr(pb[0], pb[0][:, 512 + i * 128:512 + (i + 1) * 128], tk["qd"][:, i, :], ident_b, [tk["qd"], cbb], inc=(i == G - 1))
        P.act(qsT, qsT[:], pb[0][:, 0:512].rearrange("p (a b) -> p a b", a=G), AF.Copy, [pb[0]])
        P.act(qdT, qdT[:], pb[0][:, 512:1024].rearrange("p (a b) -> p a b", a=G), AF.Copy, [pb[0]])
        for i in range(G):
            cs_i = slice(i * 128, (i + 1) * 128)
            last = i == G - 1
            P.mm(pf[0], pf[0][:, cs_i], knT[:, i, :], knT[:, i, :], [knT], inc=last)
        for i in range(G):
            cs_i = slice(i * 128, (i + 1) * 128)
            P.mm(pf[1], pf[1][:, cs_i], knT[:, i, :], qsT[:, i, :], [knT, qsT], inc=(i == G - 1))
        for pst, lh, rh, mk in ((pf[2], KAg, QA1, 2), (pf[3], QA1, KAg, 3), (pf[4], KAg, QA2, 1)):
            for i in range(G):
                cs_i = slice(i * 128, (i + 1) * 128)
                P.mm(pst, pst[:, cs_i], lh[:, i, :], rh[:, i, :], [lh, rh], start=True, stop=False, inc=False)
                P.mm(pst, pst[:, cs_i], ident_b, cbb[:, mk, :], [cbb], start=False, stop=True, inc=(i == G - 1))
        for dst, pst in ((E1T, pf[2]), (E1, pf[3]), (E2T, pf[4])):
            P.act(dst, dst[:], pst[:].rearrange("p (a b) -> p a b", a=G), AF.Exp, [pst])
        f4 = lambda pst: pst[:].rearrange("p (a b) -> p a b", a=G)
        P.tt("dve", AT, AT[:], f4(pf[0]), E1T[:], ALU.mult, [pf[0], E1T])
        P.tt("dve", Am, Am[:], f4(pf[0]), E1[:], ALU.mult, [pf[0], E1])
        P.tt("dve", attnT, attnT[:], f4(pf[1]), E2T[:], ALU.mult, [pf[1], E2T])
        cur = 0
        P.tt("pool", Eb[0], Eb[0][:], Am[:], bc_mid(cbb[:, 4, :], G), ALU.mult, [Am, cbb])
        P.tt("pool", EbT[0], EbT[0][:], AT[:], bc_mid(cbb[:, 11, :], G), ALU.mult, [AT, cbb])
        P.tt("dve", Tm[cur], Tm[cur][:], bc_mid(ident_b, G), Eb[0][:], ALU.subtract, [cbb, Eb[0]])
        P.tt("dve", TTm[cur], TTm[cur][:], bc_mid(ident_b, G), EbT[0][:], ALU.subtract, [cbb, EbT[0]])
        for l in range(1, 7):
            e = l % 2
            lastl = l == 6
            P.tt("pool", Eb[e], Eb[e][:], Am[:], bc_mid(cbb[:, 4 + l, :], G), ALU.mult, [Am, cbb])
            if not lastl:
                P.tt("pool", EbT[e], EbT[e][:], AT[:], bc_mid(cbb[:, 11 + l, :], G), ALU.mult, [AT, cbb])
            for i in range(G):
                P.mm(pf[0], pf[0][:, i * 128:(i + 1) * 128], Eb[e][:, i, :], TTm[cur][:, i, :], [Eb[e], TTm[cur]], inc=(i == G - 1))
            if not lastl:
                for i in range(G):
                    P.mm(pf[1], pf[1][:, i * 128:(i + 1) * 128], EbT[e][:, i, :], Tm[cur][:, i, :], [EbT[e], Tm[cur]], inc=(i == G - 1))
            P.act(Y2s, Y2s[:], f4(pf[0]), AF.Copy, [pf[0]])
            if not lastl:
                P.cp("dve", Y1s, Y1s[:], f4(pf[1]), [pf[1]])
            for i in range(G):
                P.mm(pf[2], pf[2][:, i * 128:(i + 1) * 128], Tm[cur][:, i, :], Y2s[:, i, :], [Tm[cur], Y2s], inc=(i == G - 1))
            if not lastl:
                for i in range(G):
                    P.mm(pf[3], pf[3][:, i * 128:(i + 1) * 128], TTm[cur][:, i, :], Y1s[:, i, :], [TTm[cur], Y1s], inc=(i == G - 1))
            P.tt("dve", TTm[1 - cur], TTm[1 - cur][:], TTm[cur][:], f4(pf[2]), ALU.subtract, [TTm[cur], pf[2]])
            if not lastl:
                P.tt("dve", Tm[1 - cur], Tm[1 - cur][:], Tm[cur][:], f4(pf[3]), ALU.subtract, [Tm[cur], pf[3]])
            cur = 1 - cur
        TT_ = TTm[cur]
        for i in range(G):
            P.mm(pf[4], pf[4][:, i * 128:(i + 1) * 128], tk["Rw"][:, i, :], TT_[:, i, :], [tk["Rw"], TT_], inc=(i == G - 1))
        P.act(negwT, negwT[:], f4(pf[4]), AF.Copy, [pf[4]], scale=-1.0)
        for i in range(G):
            cs_i = slice(i * 128, (i + 1) * 128)
            P.mm(pf[5], pf[5][:, cs_i], TT_[:, i, :], tk["Rv"][:, i, :], [TT_, tk["Rv"]], start=True, stop=False, inc=False)
            P.mm(pf[5], pf[5][:, cs_i], negwT[:, i, :], S_b[:, i, :], [negwT, S_b], start=False, stop=True, inc=(i == G - 1))
        P.act(u_bf, u_bf[:], f4(pf[5]), AF.Copy, [pf[5]])
        for i in range(G):
            cs_i = slice(i * 128, (i + 1) * 128)
            P.mm(pf[0], pf[0][:, cs_i], qdT[:, i, :], S_b[:, i, :], [qdT, S_b], start=True, stop=False, inc=False)
            P.mm(pf[0], pf[0][:, cs_i], attnT[:, i, :], u_bf[:, i, :], [attnT, u_bf], start=False, stop=True, inc=(i == G - 1))
        for i in range(G):
            cs_i = slice(i * 128, (i + 1) * 128)
            P.mm(pf[1], pf[1][:, cs_i], tk["kd"][:, i, :], u_bf[:, i, :], [tk["kd"], u_bf], inc=(i == G - 1))
        P.tt("dve", S_f, S_f[:], S_f[:], bc_last(cols[:, t, 3, hsl], 128), ALU.mult, [S_f, cols])
        P.tt("dve", S_f, S_f[:], S_f[:], f4(pf[1]), ALU.add, [S_f, pf[1]])
        P.cp("pool", S_b, S_b[:], S_f[:], [S_f])
        for i in range(G):
            P.act(junk, junk[:], pf[0][:, i * 128:(i + 1) * 128], AF.Square, [pf[0]], accum=sc[:, 32 + i:33 + i], extra_w=[sc])
        P.tsc("dve", sc, sc[:, 36:40], sc[:, 32:36], 1.0 / DH, EPS, ALU.mult, ALU.add, [sc])
        P.act(sc, sc[:, 36:40], sc[:, 36:40], AF.Ln, [sc])
        P.act(sc, sc[:, 36:40], sc[:, 36:40], AF.Exp, [sc], scale=-0.5)
        P.tt("dve", o_f, o_f[:], f4(pf[0]), bc_last(sc[:, 36:40], 128), ALU.mult, [pf[0], sc])
        P.tt("dve", o_f, o_f[:], o_f[:], bc_mid(gnorm[:], G), ALU.mult, [o_f, gnorm])
        P.tt("dve", on_b, on_b[:], o_f[:], gzs[:], ALU.mult, [o_f, gzs])
        for i in range(G):
            P.tr(pb[1], pb[1][:, i * 128:(i + 1) * 128], on_b[:, i, :], ident_b, [on_b, cbb], inc=(i == G - 1))
        P.act(ygdnT, ygdnT[:, g * G:(g + 1) * G, cs], pb[1][:, 0:512].rearrange("p (a b) -> p a b", a=G), AF.Copy, [pb[1]], key=(g, t))
def tail_phase(P, AR, L, s):
    T = L["T"]
    hnT, yfoxT, ygdnT, cbb, ident_b, cwf = L["hnT"], L["yfoxT"], L["ygdnT"], L["cbb"], L["ident_b"], L["cwf"]
    pf, pb, wtt, dG, dOut = L["pf"], L["pb"], L["wtt"], L["dG"], L["dOut"]
    wb_in, wb_bf, wb_bg, wb_out, wb_up, wb_dn = L["wb_in"], L["wb_bf"], L["wb_bg"], L["wb_out"], L["wb_up"], L["wb_dn"]
    nrm_d, x_d, out_d, wslice, rmsnorm_tile = L["nrm_d"], L["x_d"], L["out_d"], L["wslice"], L["rmsnorm_tile"]
    BWT = min(512, T)
    QT = BWT // 128
    NBT = T // BWT
    AR.reset()
    A = AR.alloc
    gam2 = A("gam2", [128, D], F32)
    gam3 = A("gam3", [128, D], F32)
    h1 = A("h1", [128, QT, D], F32)
    xres = A("xres", [128, D], F32)
    hn2b = A("hn2b", [128, D], BF16)
    hn2T = A("hn2T", [128, 8, BWT], BF16)
    aT = A("aT", [128, NF, BWT], BF16)
    halo = A("fhalo", [128, 44, 2], F32)
    st4 = A("st4t", [128, 8], F32)
    base = AR.off
    NWM = 2
    wm = [[A("wm%d_%d" % (k, i), [128, 8, 128], BF16) for i in range(4)] for k in range(NWM)]
    woc = [A("woc%d" % i, [128, D], BF16) for i in range(3)]
    yT = A("yT", [128, 8, BWT], BF16)
    sg = [A("sg%d" % i, [128, BWT], F32) for i in range(2)]
    end_early = AR.off
    AR.off = base
    NWU, NWD = 4, 4
    wupc = [A("wupc%d" % i, [128, 8, 256], BF16) for i in range(NWU)]
    stage = [A("fstage%d" % i, [128, 2 + BWT], F32) for i in range(2)]
    acc = [A("facc%d" % i, [128, BWT], F32) for i in range(2)]
    sil = A("sil", [128, BWT], F32)
    wdc = [A("wdc%d" % i, [128, D], BF16) for i in range(NWD)]
    ot = A("ot", [128, D], F32)
    AR.off = max(AR.off, end_early)
    sqj = xres
    P.dma("sp", gam2, gam2[:], nrm_d[1:2, :].partition_broadcast(128), [], dG)
    P.dma("sp", gam3, gam3[:], nrm_d[2:3, :].partition_broadcast(128), [], dG)
    cnt = {"wm": 0, "wo": 0, "wu": 0, "wd": 0}
    for b in range(NBT):
        cs = slice(b * BWT, (b + 1) * BWT)
        for dc in range(8):
            srcs = ((wb_in, OFF_GF + dc * 128, hnT), (wb_in, OFF_GG + dc * 128, hnT), (wb_bf, dc * 128, yfoxT), (wb_bg, dc * 128, ygdnT))
            keys = ("in", "in", "bf", "bg")
            wset = wm[cnt["wm"] % NWM]
            cnt["wm"] += 1
            for i, (wb, c0, _) in enumerate(srcs):
                P.dma("sp", wset[i], wset[i][:], wslice(wb, c0, 128), [wtt[keys[i]]], True)
            for i, (_, _, act_) in enumerate(srcs):
                for c in range(8):
                    P.mm(pf[i], pf[i][:, 0:BWT], wset[i][:, c, :], act_[:, c, cs], [wset[i], act_], start=(c == 0), stop=(c == 7))
            P.act(sg[0], sg[0][:], pf[0][:, 0:BWT], AF.Sigmoid, [pf[0]])
            P.act(sg[1], sg[1][:], pf[1][:, 0:BWT], AF.Sigmoid, [pf[1]])
            P.tt("dve", sg[0], sg[0][:], sg[0][:], pf[2][:, 0:BWT], ALU.mult, [sg[0], pf[2]])
            P.tt("dve", sg[1], sg[1][:], sg[1][:], pf[3][:, 0:BWT], ALU.mult, [sg[1], pf[3]])
            P.tt("dve", yT, yT[:, dc, :], sg[0][:], sg[1][:], ALU.add, [sg[0], sg[1]], key=dc)
        for c in range(8):
            w = woc[cnt["wo"] % 3]
            cnt["wo"] += 1
            P.dma("sp", w, w[:], wb_out[c * 128:(c + 1) * 128, :], [wtt["out"]], True)
            for ti in range(QT):
                for hf in range(2):
                    pst = pf[ti * 2 + hf]
                    P.mm(pst, pst[:, 0:512], yT[:, c, ti * 128:(ti + 1) * 128], w[:, hf * 512:(hf + 1) * 512], [yT, w], start=(c == 0), stop=(c == 7))
        for ti in range(QT):
            tok0 = b * BWT + ti * 128
            P.dma("sp", xres, xres[:], x_d[s, tok0:tok0 + 128, :], [], True)
            for hf in range(2):
                pst = pf[ti * 2 + hf]
                P.tt("dve", h1, h1[:, ti, hf * 512:(hf + 1) * 512], xres[:, hf * 512:(hf + 1) * 512], pst[:, 0:512], ALU.add, [xres, pst], key=(ti, hf))
        for ti in range(QT):
            rmsnorm_tile(h1[:, ti, :], h1, gam2[:], gam2, st4, sqj, hn2b)
            for half in range(2):
                pbt = pb[half]
                for c in range(4):
                    cc = half * 4 + c
                    P.tr(pbt, pbt[:, c * 128:(c + 1) * 128], hn2b[:, cc * 128:(cc + 1) * 128], ident_b, [hn2b, cbb], key=c)
                P.act(hn2T, hn2T[:, half * 4:half * 4 + 4, ti * 128:(ti + 1) * 128],
                      pbt[:, 0:512].rearrange("p (c n) -> p c n", c=4), AF.Copy, [pbt], key=(ti, half))
        P.barrier()
        for f in range(NF):
            w = wupc[cnt["wu"] % NWU]
            cnt["wu"] += 1
            P.dma("sp", w, w[:, :, 0:128], wslice(wb_up, f * 128, 128), [wtt["up"]], True, key=0)
            P.dma("sp", w, w[:, :, 128:256], wslice(wb_up, DFF + f * 128, 128), [wtt["up"]], True, key=1)
            for part in range(2):
                pst = pf[part]
                stg = stage[part]
                ac = acc[part]
                ch = part * NF + f
                for c in range(8):
                    P.mm(pst, pst[:, 0:BWT], w[:, c, part * 128:(part + 1) * 128], hn2T[:, c, :], [w, hn2T], start=(c == 0), stop=(c == 7))
                P.act(stg, stg[:, 2:2 + BWT], pst[:, 0:BWT], AF.Copy, [pst], key='x')
                if b == 0:
                    P.op("pool", lambda e, stg=stg: e.memset(stg[:, 0:2], 0.0), writes=[stg], key='h')
                else:
                    P.cp("pool", stg, stg[:, 0:2], halo[:, ch, :], [halo], key='h')
                P.cp("pool", halo, halo[:, ch, :], stg[:, BWT:BWT + 2], [stg], key=ch)
                P.tsc("dve", ac, ac[:], stg[:, 2:2 + BWT], cwf[:, ch, 2:3], None, ALU.mult, None, [stg, cwf])
                for tap in (1, 0):
                    P.stt(ac, ac[:], stg[:, tap:tap + BWT], cwf[:, ch, tap:tap + 1], ac[:], ALU.mult, ALU.add, [stg, cwf, ac])
            P.act(sil, sil[:], acc[0][:], AF.Silu, [acc[0]])
            P.tt("dve", aT, aT[:, f, :], sil[:], acc[1][:], ALU.mult, [sil, acc[1]], key=f)
        for f in range(NF):
            w = wdc[cnt["wd"] % NWD]
            cnt["wd"] += 1
            P.dma("sp", w, w[:], wb_dn[f * 128:(f + 1) * 128, :], [wtt["dn"]], True)
            for ti in range(QT):
                for hf in range(2):
                    pst = pf[ti * 2 + hf]
                    P.mm(pst, pst[:, 0:512], aT[:, f, ti * 128:(ti + 1) * 128], w[:, hf * 512:(hf + 1) * 512], [aT, w], start=(f == 0), stop=(f == NF - 1))
        for ti in range(QT):
            tok0 = b * BWT + ti * 128
            for hf in range(2):
                pst = pf[ti * 2 + hf]
                P.tt("dve", ot, ot[:, hf * 512:(hf + 1) * 512], h1[:, ti, hf * 512:(hf + 1) * 512], pst[:, 0:512], ALU.add, [h1, pst], key=hf)
            rmsnorm_tile(ot[:], ot, gam3[:], gam3, st4, sqj, ot)
            P.dma("sp", None, out_d[s, tok0:tok0 + 128, :], ot[:], [ot], dOut, write=False)
        if b < NBT - 1:
            P.barrier()


def make_in_maps(inp, NSEQ, ncores, same=False):
    cb, sel_last, selbig = host_consts()
    f = lambda a: np.ascontiguousarray(np.asarray(a, dtype=np.float32))
    common = {
        "w_in": f(inp["w_in"][0]), "w_bf": f(inp["w_branch_fox"][0]), "w_bg": f(inp["w_branch_gdn"][0]),
        "w_out": f(inp["w_out"][0]), "w_up": f(inp["w_up"][0]), "w_down": f(inp["w_down"][0]),
        "norms": f(np.stack([np.asarray(inp["norm_mix"])[0], np.asarray(inp["norm_ffn"])[0], np.asarray(inp["norm_final"])], 0)),
        "gdn_norm": f(inp["gdn_norm"]),
        "cwg": f(np.asarray(inp["gdn_conv_w"])[0].T.reshape(24, 128, 4).transpose(1, 0, 2)),
        "cwf": f(np.asarray(inp["ffn_conv_w"])[0].T.reshape(44, 128, 3).transpose(1, 0, 2)),
        "hv": f(np.stack([np.asarray(inp["fox_f_bias"])[0], np.asarray(inp["gdn_a_log"])[0], np.asarray(inp["gdn_dt_bias"])[0]], 1)),
        "cb": cb, "sel_last": sel_last, "selbig": selbig,
    }
    x = np.asarray(inp["x"], np.float32)
    maps = []
    for c in range(ncores):
        m = dict(common)
        m["x"] = f(x[0:NSEQ] if same else x[c * NSEQ:(c + 1) * NSEQ])
        maps.append(m)
    return maps


_NC_CACHE = {}


def kernel(**inputs):
    x = np.asarray(inputs["x"])
    B, T, _ = x.shape
    ncores = 8
    NSEQ = B // ncores
    key = (T, NSEQ)
    if key not in _NC_CACHE:
        _NC_CACHE[key] = build(T, NSEQ)
    nc = _NC_CACHE[key]
    maps = make_in_maps(inputs, NSEQ, ncores)
    res = run_bass_kernel_spmd(nc, maps, core_ids=list(range(ncores)))
    out = np.concatenate([np.asarray(r["out"]) for r in res.results], axis=0)
    return out.astype(np.float32)
```

```python
import numpy as np
import concourse.bass as bass
import concourse.mybir as mybir
from concourse.bass_utils import run_bass_kernel_spmd

F32 = mybir.dt.float32
BF16 = mybir.dt.bfloat16
AF = mybir.ActivationFunctionType
ALU = mybir.AluOpType
AX = mybir.AxisListType

D = 1024
H = 8
DH = 128
DFF = 2816
NF = 22
OFF_FQ, OFF_FK, OFF_FV, OFF_FF = 0, 1024, 2048, 3072
OFF_GQ, OFF_GK, OFF_GV, OFF_GA, OFF_GB, OFF_GZ = 3080, 4104, 5128, 6152, 6160, 6168
OFF_GF, OFF_GG = 7192, 8216
DIN = 9240
EPS = 1e-6
NEG = -30000.0
NCB = 18
import os
SCHED_DEBUG = bool(os.environ.get("SCHED_DEBUG"))
ATTACH_WAITS = False
PRIO_BL = True
POOL_VCONV = False
N_SW_SEMS = 2


class TT:
    __slots__ = ("ap", "w", "r", "name", "pg")

    def __init__(self, ap, name=""):
        self.ap = ap
        self.w = {}
        self.r = []
        self.pg = []
        self.name = name

    def __getitem__(self, idx):
        return self.ap[idx]


class TTV:
    __slots__ = ("p", "ap", "name")

    def __init__(self, p, ap):
        self.p = p
        self.ap = ap
        self.name = p.name + "_v"

    def __getitem__(self, idx):
        return self.ap[idx]

    w = property(lambda self: self.p.w, lambda self, v: setattr(self.p, "w", v))
    r = property(lambda self: self.p.r, lambda self, v: setattr(self.p, "r", v))
    pg = property(lambda self: self.p.pg, lambda self, v: setattr(self.p, "pg", v))


class _Op:
    __slots__ = ("id", "eng", "fn", "deps", "dur", "lat", "is_dma", "multi", "seg", "users", "nun", "start", "fin",
                 "needs_inc", "incval", "dsem", "dcount", "dprev", "waits", "qpos")


def _free_n(ap):
    n = 1
    for d in ap.shape[1:]:
        n *= d
    return n


class Prog:
    ENGS = ("pe", "act", "dve", "pool", "sp")
    NDS = 32
    WIN = 48
    XLAT = 60.0

    def __init__(self, nc, same_sync=True):
        self.nc = nc
        self.sems = []
        self._ctx = []
        self.esem = {n: self._newsem("e_" + n) for n in self.ENGS}
        self.dsems_ = [self._newsem("dq%d" % i) for i in range(self.NDS)]
        self.ops = []
        self.seg = 0
        self.nsegs = 1
        self.seginfo = []
        self.lastpe = {}

    def _newsem(self, name):
        cm = self.nc.semaphore(name)
        h = cm.__enter__()
        self._ctx.append(cm)
        self.sems.append(h)
        return len(self.sems) - 1

    def dsem(self, name=None):
        return True

    def sbuf(self, name, shape, dtype):
        cm = self.nc.sbuf_tensor(name, shape, dtype)
        t = cm.__enter__()
        self._ctx.append(cm)
        return t

    def psum(self, name, shape, dtype):
        cm = self.nc.psum_tensor(name, shape, dtype)
        t = cm.__enter__()
        self._ctx.append(cm)
        return t

    def st(self, name, shape, dtype):
        return TT(self.sbuf(name, shape, dtype)[:], name)

    def barrier(self):
        self.seg += 1
        self.nsegs = self.seg + 1

    def op(self, eng, fn, reads=(), writes=(), inc=True, dsem=None, key=None, dur=100.0, lat=None, multi=False):
        o = _Op()
        o.id = len(self.ops)
        o.eng = eng
        o.fn = fn
        o.dur = dur
        o.is_dma = dsem is not None
        o.lat = lat if lat is not None else dur
        o.multi = multi
        o.seg = self.seg
        deps = {}

        def add(i, raw):
            if i is None:
                return
            if raw or i not in deps:
                deps[i] = raw or deps.get(i, False)

        for t in reads:
            if t is None:
                continue
            for i in t.w.values():
                add(i, True)
        for t in writes:
            if t is None:
                continue
            if eng == "pe":
                tp_ = t.p if isinstance(t, TTV) else t
                add(self.lastpe.get(id(tp_)), False)
                self.lastpe[id(tp_)] = o.id
            if key is None:
                for i in t.w.values():
                    add(i, False)
                for i in t.r:
                    add(i, False)
                t.w = {None: o.id}
                t.r = []
                t.pg = []
            else:
                if t.r:
                    t.pg = list(t.r) + list(t.w.values())
                    for i in t.pg:
                        add(i, False)
                    t.w = {key: o.id}
                    t.r = []
                else:
                    add(t.w.get(key), False)
                    add(t.w.get(None), False)
                    for i in t.pg:
                        add(i, False)
                    t.w[key] = o.id
        for t in reads:
            if t is None or t in writes:
                continue
            t.r.append(o.id)
        o.deps = deps
        self.ops.append(o)
        return o.id

    def mm(self, ot, out, lhsT, rhs, reads, start=True, stop=True, inc=True, skip=False, key=None):
        n = _free_n(rhs)
        f = 4.0 if rhs.dtype == F32 else 1.0
        self.op("pe", lambda e: e.matmul(out, lhsT=lhsT, rhs=rhs, start=start, stop=stop, skip_group_check=skip),
                reads=reads, writes=[ot], key=key, dur=f * max(n, 64) / 2.4 + 6.0)

    def tr(self, ot, out, in_, ident, reads, inc=True, key=None):
        f = 4.0 if in_.dtype == F32 else 1.0
        self.op("pe", lambda e: e.transpose(out=out, in_=in_, identity=ident), reads=reads, writes=[ot], key=key, dur=f * 64 / 2.4 + 6.0)

    def act(self, ot, out, in_, func, reads, bias=None, scale=None, accum=None, extra_w=(), key=None):
        kw = {}
        d = 224.0 + _free_n(in_) / 1.2
        if bias is not None:
            kw["bias"] = bias
            if not isinstance(bias, float):
                d += 93
        if scale is not None:
            kw["scale"] = scale
        if accum is not None:
            kw["accum_out"] = accum
            d += 93
        self.op("act", lambda e: e.activation(out=out, in_=in_, func=func, **kw), reads=reads, writes=[ot] + list(extra_w),
                key=key, dur=d, multi=(accum is not None))

    def _vd(self, eng, ap):
        n = _free_n(ap)
        return (60.0 + n / 0.96) if eng == "dve" else (130.0 + n / 0.5)

    def tsc(self, eng, ot, out, in0, s1, s2, op0, op1, reads, key=None):
        if op1 is None:
            self.op(eng, lambda e: e.tensor_scalar(out=out, in0=in0, scalar1=s1, scalar2=None, op0=op0), reads=reads, writes=[ot], key=key, dur=self._vd(eng, in0))
        else:
            self.op(eng, lambda e: e.tensor_scalar(out=out, in0=in0, scalar1=s1, scalar2=s2, op0=op0, op1=op1), reads=reads, writes=[ot], key=key, dur=self._vd(eng, in0))

    def stt(self, ot, out, in0, scalar, in1, op0, op1, reads, key=None):
        self.op("dve", lambda e: e.scalar_tensor_tensor(out=out, in0=in0, scalar=scalar, in1=in1, op0=op0, op1=op1), reads=reads, writes=[ot], key=key, dur=self._vd("dve", in0))

    def tt(self, eng, ot, out, in0, in1, op, reads, key=None):
        self.op(eng, lambda e: e.tensor_tensor(out=out, in0=in0, in1=in1, op=op), reads=reads, writes=[ot], key=key, dur=self._vd(eng, out))

    def cp(self, eng, ot, out, in_, reads, key=None):
        self.op(eng, lambda e: e.tensor_copy(out=out, in_=in_), reads=reads, writes=[ot], key=key, dur=self._vd(eng, in_))

    def dma(self, eng, ot, out, in_, reads, dsem, write=True, key=None):
        n = 1
        for d in out.shape:
            n *= d
        nbytes = n * (4 if out.dtype == F32 else 2)
        self.op(eng, lambda e: e.dma_start(out=out, in_=in_), reads=reads, writes=([ot] if write else []), dsem=True, key=key,
                dur=60.0, lat=2200.0 + nbytes / 60.0)

    def _schedule_segment(self, ids, t0):
        ops = self.ops
        inseg = set(ids)
        q = {n: [] for n in self.ENGS}
        for i in ids:
            o = ops[i]
            o.deps = {d: r for d, r in o.deps.items() if d in inseg}
            o.nun = len(o.deps)
            o.users = []
            o.start = None
            q[o.eng].append(i)
        for i in ids:
            for d in ops[i].deps:
                ops[d].users.append(i)
        bl = {}
        for i in reversed(ids):
            o = ops[i]
            m = 0.0
            for u in o.users:
                if bl[u] > m:
                    m = bl[u]
            bl[i] = m + o.lat + (self.XLAT if o.is_dma else 0.0)
        head = {n: 0 for n in self.ENGS}
        free = {n: t0 for n in self.ENGS}
        order = {n: [] for n in self.ENGS}
        cand = {n: None for n in self.ENGS}
        dirty = set(self.ENGS)
        remaining = len(ids)
        WIN = self.WIN
        XL = self.XLAT

        def rescan(n):
            lst = q[n]
            h = head[n]
            while h < len(lst) and ops[lst[h]].start is not None:
                h += 1
            head[n] = h
            best = None
            cnt = 0
            j = h
            fr = free[n]
            while j < len(lst) and cnt < WIN:
                o = ops[lst[j]]
                j += 1
                if o.start is not None:
                    continue
                cnt += 1
                if o.nun:
                    continue
                rd = fr
                for d in o.deps:
                    od = ops[d]
                    f = od.fin + (XL if od.eng != n or od.is_dma else 0.0)
                    if f > rd:
                        rd = f
                if PRIO_BL:
                    kk = (rd, -bl[o.id], o.id)
                    if best is None or kk < best:
                        best = kk
                else:
                    if best is None or rd < best[0]:
                        best = (rd, 0.0, o.id)
                        if rd <= fr:
                            break
            cand[n] = best

        while remaining:
            for n in dirty:
                rescan(n)
            dirty = set()
            bn = None
            for n in self.ENGS:
                c = cand[n]
                if c is not None and (bn is None or c < cand[bn]):
                    bn = n
            if bn is None:
                raise RuntimeError("scheduler stuck: dependency cycle / window too small")
            st, _, i = cand[bn]
            o = ops[i]
            o.start = st
            o.fin = st + o.lat
            free[bn] = st + o.dur
            o.qpos = len(order[bn])
            order[bn].append(i)
            remaining -= 1
            dirty.add(bn)
            for u in o.users:
                ou = ops[u]
                ou.nun -= 1
                if ou.nun == 0:
                    dirty.add(ou.eng)
        tend = max([free[n] for n in self.ENGS] + [ops[i].fin for i in ids]) if ids else t0
        return order, tend

    def emit(self):
        nc = self.nc
        ops = self.ops
        segs = [[] for _ in range(self.nsegs)]
        for o in ops:
            segs[o.seg].append(o.id)
        ecount = {n: 0 for n in self.ENGS}
        dcount = [0] * self.NDS
        dnext = 0
        dnext_sw = 0
        known = {n: {} for n in self.ENGS}
        streams = {n: [] for n in self.ENGS}
        t0 = 0.0
        for sids in segs:
            tprev = t0
            order, t0 = self._schedule_segment(sids, t0)
            if SCHED_DEBUG:
                busy = {n: sum(ops[i].dur for i in order[n]) for n in self.ENGS}
                print("seg %3d n=%5d len_us=%8.1f  " % (len(self.seginfo), len(sids), (t0 - tprev) / 1e3) + " ".join("%s=%.0f" % (n, busy[n] / 1e3) for n in self.ENGS))
            self.seginfo.append(t0 - tprev)
            for i in sids:
                o = ops[i]
                o.needs_inc = False
            for i in sids:
                o = ops[i]
                for d, raw in o.deps.items():
                    od = ops[d]
                    if od.is_dma:
                        continue
                    if od.eng != o.eng or raw or o.eng != "pe":
                        od.needs_inc = True
            for n in self.ENGS:
                for i in reversed(order[n]):
                    if not ops[i].is_dma:
                        ops[i].needs_inc = True
                        break
            for n in self.ENGS:
                for i in order[n]:
                    o = ops[i]
                    if o.is_dma:
                        if n == "pool":
                            k = self.NDS - 8 + (dnext_sw % N_SW_SEMS)
                            dnext_sw += 1
                        else:
                            k = dnext % 16
                            dnext += 1
                        o.dsem = k
                        o.dprev = dcount[k]
                        dcount[k] += 16
                        o.dcount = dcount[k]
                    elif o.needs_inc:
                        ecount[n] += 1
                        o.incval = ecount[n]
            for n in self.ENGS:
                kn = known[n]
                for i in order[n]:
                    o = ops[i]
                    w = {}
                    for d, raw in o.deps.items():
                        od = ops[d]
                        if od.is_dma:
                            s, v = self.dsems_[od.dsem], od.dcount
                        elif od.eng != n or raw or n != "pe":
                            s, v = self.esem[od.eng], od.incval
                        else:
                            continue
                        if w.get(s, 0) < v:
                            w[s] = v
                    if o.is_dma and o.dprev:
                        s = self.dsems_[o.dsem]
                        if w.get(s, 0) < o.dprev:
                            w[s] = o.dprev
                    wl = []
                    for s, v in w.items():
                        if kn.get(s, 0) < v:
                            kn[s] = v
                            wl.append((s, v))
                    if o.is_dma:
                        incspec = (self.dsems_[o.dsem], 16)
                    elif o.needs_inc:
                        incspec = (self.esem[n], 1)
                    else:
                        incspec = None
                    streams[n].append((wl, o.fn, incspec, ATTACH_WAITS and not o.multi and not o.is_dma))
            toks = [(self.esem[n], ecount[n]) for n in self.ENGS if ecount[n]]
            toks += [(self.dsems_[k], dcount[k]) for k in range(self.NDS) if dcount[k]]
            for n in self.ENGS:
                wl = []
                for s, v in toks:
                    if s != self.esem[n] and known[n].get(s, 0) < v:
                        known[n][s] = v
                        wl.append((s, v))
                if wl:
                    streams[n].append((wl, None, None, False))
        self.sim_end = t0
        with nc.Block() as block:
            def mk(n):
                def body(e):
                    for wl, fn, incspec, attach in streams[n]:
                        if fn is None:
                            for s, v in wl:
                                e.wait_ge(self.sems[s], v)
                            continue
                        rest = wl
                        first = None
                        if attach and wl:
                            first = wl[0]
                            rest = wl[1:]
                        for s, v in rest:
                            e.wait_ge(self.sems[s], v)
                        ins = fn(e)
                        if first is not None:
                            ins._wait_ge(self.sems[first[0]], first[1])
                        if incspec is not None:
                            ins.then_inc(self.sems[incspec[0]], incspec[1])
                return body
            for n, f in (("sp", block.sync), ("pe", block.tensor), ("act", block.scalar), ("dve", block.vector), ("pool", block.gpsimd)):
                if streams[n]:
                    f(mk(n))

    def close(self):
        for cm in reversed(self._ctx):
            cm.__exit__(None, None, None)
        self._ctx = []


def host_consts():
    p = np.arange(128)[:, None]
    c = np.arange(128)[None, :]
    cb = np.zeros((128, NCB, 128), np.float32)
    cb[:, 0, :] = np.eye(128)
    cb[:, 1, :] = np.where(p > c, NEG, 0.0)
    cb[:, 2, :] = np.where(p >= c, NEG, 0.0)
    cb[:, 3, :] = np.where(c >= p, NEG, 0.0)
    for l in range(7):
        b = 1 << l
        m = ((p // (2 * b)) == (c // (2 * b))) & ((p % (2 * b)) >= b) & ((c % (2 * b)) < b)
        cb[:, 4 + l, :] = m
        cb[:, 11 + l, :] = m.T
    sel_last = np.zeros((128, 128), np.float32)
    sel_last[127, :] = 1.0
    sb = np.zeros((128, 2, 8, 6), np.float32)
    for h in range(8):
        for pc in range(3):
            sb[32 * pc + h, 0, h, pc] = -1.0
            sb[32 * pc + h, 1, h, 3 + pc] = 1.0
        sb[96, 0, h, 3:6] = 1.0
        sb[96, 1, h, 0:3] = 1.0
    return cb, sel_last, sb.reshape(128, 96)


class Arena:
    def __init__(self, P, nbytes):
        self.t = P.sbuf("arena", [128, nbytes // 2], BF16)
        self.n = nbytes
        self.off = 0

    def reset(self):
        self.off = 0

    def alloc(self, name, shape, dtype):
        esz = 4 if dtype == F32 else 2
        n = 1
        for d in shape[1:]:
            n *= d
        nb = (n * esz + 63) // 64 * 64
        if self.off + nb > self.n:
            raise RuntimeError("arena overflow at %s: %d + %d > %d" % (name, self.off, nb, self.n))
        a = self.off // 2
        ap = self.t[0:shape[0], a:a + n * esz // 2]
        if dtype == F32:
            ap = ap.bitcast(F32)
        if len(shape) == 3:
            ap = ap.rearrange("p (a b) -> p a b", a=shape[1])
        elif len(shape) == 4:
            ap = ap.rearrange("p (a b c) -> p a b c", a=shape[1], b=shape[2])
        self.off += nb
        return TT(ap, name)


def bc_last(ap2, n):
    return ap2.rearrange("p (h o) -> p h o", o=1).broadcast_to([ap2.shape[0], ap2.shape[1], n])


def bc_mid(ap2, h):
    return ap2.rearrange("p (o n) -> p o n", o=1).broadcast_to([ap2.shape[0], h, ap2.shape[1]])


def build(T, NSEQ, debug=False, arena_bytes=100 * 1024):
    NT = T // 128
    BW = min(512, T)
    NB = T // BW
    QPB = BW // 128
    nc = bass.Bass("TRN2", target_bir_lowering=False)

    def din(name, shape, dt=F32):
        return nc.dram_tensor(name, shape, dt, kind="ExternalInput").ap()

    x_d = din("x", [NSEQ, T, D])
    win_d = din("w_in", [D, DIN])
    wbf_d = din("w_bf", [D, D])
    wbg_d = din("w_bg", [D, D])
    wout_d = din("w_out", [D, D])
    wup_d = din("w_up", [D, 2 * DFF])
    wdn_d = din("w_down", [DFF, D])
    nrm_d = din("norms", [3, D])
    gnorm_d = din("gdn_norm", [1, DH])
    cwg_d = din("cwg", [128, 24, 4])
    cwf_d = din("cwf", [128, 44, 3])
    hv_d = din("hv", [8, 3])
    cb_d = din("cb", [128, NCB, 128])
    sl_d = din("sel_last", [128, 128])
    sb_d = din("selbig", [128, 96])
    out_d = nc.dram_tensor("out", [NSEQ, T, D], F32, kind="ExternalOutput").ap()
    if debug:
        dbg_fox = nc.dram_tensor("dbg_fox", [128, H, T], BF16, kind="ExternalOutput").ap()
        dbg_gdn = nc.dram_tensor("dbg_gdn", [128, H, T], BF16, kind="ExternalOutput").ap()

    def scratch(name, shape):
        return nc.dram_tensor(name, shape, BF16, kind="Internal").ap()

    wb_in = scratch("wb_in", [D, DIN])
    wb_bf = scratch("wb_bf", [D, D])
    wb_bg = scratch("wb_bg", [D, D])
    wb_out = scratch("wb_out", [D, D])
    wb_up = scratch("wb_up", [D, 2 * DFF])
    wb_dn = scratch("wb_dn", [DFF, D])

    P = Prog(nc)
    S = P.st
    cbb = S("cbb", [128, NCB, 128], BF16)
    mkw = S("mkw", [128, 512], BF16)
    ones_b = S("ones_b", [128, 128], BF16)
    idf8 = S("idf8", [8, 8], F32)
    sl_f = S("sl_f", [128, 128], F32)
    sb_b = S("sb_b", [128, 96], BF16)
    gnorm = S("gnorm", [128, DH], F32)
    cwg = S("cwg_s", [128, 24, 4], F32)
    cwf = S("cwf_s", [128, 44, 3], F32)
    hv = S("hv_s", [8, 8], F32)
    wsm = S("wsm", [128, 8, 24], BF16)
    onesr = S("onesr", [8, 128], F32)
    hnT = S("hnT", [128, 8, T], BF16)
    yfoxT = S("yfoxT", [128, 8, T], BF16)
    ygdnT = S("ygdnT", [128, 8, T], BF16)

    def alloc_pk():
        AR.reset()
        pk_ = {n: AR.alloc("PK_" + n, [128, T], BF16) for n in ("c", "L", "LB")}
        cols_ = AR.alloc("cols", [128, NT, 5, 8], F32)
        return pk_, cols_
    AR = Arena(P, arena_bytes)

    pf = [TT(P.psum("pf%d" % i, [128, 512], F32)[:], "pf%d" % i) for i in range(8)]
    pb = [TTV(pf[6 + i], pf[6 + i][:].bitcast(BF16)) for i in range(2)]
    ident_b = cbb[:, 0, :]

    dW = P.dsem("dW")
    wtt = {}
    for name, src, dst, rows_ in (("in", win_d, wb_in, D), ("bf", wbf_d, wb_bf, D), ("bg", wbg_d, wb_bg, D),
                                  ("out", wout_d, wb_out, D), ("up", wup_d, wb_up, D), ("dn", wdn_d, wb_dn, DFF)):
        t = TT(dst, "wb_" + name)
        wtt[name] = t
        for r0 in range(0, rows_, 128):
            P.dma("pool", t, dst[r0:r0 + 128, :], src[r0:r0 + 128, :], [], dW, key=r0)
    dC = P.dsem("dC")
    cbf = AR.alloc("cbf", [128, NCB, 128], F32)
    sb_f = AR.alloc("sb_f", [128, 96], F32)
    P.dma("sp", cbf, cbf[:], cb_d, [], dC)
    P.dma("sp", sb_f, sb_f[:], sb_d, [], dC)
    P.dma("sp", sl_f, sl_f[:], sl_d, [], dC)
    P.dma("sp", idf8, idf8[:], cb_d[0:8, 0, 0:8], [], dC)
    P.dma("sp", gnorm, gnorm[:], gnorm_d.partition_broadcast(128), [], dC)
    P.dma("sp", cwg, cwg[:], cwg_d, [], dC)
    P.dma("sp", cwf, cwf[:], cwf_d, [], dC)
    P.dma("sp", hv, hv[:, 0:3], hv_d, [], dC)
    P.cp("dve", cbb, cbb[:], cbf[:], [cbf])
    P.cp("dve", sb_b, sb_b[:], sb_f[:], [sb_f])
    P.op("dve", lambda e: e.memset(mkw[:], 0.0), writes=[mkw])
    P.cp("dve", mkw, mkw[:, 0:128], cbb[:, 1, :], [cbb])
    P.op("dve", lambda e: e.memset(ones_b[:], 1.0), writes=[ones_b])
    P.op("dve", lambda e: e.memset(onesr[:], 1.0), writes=[onesr])
    P.tsc("dve", hv, hv[:, 3:4], hv[:, 0:1], -1.0, None, ALU.mult, None, [hv])
    P.act(hv, hv[:, 4:5], hv[:, 1:2], AF.Exp, [hv])
    dS = P.dsem("dS")
    for i, off in enumerate((OFF_FF, OFF_GA, OFF_GB)):
        P.dma("sp", wsm, wsm[:, :, i * 8:(i + 1) * 8], wb_in[:, off:off + 8].rearrange("(c p) n -> p c n", p=128), [wtt["in"]], dS, key=i)

    def selbig(kind, h):
        o = (kind * 8 + h) * 6
        return sb_b[:, o:o + 6]

    dX = [P.dsem("dX%d" % i) for i in range(2)]
    dWh = [P.dsem("dWh%d" % i) for i in range(8)]
    dPK = P.dsem("dPK")
    dOut = P.dsem("dOut")
    dG = P.dsem("dG")
    dTail = [P.dsem("dTl%d" % i) for i in range(12)]

    def wslice(wb, c0, n):
        return wb[:, c0:c0 + n].rearrange("(c p) n -> p c n", p=128)

    def rstd_from_ss(stt_, cin, cout, n, scale):
        P.tsc("dve", stt_, stt_[:, cout:cout + n], stt_[:, cin:cin + n], scale, EPS, ALU.mult, ALU.add, [stt_])
        P.act(stt_, stt_[:, cout:cout + n], stt_[:, cout:cout + n], AF.Ln, [stt_])
        P.act(stt_, stt_[:, cout:cout + n], stt_[:, cout:cout + n], AF.Exp, [stt_], scale=-0.5)

    def rmsnorm_tile(xin, xin_tt, gam_ap, gam_tt, stt_, junk, outb):
        P.act(junk, junk[:], xin, AF.Square, [xin_tt], accum=stt_[:, 0:1], extra_w=[stt_])
        rstd_from_ss(stt_, 0, 1, 1, 1.0 / D)
        P.stt(outb, outb[:], xin, stt_[:, 1:2], gam_ap, ALU.mult, ALU.mult, [xin_tt, stt_, gam_tt])

    xcnt = [0]
    P.barrier()

    for s in range(NSEQ):
        AR.reset()
        gam1 = AR.alloc("gam1", [128, D], F32)
        xt = [AR.alloc("xt%d" % i, [128, D], F32) for i in range(2)]
        sqj = AR.alloc("sqj", [128, D], F32)
        hnb = [AR.alloc("hnb%d" % i, [128, D], BF16) for i in range(2)]
        st4 = [AR.alloc("st4_%d" % i, [128, 8], F32) for i in range(2)]
        P.dma("sp", gam1, gam1[:], nrm_d[0:1, :].partition_broadcast(128), [], dG)
        for tt_ in range(NT):
            i = xcnt[0] % 2
            xcnt[0] += 1
            P.dma("sp", xt[i], xt[i][:], x_d[s, tt_ * 128:(tt_ + 1) * 128, :], [], dX[i])
            rmsnorm_tile(xt[i][:], xt[i], gam1[:], gam1, st4[i], sqj, hnb[i])
            for half in range(2):
                pbt = pb[half]
                for c in range(4):
                    cc = half * 4 + c
                    P.tr(pbt, pbt[:, c * 128:(c + 1) * 128], hnb[i][:, cc * 128:(cc + 1) * 128], ident_b, [hnb[i], cbb], inc=(c == 3), key=c)
                P.act(hnT, hnT[:, half * 4:half * 4 + 4, tt_ * 128:(tt_ + 1) * 128],
                      pbt[:, 0:512].rearrange("p (c n) -> p c n", c=4), AF.Copy, [pbt], key=(tt_, half))
        P.barrier()

        PK, cols = alloc_pk()
        for n in PK:
            P.op("pool", lambda e, n=n: e.memset(PK[n][:], 1.0), writes=[PK[n]], dur=4000.0)
        rows = {n: AR.alloc("row_" + n, [8, T], F32) for n in ("nlf", "ncum", "negg", "nlb", "L", "LB", "r1", "r2")}
        rows["tmp"] = rows["r1"]
        pcs = [AR.alloc("pc%d" % i, [8, T], BF16) for i in range(3)] * 3
        nlf, ncum, negg, nlb, Lr, LBr, tmp = (rows[n] for n in ("nlf", "ncum", "negg", "nlb", "L", "LB", "tmp"))
        Lc = AR.alloc("Lc", [128, 8], F32)
        nlc = AR.alloc("nlc", [128, 8], F32)
        t8 = AR.alloc("t8", [128, 8], F32)
        for b in range(NB):
            cs = slice(b * BW, (b + 1) * BW)
            for i in range(3):
                pst = pf[i]
                for c in range(8):
                    P.mm(pst, pst[0:8, 0:BW], wsm[:, c, i * 8:(i + 1) * 8], hnT[:, c, cs], [wsm, hnT], start=(c == 0), stop=(c == 7), inc=(c == 7))
            P.act(tmp, tmp[:, cs], pf[0][0:8, 0:BW], AF.Exp, [pf[0], hv], bias=hv[:, 3:4], scale=-1.0)
            P.act(nlf, nlf[:, cs], tmp[:, cs], AF.Ln, [tmp], bias=1.0)
            P.act(tmp, tmp[:, cs], pf[1][0:8, 0:BW], AF.Exp, [pf[1], hv], bias=hv[:, 2:3], scale=1.0)
            P.act(negg, negg[:, cs], tmp[:, cs], AF.Ln, [tmp], bias=1.0)
            P.tsc("dve", negg, negg[:, cs], negg[:, cs], hv[:, 4:5], None, ALU.mult, None, [negg, hv])
            P.act(tmp, tmp[:, cs], pf[2][0:8, 0:BW], AF.Exp, [pf[2]], scale=-1.0)
            P.act(nlb, nlb[:, cs], tmp[:, cs], AF.Ln, [tmp], bias=1.0)
        for tt_ in range(NT):
            cs = slice(tt_ * 128, (tt_ + 1) * 128)
            init = 0.0 if tt_ == 0 else ncum[:, tt_ * 128 - 1:tt_ * 128]
            P.op("dve", lambda e, cs=cs, init=init: e.tensor_tensor_scan(out=ncum[:, cs], data0=onesr[:], data1=nlf[:, cs], initial=init, op0=ALU.mult, op1=ALU.add),
                 reads=[onesr, nlf, ncum], writes=[ncum])
            P.op("dve", lambda e, cs=cs: e.tensor_tensor_scan(out=Lr[:, cs], data0=onesr[:], data1=negg[:, cs], initial=0.0, op0=ALU.mult, op1=ALU.add),
                 reads=[onesr, negg], writes=[Lr])
        P.tt("dve", LBr, LBr[:], Lr[:], nlb[:], ALU.add, [Lr, nlb])
        r1, r2 = rows["r1"], rows["r2"]
        for k, (src, pk) in enumerate(((ncum, PK["c"]), (Lr, PK["L"]), (LBr, PK["LB"]))):
            pc = pcs[3 * k:3 * k + 3]
            P.cp("dve", pc[0], pc[0][:], src[:], [src])
            P.tt("dve", r1, r1[:], src[:], pc[0][:], ALU.subtract, [src, pc[0]])
            P.cp("dve", pc[1], pc[1][:], r1[:], [r1])
            P.tt("dve", r2, r2[:], r1[:], pc[1][:], ALU.subtract, [r1, pc[1]])
            P.cp("dve", pc[2], pc[2][:], r2[:], [r2])
            for j in range(3):
                P.dma("sp", pk, pk[32 * j:32 * j + 8, :], pc[j][:], [pc[j]], dPK, key=j)
        for tt_ in range(NT):
            cs = slice(tt_ * 128, (tt_ + 1) * 128)
            pst = pf[3 + tt_ % 2]
            P.tr(pst, pst[:, 0:8], Lr[:, cs], idf8[:], [Lr, idf8], inc=False)
            P.tr(pst, pst[:, 8:16], nlb[:, cs], idf8[:], [nlb, idf8], inc=True)
            P.cp("dve", Lc, Lc[:], pst[:, 0:8], [pst])
            P.cp("dve", nlc, nlc[:], pst[:, 8:16], [pst])
            P.mm(pst, pst[:, 16:24], sl_f[:], Lc[:], [sl_f, Lc])
            P.tt("dve", t8, t8[:], Lc[:], nlc[:], ALU.add, [Lc, nlc])
            P.act(cols, cols[:, tt_, 0, :], t8[:], AF.Exp, [t8], scale=-1.0, key=(tt_, 0))
            P.act(cols, cols[:, tt_, 1, :], nlc[:], AF.Exp, [nlc], scale=-1.0, key=(tt_, 1))
            P.tt("dve", t8, t8[:], Lc[:], pst[:, 16:24], ALU.subtract, [Lc, pst])
            P.act(cols, cols[:, tt_, 2, :], t8[:], AF.Exp, [t8], key=(tt_, 2))
            P.act(cols, cols[:, tt_, 3, :], pst[:, 16:24], AF.Exp, [pst], scale=-1.0, key=(tt_, 3))
            P.act(cols, cols[:, tt_, 4, :], Lc[:], AF.Exp, [Lc], scale=-1.0, key=(tt_, 4))
        P.barrier()

        PK, cols = alloc_pk()
        wh = [[AR.alloc("wh%d_%d" % (i, j), [128, 8, 128], BF16) for j in range(3)] for i in range(2)]
        qT = AR.alloc("qT", [128, T], BF16)
        kT = AR.alloc("kT", [128, T], BF16)
        vtok = AR.alloc("vtok", [128, NT, 128], BF16)
        QA = AR.alloc("QA", [6, T], BF16)
        KA = AR.alloc("KA", [6, T], BF16)
        pT = [AR.alloc("pT%d" % i, [128, BW], BF16) for i in range(3)]
        rsb = AR.alloc("rsb", [128, BW], F32)
        for h in range(H):
            hs = h % 2
            for j, off in enumerate((OFF_FQ, OFF_FK, OFF_FV)):
                P.dma("sp", wh[hs][j], wh[hs][j][:], wslice(wb_in, off + h * 128, 128), [wtt["in"]], dWh[hs * 3 + j])
            for b in range(NB):
                cs = slice(b * BW, (b + 1) * BW)
                for j, dst in ((0, qT), (1, kT)):
                    pst = pf[j]
                    for c in range(8):
                        P.mm(pst, pst[:, 0:BW], wh[hs][j][:, c, :], hnT[:, c, cs], [wh[hs][j], hnT], start=(c == 0), stop=(c == 7), inc=(c == 7))
                    if j == 0:
                        P.act(dst, dst[:, cs], pst[:, 0:BW], AF.Copy, [pst], scale=DH ** -0.5)
                    else:
                        P.cp("dve", dst, dst[:, cs], pst[:, 0:BW], [pst])
                for kind, dst in ((0, QA), (1, KA)):
                    pst = pf[3]
                    P.mm(pst, pst[0:6, 0:BW], selbig(kind, h), PK["c"][:, cs], [sb_b, PK["c"]])
                    P.cp("dve", dst, dst[:, cs], pst[0:6, 0:BW], [pst])
            for t0 in range(0, NT, 4):
                n4 = min(4, NT - t0)
                pst = pf[2]
                for ti in range(n4):
                    tcs = slice((t0 + ti) * 128, (t0 + ti + 1) * 128)
                    for c in range(8):
                        P.mm(pst, pst[:, ti * 128:(ti + 1) * 128], hnT[:, c, tcs], wh[hs][2][:, c, :], [wh[hs][2], hnT], start=(c == 0), stop=(c == 7), inc=(c == 7 and ti == n4 - 1), key=ti)
                P.act(vtok, vtok[:, t0:t0 + n4, :], pst[:, 0:n4 * 128].rearrange("p (t n) -> p t n", t=n4), AF.Copy, [pst], key=t0)
            pcount = 0
            for qb in range(NB):
                ot, sm = pf[4], pf[5]
                nk = QPB * qb + QPB
                for kt in range(nk):
                    j = max(0, kt - QPB * qb)
                    c0 = 128 * j
                    ncol = BW - c0
                    qcs = slice(qb * BW + c0, (qb + 1) * BW)
                    kcs = slice(kt * 128, (kt + 1) * 128)
                    diag = kt >= QPB * qb
                    sps = pf[pcount % 2]
                    ptile = pT[pcount % 3]
                    pcount += 1
                    P.mm(sps, sps[:, c0:BW], kT[:, kcs], qT[:, qcs], [kT, qT], start=True, stop=False, inc=False)
                    P.mm(sps, sps[:, c0:BW], KA[:, kcs], QA[:, qcs], [KA, QA], start=False, stop=(not diag), inc=(not diag))
                    if diag:
                        P.mm(sps, sps[:, c0:BW], ident_b, mkw[:, 0:ncol], [cbb, mkw], start=False, stop=True)
                    P.act(ptile, ptile[:, c0:BW], sps[:, c0:BW], AF.Exp, [sps])
                    last = kt == nk - 1
                    P.mm(ot, ot[:, c0:BW], vtok[:, kt, :], ptile[:, c0:BW], [vtok, ptile], start=(kt == 0), stop=last, inc=False, skip=True)
                    P.mm(sm, sm[:, c0:BW], ones_b[:], ptile[:, c0:BW], [ones_b, ptile], start=(kt == 0), stop=last, inc=True, skip=True)
                P.op("dve", lambda e, sm=sm: e.reciprocal(out=rsb[:, 0:BW], in_=sm[:, 0:BW]), reads=[sm], writes=[rsb])
                P.tt("dve", yfoxT, yfoxT[:, h, qb * BW:(qb + 1) * BW], ot[:, 0:BW], rsb[:, 0:BW], ALU.mult, [ot, rsb], key=(h, qb))
        if debug and s == 0:
            P.dma("sp", None, dbg_fox, yfoxT[:], [yfoxT], dOut, write=False)
        P.barrier()

        for g in range(2):
            gdn_half(P, AR, locals(), s, g)
            P.barrier()
        if debug and s == 0:
            P.dma("sp", None, dbg_gdn, ygdnT[:], [ygdnT], dOut, write=False)

        tail_phase(P, AR, locals(), s)
        P.barrier()

    P.emit()
    P.close()
    return nc
def gdn_half(P, AR, L, s, g):
    T, NT = L["T"], L["NT"]
    hnT, ygdnT, cbb, ident_b = L["hnT"], L["ygdnT"], L["cbb"], L["ident_b"]
    pf, pb, wb_in, wtt, dWh, cwg, gnorm, sb_b = L["pf"], L["pb"], L["wb_in"], L["wtt"], L["dWh"], L["cwg"], L["gnorm"], L["sb_b"]
    selbig, wslice = L["selbig"], L["wslice"]
    G = 4
    PK, cols = L["alloc_pk"]()
    A = AR.alloc
    wg = [A("wg%d" % qi, [128, G, 8, 128], BF16) for qi in range(4)]
    halo = A("halo", [128, 12, 3], F32)
    stage = [A("stage%d" % i, [128, 131], F32) for i in range(2)]
    acc = [A("acc%d" % i, [128, 128], F32) for i in range(2)]
    cT = A("cT", [128, 12, 128], BF16)
    QA1 = A("QA1", [6, G, 128], BF16)
    QA2 = A("QA2", [6, G, 128], BF16)
    KAg = A("KAg", [6, G, 128], BF16)
    tks = [{n: A("tk%d_" % p + n, [128, G, 128], BF16) for n in ("kn", "Rw", "kd", "qs", "qd", "Rv")} for p in range(2)]
    knTs = [A("knT%d" % p, [128, G, 128], BF16) for p in range(2)]
    qsTs = [A("qsT%d" % p, [128, G, 128], BF16) for p in range(2)]
    qdTs = [A("qdT%d" % p, [128, G, 128], BF16) for p in range(2)]
    E1Ts = [A("E1T%d" % p, [128, G, 128], BF16) for p in range(2)]
    E1s = [A("E1_%d" % p, [128, G, 128], BF16) for p in range(2)]
    E2Ts = [A("E2T%d" % p, [128, G, 128], BF16) for p in range(2)]
    NEB, NEBT = 2, 2
    Eb = [A("Eb%d" % i, [128, G, 128], BF16) for i in range(NEB)]
    EbT = [A("EbT%d" % i, [128, G, 128], BF16) for i in range(NEBT)]
    Tm = [A("Tm%d" % i, [128, G, 128], BF16) for i in range(2)]
    TTm = [A("TTm%d" % i, [128, G, 128], BF16) for i in range(2)]
    Y1s = A("Y1s", [128, G, 128], BF16)
    Y2s = A("Y2s", [128, G, 128], BF16)
    negwT = Y1s
    u_bf = Y2s
    S_f = A("S_f", [128, G, 128], F32)
    S_b = A("S_b", [128, G, 128], BF16)
    gzss = [A("gzs%d" % p, [128, G, 128], BF16) for p in range(2)]
    o_f = A("o_f", [128, G, 128], F32)
    on_b = A("on_b", [128, G, 128], BF16)
    junk = A("junk", [128, 128], BF16)
    ptmp = A("ptmp", [128, 128], F32)
    scs = [A("sc%d" % p, [128, 40], F32) for p in range(2)]
    if SCHED_DEBUG:
        print("gdn arena used", AR.off, "of", AR.n)

    offs = (OFF_GQ, OFF_GK, OFF_GV, OFF_GZ)
    for qi in range(4):
        for i in range(G):
            hh = g * G + i
            P.dma("sp", wg[qi], wg[qi][:, i, :, :], wslice(wb_in, offs[qi] + hh * 128, 128), [wtt["in"]], dWh[6], key=i)
    P.op("dve", lambda e: e.memset(S_f[:], 0.0), writes=[S_f])
    P.op("dve", lambda e: e.memset(S_b[:], 0.0), writes=[S_b])
    hsl = slice(g * G, (g + 1) * G)
    scale = DH ** -0.5

    def v4(t_, i):
        return t_[:, i, :]

    f4 = lambda pst: pst[:].rearrange("p (a b) -> p a b", a=G)

    def early(t):
            cs = slice(t * 128, (t + 1) * 128)
            tk, knT, qsT, qdT, sc = tks[t % 2], knTs[t % 2], qsTs[t % 2], qdTs[t % 2], scs[t % 2]
            E1T, E1, E2T = E1Ts[t % 2], E1s[t % 2], E2Ts[t % 2]
            AT, Am, attnT = E1T, E1, E2T
            gzs = gzss[t % 2]
            k = 0
            for qi in range(3):
                for i in range(G):
                    hh = g * G + i
                    ch = qi * 8 + hh
                    pst = pf[4 + k % 2]
                    stg = stage[k % 2]
                    ac = acc[k % 2]
                    k += 1
                    for c in range(8):
                        P.mm(pst, pst[:, 0:128], wg[qi][:, i, c, :], hnT[:, c, cs], [wg[qi], hnT], start=(c == 0), stop=(c == 7), inc=(c == 7))
                    P.act(stg, stg[:, 3:131], pst[:, 0:128], AF.Copy, [pst], key='x')
                    hi = qi * G + i
                    if t == 0:
                        P.op("pool", lambda e, stg=stg: e.memset(stg[:, 0:3], 0.0), writes=[stg], key='h')
                    else:
                        P.cp("pool", stg, stg[:, 0:3], halo[:, hi, :], [halo], key='h')
                    P.cp("pool", halo, halo[:, hi, :], stg[:, 128:131], [stg], key=hi)
                    if qi == 2 and POOL_VCONV:
                        P.tsc("pool", ac, ac[:], stg[:, 3:131], cwg[:, ch, 3:4], None, ALU.mult, None, [stg, cwg])
                        for tap in (2, 1, 0):
                            P.tsc("pool", ptmp, ptmp[:], stg[:, tap:tap + 128], cwg[:, ch, tap:tap + 1], None, ALU.mult, None, [stg, cwg])
                            P.tt("pool", ac, ac[:], ac[:], ptmp[:], ALU.add, [ac, ptmp])
                    else:
                        P.tsc("dve", ac, ac[:], stg[:, 3:131], cwg[:, ch, 3:4], None, ALU.mult, None, [stg, cwg])
                        for tap in (2, 1, 0):
                            P.stt(ac, ac[:], stg[:, tap:tap + 128], cwg[:, ch, tap:tap + 1], ac[:], ALU.mult, ALU.add, [stg, cwg, ac])
                    P.act(cT, cT[:, hi, :], ac[:], AF.Silu, [ac], key=hi)
            for i in range(G):
                for c in range(8):
                    P.mm(pf[5], pf[5][:, i * 128:(i + 1) * 128], hnT[:, c, cs], wg[3][:, i, c, :], [wg[3], hnT], start=(c == 0), stop=(c == 7), inc=(c == 7 and i == G - 1))
            P.act(gzs, gzs[:], pf[5][:].rearrange("p (a b) -> p a b", a=G), AF.Silu, [pf[5]])
            for dst, pk, kind, pst in ((QA1, PK["LB"], 0, pf[3]), (QA2, PK["L"], 0, pf[4]), (KAg, PK["L"], 1, pf[5])):
                for i in range(G):
                    P.mm(pst, pst[0:6, i * 128:(i + 1) * 128], selbig(kind, g * G + i), pk[:, cs], [sb_b, pk], inc=(i == G - 1))
                P.cp("dve", dst, dst[:], pst[0:6, :].rearrange("p (a b) -> p a b", a=G), [pst])
            for i in range(G):
                P.tr(pb[0], pb[0][:, i * 128:(i + 1) * 128], cT[:, 1 * G + i, :], ident_b, [cT, cbb], inc=False)
            for i in range(G):
                P.tr(pb[0], pb[0][:, 512 + i * 128:512 + (i + 1) * 128], cT[:, 0 * G + i, :], ident_b, [cT, cbb], inc=(i == G - 1))
            for i in range(G):
                P.tr(pb[1], pb[1][:, i * 128:(i + 1) * 128], cT[:, 2 * G + i, :], ident_b, [cT, cbb], inc=(i == G - 1))
            for i in range(G):
                P.act(junk, junk[:], pb[0][:, i * 128:(i + 1) * 128], AF.Square, [pb[0]], accum=sc[:, i:i + 1], extra_w=[sc])
                P.act(junk, junk[:], pb[0][:, 512 + i * 128:512 + (i + 1) * 128], AF.Square, [pb[0]], accum=sc[:, 4 + i:5 + i], extra_w=[sc])
            P.tsc("dve", sc, sc[:, 8:16], sc[:, 0:8], 1.0, EPS, ALU.mult, ALU.add, [sc])
            P.act(sc, sc[:, 8:16], sc[:, 8:16], AF.Ln, [sc])
            P.act(sc, sc[:, 8:16], sc[:, 8:16], AF.Exp, [sc], scale=-0.5)
            P.tt("dve", sc, sc[:, 16:20], sc[:, 8:12], cols[:, t, 0, hsl], ALU.mult, [sc, cols])
            P.tt("dve", sc, sc[:, 20:24], sc[:, 8:12], cols[:, t, 2, hsl], ALU.mult, [sc, cols])
            P.tsc("dve", sc, sc[:, 24:28], sc[:, 12:16], scale, None, ALU.mult, None, [sc])
            P.tt("dve", sc, sc[:, 28:32], sc[:, 24:28], cols[:, t, 4, hsl], ALU.mult, [sc, cols])
            k4 = pb[0][:, 0:512].rearrange("p (a b) -> p a b", a=G)
            q4 = pb[0][:, 512:1024].rearrange("p (a b) -> p a b", a=G)
            v4_ = pb[1][:, 0:512].rearrange("p (a b) -> p a b", a=G)
            for name, src, scl, srct in (("kn", k4, sc[:, 8:12], pb[0]), ("Rw", k4, sc[:, 16:20], pb[0]), ("kd", k4, sc[:, 20:24], pb[0]),
                                         ("qs", q4, sc[:, 24:28], pb[0]), ("qd", q4, sc[:, 28:32], pb[0])):
                P.tt("dve", tk[name], tk[name][:], src, bc_last(scl, 128), ALU.mult, [srct, sc])
            P.tt("dve", tk["Rv"], tk["Rv"][:], v4_, bc_last(cols[:, t, 1, hsl], 128), ALU.mult, [pb[1], cols])
            for i in range(G):
                P.tr(pb[1], pb[1][:, 512 + i * 128:512 + (i + 1) * 128], tk["kn"][:, i, :], ident_b, [tk["kn"], cbb], inc=(i == G - 1))
            P.act(knT, knT[:], pb[1][:, 512:1024].rearrange("p (a b) -> p a b", a=G), AF.Copy, [pb[1]])
            for i in range(G):
                P.tr(pb[0], pb[0][:, i * 128:(i + 1) * 128], tk["qs"][:, i, :], ident_b, [tk["qs"], cbb], inc=False)
            for i in range(G):
                P.tr(pb[0], pb[0][:, 512 + i * 128:512 + (i + 1) * 128], tk["qd"][:, i, :], ident_b, [tk["qd"], cbb], inc=(i == G - 1))
            P.act(qsT, qsT[:], pb[0][:, 0:512].rearrange("p (a b) -> p a b", a=G), AF.Copy, [pb[0]])
            P.act(qdT, qdT[:], pb[0][:, 512:1024].rearrange("p (a b) -> p a b", a=G), AF.Copy, [pb[0]])
            for i in range(G):
                cs_i = slice(i * 128, (i + 1) * 128)
                last = i == G - 1
                P.mm(pf[0], pf[0][:, cs_i], knT[:, i, :], knT[:, i, :], [knT], inc=last)
            for i in range(G):
                cs_i = slice(i * 128, (i + 1) * 128)
                P.mm(pf[1], pf[1][:, cs_i], knT[:, i, :], qsT[:, i, :], [knT, qsT], inc=(i == G - 1))
            for pst, lh, rh, mk in ((pf[2], KAg, QA1, 2), (pf[3], QA1, KAg, 3), (pf[4], KAg, QA2, 1)):
                for i in range(G):
                    cs_i = slice(i * 128, (i + 1) * 128)
                    P.mm(pst, pst[:, cs_i], lh[:, i, :], rh[:, i, :], [lh, rh], start=True, stop=False, inc=False)
                    P.mm(pst, pst[:, cs_i], ident_b, cbb[:, mk, :], [cbb], start=False, stop=True, inc=(i == G - 1))
            for dst, pst in ((E1T, pf[2]), (E1, pf[3]), (E2T, pf[4])):
                P.act(dst, dst[:], pst[:].rearrange("p (a b) -> p a b", a=G), AF.Exp, [pst])
            P.tt("dve", AT, AT[:], f4(pf[0]), E1T[:], ALU.mult, [pf[0], E1T])
            P.tt("dve", Am, Am[:], f4(pf[0]), E1[:], ALU.mult, [pf[0], E1])
            P.tt("dve", attnT, attnT[:], f4(pf[1]), E2T[:], ALU.mult, [pf[1], E2T])
    def rec(t):
            cs = slice(t * 128, (t + 1) * 128)
            tk, knT, qsT, qdT, sc = tks[t % 2], knTs[t % 2], qsTs[t % 2], qdTs[t % 2], scs[t % 2]
            E1T, E1, E2T = E1Ts[t % 2], E1s[t % 2], E2Ts[t % 2]
            AT, Am, attnT = E1T, E1, E2T
            gzs = gzss[t % 2]
            cur = 0
            P.tt("pool", Eb[0], Eb[0][:], Am[:], bc_mid(cbb[:, 4, :], G), ALU.mult, [Am, cbb])
            P.tt("pool", EbT[0], EbT[0][:], AT[:], bc_mid(cbb[:, 11, :], G), ALU.mult, [AT, cbb])
            P.tt("dve", Tm[cur], Tm[cur][:], bc_mid(ident_b, G), Eb[0][:], ALU.subtract, [cbb, Eb[0]])
            P.tt("dve", TTm[cur], TTm[cur][:], bc_mid(ident_b, G), EbT[0][:], ALU.subtract, [cbb, EbT[0]])
            for l in range(1, 7):
                lastl = l == 6
                ebuf = Eb[l % NEB]
                ebtbuf = EbT[l % NEBT]
                P.tt("pool", ebuf, ebuf[:], Am[:], bc_mid(cbb[:, 4 + l, :], G), ALU.mult, [Am, cbb])
                if not lastl:
                    P.tt("pool", ebtbuf, ebtbuf[:], AT[:], bc_mid(cbb[:, 11 + l, :], G), ALU.mult, [AT, cbb])
                for i in range(G):
                    P.mm(pf[0], pf[0][:, i * 128:(i + 1) * 128], ebuf[:, i, :], TTm[cur][:, i, :], [ebuf, TTm[cur]], inc=(i == G - 1))
                if not lastl:
                    for i in range(G):
                        P.mm(pf[1], pf[1][:, i * 128:(i + 1) * 128], ebtbuf[:, i, :], Tm[cur][:, i, :], [ebtbuf, Tm[cur]], inc=(i == G - 1))
                P.act(Y2s, Y2s[:], f4(pf[0]), AF.Copy, [pf[0]])
                if not lastl:
                    P.act(Y1s, Y1s[:], f4(pf[1]), AF.Copy, [pf[1]])
                for i in range(G):
                    P.mm(pf[2], pf[2][:, i * 128:(i + 1) * 128], Tm[cur][:, i, :], Y2s[:, i, :], [Tm[cur], Y2s], inc=(i == G - 1))
                if not lastl:
                    for i in range(G):
                        P.mm(pf[3], pf[3][:, i * 128:(i + 1) * 128], TTm[cur][:, i, :], Y1s[:, i, :], [TTm[cur], Y1s], inc=(i == G - 1))
                P.tt("dve", TTm[1 - cur], TTm[1 - cur][:], TTm[cur][:], f4(pf[2]), ALU.subtract, [TTm[cur], pf[2]])
                if not lastl:
                    P.tt("dve", Tm[1 - cur], Tm[1 - cur][:], Tm[cur][:], f4(pf[3]), ALU.subtract, [Tm[cur], pf[3]])
                cur = 1 - cur
            TT_ = TTm[cur]
            TTfin[t] = TT_
    def late(t):
            cs = slice(t * 128, (t + 1) * 128)
            tk, knT, qsT, qdT, sc = tks[t % 2], knTs[t % 2], qsTs[t % 2], qdTs[t % 2], scs[t % 2]
            E1T, E1, E2T = E1Ts[t % 2], E1s[t % 2], E2Ts[t % 2]
            AT, Am, attnT = E1T, E1, E2T
            gzs = gzss[t % 2]
            TT_ = TTfin[t]
            for i in range(G):
                P.mm(pf[4], pf[4][:, i * 128:(i + 1) * 128], tk["Rw"][:, i, :], TT_[:, i, :], [tk["Rw"], TT_], inc=(i == G - 1))
            P.act(negwT, negwT[:], f4(pf[4]), AF.Copy, [pf[4]], scale=-1.0)
            for i in range(G):
                cs_i = slice(i * 128, (i + 1) * 128)
                P.mm(pf[5], pf[5][:, cs_i], TT_[:, i, :], tk["Rv"][:, i, :], [TT_, tk["Rv"]], start=True, stop=False, inc=False)
                P.mm(pf[5], pf[5][:, cs_i], negwT[:, i, :], S_b[:, i, :], [negwT, S_b], start=False, stop=True, inc=(i == G - 1))
            P.act(u_bf, u_bf[:], f4(pf[5]), AF.Copy, [pf[5]])
            for i in range(G):
                cs_i = slice(i * 128, (i + 1) * 128)
                P.mm(pf[0], pf[0][:, cs_i], qdT[:, i, :], S_b[:, i, :], [qdT, S_b], start=True, stop=False, inc=False)
                P.mm(pf[0], pf[0][:, cs_i], attnT[:, i, :], u_bf[:, i, :], [attnT, u_bf], start=False, stop=True, inc=(i == G - 1))
            for i in range(G):
                cs_i = slice(i * 128, (i + 1) * 128)
                P.mm(pf[1], pf[1][:, cs_i], tk["kd"][:, i, :], u_bf[:, i, :], [tk["kd"], u_bf], inc=(i == G - 1))
            P.tt("dve", S_f, S_f[:], S_f[:], bc_last(cols[:, t, 3, hsl], 128), ALU.mult, [S_f, cols])
            P.tt("dve", S_f, S_f[:], S_f[:], f4(pf[1]), ALU.add, [S_f, pf[1]])
            P.cp("pool", S_b, S_b[:], S_f[:], [S_f])
            for i in range(G):
                P.act(junk, junk[:], pf[0][:, i * 128:(i + 1) * 128], AF.Square, [pf[0]], accum=sc[:, 32 + i:33 + i], extra_w=[sc])
            P.tsc("dve", sc, sc[:, 36:40], sc[:, 32:36], 1.0 / DH, EPS, ALU.mult, ALU.add, [sc])
            P.act(sc, sc[:, 36:40], sc[:, 36:40], AF.Ln, [sc])
            P.act(sc, sc[:, 36:40], sc[:, 36:40], AF.Exp, [sc], scale=-0.5)
            P.tt("dve", o_f, o_f[:], f4(pf[0]), bc_last(sc[:, 36:40], 128), ALU.mult, [pf[0], sc])
            P.tt("dve", o_f, o_f[:], o_f[:], bc_mid(gnorm[:], G), ALU.mult, [o_f, gnorm])
            P.tt("dve", on_b, on_b[:], o_f[:], gzs[:], ALU.mult, [o_f, gzs])
            for i in range(G):
                P.tr(pb[1], pb[1][:, i * 128:(i + 1) * 128], on_b[:, i, :], ident_b, [on_b, cbb], inc=(i == G - 1))
            P.act(ygdnT, ygdnT[:, g * G:(g + 1) * G, cs], pb[1][:, 0:512].rearrange("p (a b) -> p a b", a=G), AF.Copy, [pb[1]], key=(g, t))

    TTfin = {}
    early(0)
    for t in range(NT):
        rec(t)
        if t + 1 < NT:
            early(t + 1)
        late(t)


def tail_phase(P, AR, L, s):
    T = L["T"]
    hnT, yfoxT, ygdnT, cbb, ident_b, cwf = L["hnT"], L["yfoxT"], L["ygdnT"], L["cbb"], L["ident_b"], L["cwf"]
    pf, pb, wtt, dG, dOut = L["pf"], L["pb"], L["wtt"], L["dG"], L["dOut"]
    wb_in, wb_bf, wb_bg, wb_out, wb_up, wb_dn = L["wb_in"], L["wb_bf"], L["wb_bg"], L["wb_out"], L["wb_up"], L["wb_dn"]
    nrm_d, x_d, out_d, wslice, rmsnorm_tile = L["nrm_d"], L["x_d"], L["out_d"], L["wslice"], L["rmsnorm_tile"]
    BWT = min(512, T)
    QT = BWT // 128
    NBT = T // BWT
    SPLIT_E = (QT == 4)
    AR.reset()
    A = AR.alloc
    gam2 = A("gam2", [128, D], F32)
    gam3 = A("gam3", [128, D], F32)
    h1 = A("h1", [128, QT, D], F32)
    junkb = A("junkb", [128, D], BF16)
    hn2b = A("hn2b", [128, D], BF16)
    hn2T = A("hn2T", [128, 8, BWT], BF16)
    aT = A("aT", [128, NF, BWT], BF16)
    halo = A("fhalo", [128, 44, 2], F32)
    st4 = A("st4t", [128, 8], F32)
    base = AR.off
    NWM = 2
    wm = [[A("wm%d_%d" % (k, i), [128, 8, 128], BF16) for i in range(4)] for k in range(NWM)]
    NWO = 2
    woc = [A("woc%d" % i, [128, D], BF16) for i in range(NWO)]
    xress = [A("xres%d" % i, [128, D], F32) for i in range(2)]
    yT = A("yT", [128, 8, BWT], BF16)
    sg = [A("sg%d" % i, [128, BWT], F32) for i in range(2)]
    end_early = AR.off
    AR.off = base
    NWU, NWD = 3, 4
    wupc = [A("wupc%d" % i, [128, 8, 256], BF16) for i in range(NWU)]
    stage = [A("fstage%d" % i, [128, 2 + BWT], F32) for i in range(2)]
    acc = [A("facc%d" % i, [128, BWT], F32) for i in range(2)]
    sil = A("sil", [128, BWT], F32)
    wdc = [A("wdc%d" % i, [128, D], BF16) for i in range(NWD)]
    ots = [A("ot%d" % i, [128, D], F32) for i in range(2)]
    AR.off = max(AR.off, end_early)
    if SCHED_DEBUG:
        print('tail arena used', AR.off, 'early', end_early, 'of', AR.n)
    sqj = junkb
    P.dma("sp", gam2, gam2[:], nrm_d[1:2, :].partition_broadcast(128), [], dG)
    P.dma("sp", gam3, gam3[:], nrm_d[2:3, :].partition_broadcast(128), [], dG)
    cnt = {"wm": 0, "wo": 0, "wu": 0, "wd": 0}
    for b in range(NBT):
        cs = slice(b * BWT, (b + 1) * BWT)
        for dc in range(8):
            srcs = ((wb_in, OFF_GF + dc * 128, hnT), (wb_in, OFF_GG + dc * 128, hnT), (wb_bf, dc * 128, yfoxT), (wb_bg, dc * 128, ygdnT))
            keys = ("in", "in", "bf", "bg")
            wset = wm[cnt["wm"] % NWM]
            cnt["wm"] += 1
            for i, (wb, c0, _) in enumerate(srcs):
                P.dma("sp", wset[i], wset[i][:], wslice(wb, c0, 128), [wtt[keys[i]]], True)
            for i, (_, _, act_) in enumerate(srcs):
                for c in range(8):
                    P.mm(pf[i], pf[i][:, 0:BWT], wset[i][:, c, :], act_[:, c, cs], [wset[i], act_], start=(c == 0), stop=(c == 7))
            P.act(sg[0], sg[0][:], pf[0][:, 0:BWT], AF.Sigmoid, [pf[0]])
            P.act(sg[1], sg[1][:], pf[1][:, 0:BWT], AF.Sigmoid, [pf[1]])
            P.tt("dve", sg[0], sg[0][:], sg[0][:], pf[2][:, 0:BWT], ALU.mult, [sg[0], pf[2]])
            P.tt("dve", sg[1], sg[1][:], sg[1][:], pf[3][:, 0:BWT], ALU.mult, [sg[1], pf[3]])
            P.tt("dve", yT, yT[:, dc, :], sg[0][:], sg[1][:], ALU.add, [sg[0], sg[1]], key=dc)
        for c in range(8):
            w = woc[cnt["wo"] % NWO]
            cnt["wo"] += 1
            P.dma("sp", w, w[:], wb_out[c * 128:(c + 1) * 128, :], [wtt["out"]], True)
            for ti in range(QT):
                for hf in range(2):
                    pst = pf[ti * 2 + hf]
                    P.mm(pst, pst[:, 0:512], yT[:, c, ti * 128:(ti + 1) * 128], w[:, hf * 512:(hf + 1) * 512], [yT, w], start=(c == 0), stop=(c == 7))
        for ti in range(QT):
            tok0 = b * BWT + ti * 128
            xres = xress[ti % 2]
            P.dma("sp", xres, xres[:], x_d[s, tok0:tok0 + 128, :], [], True)
            for hf in range(2):
                pst = pf[ti * 2 + hf]
                P.tt("dve", h1, h1[:, ti, hf * 512:(hf + 1) * 512], xres[:, hf * 512:(hf + 1) * 512], pst[:, 0:512], ALU.add, [xres, pst], key=(ti, hf))
        for ti in range(QT):
            rmsnorm_tile(h1[:, ti, :], h1, gam2[:], gam2, st4, sqj, hn2b)
            for half in range(2):
                pbt = pb[half]
                for c in range(4):
                    cc = half * 4 + c
                    P.tr(pbt, pbt[:, c * 128:(c + 1) * 128], hn2b[:, cc * 128:(cc + 1) * 128], ident_b, [hn2b, cbb], key=c)
                P.act(hn2T, hn2T[:, half * 4:half * 4 + 4, ti * 128:(ti + 1) * 128],
                      pbt[:, 0:512].rearrange("p (c n) -> p c n", c=4), AF.Copy, [pbt], key=(ti, half))
        P.barrier()
        for f in range(NF):
            w = wupc[cnt["wu"] % NWU]
            cnt["wu"] += 1
            P.dma("sp", w, w[:, :, 0:128], wslice(wb_up, f * 128, 128), [wtt["up"]], True, key=0)
            P.dma("sp", w, w[:, :, 128:256], wslice(wb_up, DFF + f * 128, 128), [wtt["up"]], True, key=1)
            for part in range(2):
                pst = pf[(6 + part) if SPLIT_E else part]
                stg = stage[part]
                ac = acc[part]
                ch = part * NF + f
                for c in range(8):
                    P.mm(pst, pst[:, 0:BWT], w[:, c, part * 128:(part + 1) * 128], hn2T[:, c, :], [w, hn2T], start=(c == 0), stop=(c == 7))
                P.act(stg, stg[:, 2:2 + BWT], pst[:, 0:BWT], AF.Copy, [pst], key='x')
                if b == 0:
                    P.op("pool", lambda e, stg=stg: e.memset(stg[:, 0:2], 0.0), writes=[stg], key='h')
                else:
                    P.cp("pool", stg, stg[:, 0:2], halo[:, ch, :], [halo], key='h')
                P.cp("pool", halo, halo[:, ch, :], stg[:, BWT:BWT + 2], [stg], key=ch)
                P.tsc("dve", ac, ac[:], stg[:, 2:2 + BWT], cwf[:, ch, 2:3], None, ALU.mult, None, [stg, cwf])
                for tap in (1, 0):
                    P.stt(ac, ac[:], stg[:, tap:tap + BWT], cwf[:, ch, tap:tap + 1], ac[:], ALU.mult, ALU.add, [stg, cwf, ac])
            P.act(sil, sil[:], acc[0][:], AF.Silu, [acc[0]])
            P.tt("dve", aT, aT[:, f, :], sil[:], acc[1][:], ALU.mult, [sil, acc[1]], key=f)
        passes = ([list(range(QT - 1)), [QT - 1]] if SPLIT_E else [list(range(QT))])
        for tis in passes:
            for f in range(NF):
                w = wdc[cnt["wd"] % NWD]
                cnt["wd"] += 1
                P.dma("sp", w, w[:], wb_dn[f * 128:(f + 1) * 128, :], [wtt["dn"]], True)
                for ti in tis:
                    for hf in range(2):
                        pst = pf[ti * 2 + hf]
                        P.mm(pst, pst[:, 0:512], aT[:, f, ti * 128:(ti + 1) * 128], w[:, hf * 512:(hf + 1) * 512], [aT, w], start=(f == 0), stop=(f == NF - 1))
        for ti in range(QT):
            tok0 = b * BWT + ti * 128
            ot = ots[ti % 2]
            for hf in range(2):
                pst = pf[ti * 2 + hf]
                P.tt("dve", ot, ot[:, hf * 512:(hf + 1) * 512], h1[:, ti, hf * 512:(hf + 1) * 512], pst[:, 0:512], ALU.add, [h1, pst], key=hf)
            rmsnorm_tile(ot[:], ot, gam3[:], gam3, st4, sqj, ot)
            P.dma("sp", None, out_d[s, tok0:tok0 + 128, :], ot[:], [ot], dOut, write=False)
        if b < NBT - 1:
            P.barrier()


def make_in_maps(inp, NSEQ, ncores, same=False):
    cb, sel_last, selbig = host_consts()
    f = lambda a: np.ascontiguousarray(np.asarray(a, dtype=np.float32))
    common = {
        "w_in": f(inp["w_in"][0]), "w_bf": f(inp["w_branch_fox"][0]), "w_bg": f(inp["w_branch_gdn"][0]),
        "w_out": f(inp["w_out"][0]), "w_up": f(inp["w_up"][0]), "w_down": f(inp["w_down"][0]),
        "norms": f(np.stack([np.asarray(inp["norm_mix"])[0], np.asarray(inp["norm_ffn"])[0], np.asarray(inp["norm_final"])], 0)),
        "gdn_norm": f(inp["gdn_norm"]),
        "cwg": f(np.asarray(inp["gdn_conv_w"])[0].T.reshape(24, 128, 4).transpose(1, 0, 2)),
        "cwf": f(np.asarray(inp["ffn_conv_w"])[0].T.reshape(44, 128, 3).transpose(1, 0, 2)),
        "hv": f(np.stack([np.asarray(inp["fox_f_bias"])[0], np.asarray(inp["gdn_a_log"])[0], np.asarray(inp["gdn_dt_bias"])[0]], 1)),
        "cb": cb, "sel_last": sel_last, "selbig": selbig,
    }
    x = np.asarray(inp["x"], np.float32)
    maps = []
    for c in range(ncores):
        m = dict(common)
        m["x"] = f(x[0:NSEQ] if same else x[c * NSEQ:(c + 1) * NSEQ])
        maps.append(m)
    return maps


_NC_CACHE = {}


def kernel(**inputs):
    x = np.asarray(inputs["x"])
    B, T, _ = x.shape
    ncores = 8
    NSEQ = B // ncores
    key = (T, NSEQ)
    if key not in _NC_CACHE:
        _NC_CACHE[key] = build(T, NSEQ)
    nc = _NC_CACHE[key]
    maps = make_in_maps(inputs, NSEQ, ncores)
    res = run_bass_kernel_spmd(nc, maps, core_ids=list(range(ncores)))
    out = np.concatenate([np.asarray(r["out"]) for r in res.results], axis=0)
    return out.astype(np.float32)
```

```python
import numpy as np
import concourse.bass as bass
import concourse.mybir as mybir
from concourse.bass_utils import run_bass_kernel_spmd

F32 = mybir.dt.float32
BF16 = mybir.dt.bfloat16
AF = mybir.ActivationFunctionType
ALU = mybir.AluOpType
AX = mybir.AxisListType

D = 1024
H = 8
DH = 128
DFF = 2816
NF = 22
OFF_FQ, OFF_FK, OFF_FV, OFF_FF = 0, 1024, 2048, 3072
OFF_GQ, OFF_GK, OFF_GV, OFF_GA, OFF_GB, OFF_GZ = 3080, 4104, 5128, 6152, 6160, 6168
OFF_GF, OFF_GG = 7192, 8216
DIN = 9240
EPS = 1e-6
NEG = -30000.0
NCB = 18
import os
SCHED_DEBUG = bool(os.environ.get("SCHED_DEBUG"))
ATTACH_WAITS = False
PRIO_BL = True
N_SW_SEMS = 2


class TT:
    __slots__ = ("ap", "w", "r", "name", "pg")

    def __init__(self, ap, name=""):
        self.ap = ap
        self.w = {}
        self.r = []
        self.pg = []
        self.name = name

    def __getitem__(self, idx):
        return self.ap[idx]


class TTV:
    __slots__ = ("p", "ap", "name")

    def __init__(self, p, ap):
        self.p = p
        self.ap = ap
        self.name = p.name + "_v"

    def __getitem__(self, idx):
        return self.ap[idx]

    w = property(lambda self: self.p.w, lambda self, v: setattr(self.p, "w", v))
    r = property(lambda self: self.p.r, lambda self, v: setattr(self.p, "r", v))
    pg = property(lambda self: self.p.pg, lambda self, v: setattr(self.p, "pg", v))


class _Op:
    __slots__ = ("id", "eng", "fn", "deps", "dur", "lat", "is_dma", "multi", "seg", "users", "nun", "start", "fin",
                 "needs_inc", "incval", "dsem", "dcount", "dprev", "waits", "qpos")


def _free_n(ap):
    n = 1
    for d in ap.shape[1:]:
        n *= d
    return n


class Prog:
    ENGS = ("pe", "act", "dve", "pool", "sp")
    NDS = 32
    WIN = 48
    XLAT = 150.0

    def __init__(self, nc, same_sync=True):
        self.nc = nc
        self.sems = []
        self._ctx = []
        self.esem = {n: self._newsem("e_" + n) for n in self.ENGS}
        self.dsems_ = [self._newsem("dq%d" % i) for i in range(self.NDS)]
        self.ops = []
        self.seg = 0
        self.nsegs = 1
        self.seginfo = []
        self.lastpe = {}

    def _newsem(self, name):
        cm = self.nc.semaphore(name)
        h = cm.__enter__()
        self._ctx.append(cm)
        self.sems.append(h)
        return len(self.sems) - 1

    def dsem(self, name=None):
        return True

    def sbuf(self, name, shape, dtype):
        cm = self.nc.sbuf_tensor(name, shape, dtype)
        t = cm.__enter__()
        self._ctx.append(cm)
        return t

    def psum(self, name, shape, dtype):
        cm = self.nc.psum_tensor(name, shape, dtype)
        t = cm.__enter__()
        self._ctx.append(cm)
        return t

    def st(self, name, shape, dtype):
        return TT(self.sbuf(name, shape, dtype)[:], name)

    def barrier(self):
        self.seg += 1
        self.nsegs = self.seg + 1

    def op(self, eng, fn, reads=(), writes=(), inc=True, dsem=None, key=None, dur=100.0, lat=None, multi=False):
        o = _Op()
        o.id = len(self.ops)
        o.eng = eng
        o.fn = fn
        o.dur = dur
        o.is_dma = dsem is not None
        o.lat = lat if lat is not None else dur
        o.multi = multi
        o.seg = self.seg
        deps = {}

        def add(i, raw):
            if i is None:
                return
            if raw or i not in deps:
                deps[i] = raw or deps.get(i, False)

        for t in reads:
            if t is None:
                continue
            for i in t.w.values():
                add(i, True)
        for t in writes:
            if t is None:
                continue
            if eng == "pe":
                tp_ = t.p if isinstance(t, TTV) else t
                add(self.lastpe.get(id(tp_)), False)
                self.lastpe[id(tp_)] = o.id
            if key is None:
                for i in t.w.values():
                    add(i, False)
                for i in t.r:
                    add(i, False)
                t.w = {None: o.id}
                t.r = []
                t.pg = []
            else:
                if t.r:
                    t.pg = list(t.r) + list(t.w.values())
                    for i in t.pg:
                        add(i, False)
                    t.w = {key: o.id}
                    t.r = []
                else:
                    add(t.w.get(key), False)
                    add(t.w.get(None), False)
                    for i in t.pg:
                        add(i, False)
                    t.w[key] = o.id
        for t in reads:
            if t is None or t in writes:
                continue
            t.r.append(o.id)
        o.deps = deps
        self.ops.append(o)
        return o.id

    def mm(self, ot, out, lhsT, rhs, reads, start=True, stop=True, inc=True, skip=False, key=None):
        n = _free_n(rhs)
        f = 4.0 if rhs.dtype == F32 else 1.0
        self.op("pe", lambda e: e.matmul(out, lhsT=lhsT, rhs=rhs, start=start, stop=stop, skip_group_check=skip),
                reads=reads, writes=[ot], key=key, dur=f * (73.0 + 0.362 * n))

    def tr(self, ot, out, in_, ident, reads, inc=True, key=None):
        f = 4.0 if in_.dtype == F32 else 1.0
        self.op("pe", lambda e: e.transpose(out=out, in_=in_, identity=ident), reads=reads, writes=[ot], key=key, dur=f * 119.0)

    def act(self, ot, out, in_, func, reads, bias=None, scale=None, accum=None, extra_w=(), key=None):
        kw = {}
        d = 260.0 + _free_n(in_) / 1.2
        if bias is not None:
            kw["bias"] = bias
            if not isinstance(bias, float):
                d += 93
        if scale is not None:
            kw["scale"] = scale
        if accum is not None:
            kw["accum_out"] = accum
            d += 93
        self.op("act", lambda e: e.activation(out=out, in_=in_, func=func, **kw), reads=reads, writes=[ot] + list(extra_w),
                key=key, dur=d, multi=(accum is not None))

    def _vd(self, eng, ap):
        n = _free_n(ap)
        return (130.0 + n / 0.96) if eng == "dve" else (270.0 + n / 0.58)

    def tsc(self, eng, ot, out, in0, s1, s2, op0, op1, reads, key=None):
        if op1 is None:
            self.op(eng, lambda e: e.tensor_scalar(out=out, in0=in0, scalar1=s1, scalar2=None, op0=op0), reads=reads, writes=[ot], key=key, dur=self._vd(eng, in0))
        else:
            self.op(eng, lambda e: e.tensor_scalar(out=out, in0=in0, scalar1=s1, scalar2=s2, op0=op0, op1=op1), reads=reads, writes=[ot], key=key, dur=self._vd(eng, in0))

    def stt(self, ot, out, in0, scalar, in1, op0, op1, reads, key=None):
        self.op("dve", lambda e: e.scalar_tensor_tensor(out=out, in0=in0, scalar=scalar, in1=in1, op0=op0, op1=op1), reads=reads, writes=[ot], key=key, dur=self._vd("dve", in0))

    def tt(self, eng, ot, out, in0, in1, op, reads, key=None):
        self.op(eng, lambda e: e.tensor_tensor(out=out, in0=in0, in1=in1, op=op), reads=reads, writes=[ot], key=key, dur=self._vd(eng, out))

    def cp(self, eng, ot, out, in_, reads, key=None):
        self.op(eng, lambda e: e.tensor_copy(out=out, in_=in_), reads=reads, writes=[ot], key=key, dur=self._vd(eng, in_))

    def dma(self, eng, ot, out, in_, reads, dsem, write=True, key=None):
        n = 1
        for d in out.shape:
            n *= d
        nbytes = n * (4 if out.dtype == F32 else 2)
        self.op(eng, lambda e: e.dma_start(out=out, in_=in_), reads=reads, writes=([ot] if write else []), dsem=True, key=key,
                dur=60.0, lat=2200.0 + nbytes / 60.0)

    def _schedule_segment(self, ids, t0):
        ops = self.ops
        inseg = set(ids)
        q = {n: [] for n in self.ENGS}
        for i in ids:
            o = ops[i]
            o.deps = {d: r for d, r in o.deps.items() if d in inseg}
            o.nun = len(o.deps)
            o.users = []
            o.start = None
            q[o.eng].append(i)
        for i in ids:
            for d in ops[i].deps:
                ops[d].users.append(i)
        bl = {}
        for i in reversed(ids):
            o = ops[i]
            m = 0.0
            for u in o.users:
                if bl[u] > m:
                    m = bl[u]
            bl[i] = m + o.lat + (self.XLAT if o.is_dma else 0.0)
        head = {n: 0 for n in self.ENGS}
        free = {n: t0 for n in self.ENGS}
        order = {n: [] for n in self.ENGS}
        cand = {n: None for n in self.ENGS}
        dirty = set(self.ENGS)
        remaining = len(ids)
        WIN = self.WIN
        XL = self.XLAT

        def rescan(n):
            lst = q[n]
            h = head[n]
            while h < len(lst) and ops[lst[h]].start is not None:
                h += 1
            head[n] = h
            best = None
            cnt = 0
            j = h
            fr = free[n]
            while j < len(lst) and cnt < WIN:
                o = ops[lst[j]]
                j += 1
                if o.start is not None:
                    continue
                cnt += 1
                if o.nun:
                    continue
                rd = fr
                for d in o.deps:
                    od = ops[d]
                    f = od.fin + (XL if od.eng != n or od.is_dma else 0.0)
                    if f > rd:
                        rd = f
                if PRIO_BL:
                    kk = (rd, -bl[o.id], o.id)
                    if best is None or kk < best:
                        best = kk
                else:
                    if best is None or rd < best[0]:
                        best = (rd, 0.0, o.id)
                        if rd <= fr:
                            break
            cand[n] = best

        while remaining:
            for n in dirty:
                rescan(n)
            dirty = set()
            bn = None
            for n in self.ENGS:
                c = cand[n]
                if c is not None and (bn is None or c < cand[bn]):
                    bn = n
            if bn is None:
                raise RuntimeError("scheduler stuck: dependency cycle / window too small")
            st, _, i = cand[bn]
            o = ops[i]
            o.start = st
            o.fin = st + o.lat
            free[bn] = st + o.dur
            o.qpos = len(order[bn])
            order[bn].append(i)
            remaining -= 1
            dirty.add(bn)
            for u in o.users:
                ou = ops[u]
                ou.nun -= 1
                if ou.nun == 0:
                    dirty.add(ou.eng)
        tend = max([free[n] for n in self.ENGS] + [ops[i].fin for i in ids]) if ids else t0
        return order, tend

    def emit(self):
        nc = self.nc
        ops = self.ops
        segs = [[] for _ in range(self.nsegs)]
        for o in ops:
            segs[o.seg].append(o.id)
        ecount = {n: 0 for n in self.ENGS}
        dcount = [0] * self.NDS
        dnext = 0
        dnext_sw = 0
        known = {n: {} for n in self.ENGS}
        streams = {n: [] for n in self.ENGS}
        t0 = 0.0
        for sids in segs:
            tprev = t0
            order, t0 = self._schedule_segment(sids, t0)
            if SCHED_DEBUG:
                busy = {n: sum(ops[i].dur for i in order[n]) for n in self.ENGS}
                print("seg %3d n=%5d len_us=%8.1f  " % (len(self.seginfo), len(sids), (t0 - tprev) / 1e3) + " ".join("%s=%.0f" % (n, busy[n] / 1e3) for n in self.ENGS))
            self.seginfo.append(t0 - tprev)
            for i in sids:
                o = ops[i]
                o.needs_inc = False
            for i in sids:
                o = ops[i]
                for d, raw in o.deps.items():
                    od = ops[d]
                    if od.is_dma:
                        continue
                    if od.eng != o.eng or raw or o.eng != "pe":
                        od.needs_inc = True
            for n in self.ENGS:
                for i in reversed(order[n]):
                    if not ops[i].is_dma:
                        ops[i].needs_inc = True
                        break
            for n in self.ENGS:
                for i in order[n]:
                    o = ops[i]
                    if o.is_dma:
                        if n == "pool":
                            k = self.NDS - 8 + (dnext_sw % N_SW_SEMS)
                            dnext_sw += 1
                        else:
                            k = dnext % 16
                            dnext += 1
                        o.dsem = k
                        o.dprev = dcount[k]
                        dcount[k] += 16
                        o.dcount = dcount[k]
                    elif o.needs_inc:
                        ecount[n] += 1
                        o.incval = ecount[n]
            for n in self.ENGS:
                kn = known[n]
                for i in order[n]:
                    o = ops[i]
                    w = {}
                    for d, raw in o.deps.items():
                        od = ops[d]
                        if od.is_dma:
                            s, v = self.dsems_[od.dsem], od.dcount
                        elif od.eng != n or raw or n != "pe":
                            s, v = self.esem[od.eng], od.incval
                        else:
                            continue
                        if w.get(s, 0) < v:
                            w[s] = v
                    if o.is_dma and o.dprev:
                        s = self.dsems_[o.dsem]
                        if w.get(s, 0) < o.dprev:
                            w[s] = o.dprev
                    wl = []
                    for s, v in w.items():
                        if kn.get(s, 0) < v:
                            kn[s] = v
                            wl.append((s, v))
                    if o.is_dma:
                        incspec = (self.dsems_[o.dsem], 16)
                    elif o.needs_inc:
                        incspec = (self.esem[n], 1)
                    else:
                        incspec = None
                    streams[n].append((wl, o.fn, incspec, ATTACH_WAITS and not o.multi and not o.is_dma))
            toks = [(self.esem[n], ecount[n]) for n in self.ENGS if ecount[n]]
            toks += [(self.dsems_[k], dcount[k]) for k in range(self.NDS) if dcount[k]]
            for n in self.ENGS:
                wl = []
                for s, v in toks:
                    if s != self.esem[n] and known[n].get(s, 0) < v:
                        known[n][s] = v
                        wl.append((s, v))
                if wl:
                    streams[n].append((wl, None, None, False))
        self.sim_end = t0
        with nc.Block() as block:
            def mk(n):
                def body(e):
                    for wl, fn, incspec, attach in streams[n]:
                        if fn is None:
                            for s, v in wl:
                                e.wait_ge(self.sems[s], v)
                            continue
                        rest = wl
                        first = None
                        if attach and wl:
                            first = wl[0]
                            rest = wl[1:]
                        for s, v in rest:
                            e.wait_ge(self.sems[s], v)
                        ins = fn(e)
                        if first is not None:
                            ins._wait_ge(self.sems[first[0]], first[1])
                        if incspec is not None:
                            ins.then_inc(self.sems[incspec[0]], incspec[1])
                return body
            for n, f in (("sp", block.sync), ("pe", block.tensor), ("act", block.scalar), ("dve", block.vector), ("pool", block.gpsimd)):
                if streams[n]:
                    f(mk(n))

    def close(self):
        for cm in reversed(self._ctx):
            cm.__exit__(None, None, None)
        self._ctx = []


def host_consts():
    p = np.arange(128)[:, None]
    c = np.arange(128)[None, :]
    cb = np.zeros((128, NCB, 128), np.float32)
    cb[:, 0, :] = np.eye(128)
    cb[:, 1, :] = np.where(p > c, NEG, 0.0)
    cb[:, 2, :] = np.where(p >= c, NEG, 0.0)
    cb[:, 3, :] = np.where(c >= p, NEG, 0.0)
    for l in range(7):
        b = 1 << l
        m = ((p // (2 * b)) == (c // (2 * b))) & ((p % (2 * b)) >= b) & ((c % (2 * b)) < b)
        cb[:, 4 + l, :] = m
        cb[:, 11 + l, :] = m.T
    sel_last = np.zeros((128, 128), np.float32)
    sel_last[127, :] = 1.0
    sb = np.zeros((128, 2, 8, 6), np.float32)
    for h in range(8):
        for pc in range(3):
            sb[32 * pc + h, 0, h, pc] = -1.0
            sb[32 * pc + h, 1, h, 3 + pc] = 1.0
        sb[96, 0, h, 3:6] = 1.0
        sb[96, 1, h, 0:3] = 1.0
    return cb, sel_last, sb.reshape(128, 96)


class Arena:
    def __init__(self, P, nbytes):
        self.t = P.sbuf("arena", [128, nbytes // 2], BF16)
        self.n = nbytes
        self.off = 0

    def reset(self):
        self.off = 0

    def alloc(self, name, shape, dtype):
        esz = 4 if dtype == F32 else 2
        n = 1
        for d in shape[1:]:
            n *= d
        nb = (n * esz + 63) // 64 * 64
        if self.off + nb > self.n:
            raise RuntimeError("arena overflow at %s: %d + %d > %d" % (name, self.off, nb, self.n))
        a = self.off // 2
        ap = self.t[0:shape[0], a:a + n * esz // 2]
        if dtype == F32:
            ap = ap.bitcast(F32)
        if len(shape) == 3:
            ap = ap.rearrange("p (a b) -> p a b", a=shape[1])
        elif len(shape) == 4:
            ap = ap.rearrange("p (a b c) -> p a b c", a=shape[1], b=shape[2])
        self.off += nb
        return TT(ap, name)


def bc_last(ap2, n):
    return ap2.rearrange("p (h o) -> p h o", o=1).broadcast_to([ap2.shape[0], ap2.shape[1], n])


def bc_mid(ap2, h):
    return ap2.rearrange("p (o n) -> p o n", o=1).broadcast_to([ap2.shape[0], h, ap2.shape[1]])


def build(T, NSEQ, debug=False, arena_bytes=100 * 1024):
    NT = T // 128
    BW = min(512, T)
    NB = T // BW
    QPB = BW // 128
    nc = bass.Bass("TRN2", target_bir_lowering=False)

    def din(name, shape, dt=F32):
        return nc.dram_tensor(name, shape, dt, kind="ExternalInput").ap()

    x_d = din("x", [NSEQ, T, D])
    win_d = din("w_in", [D, DIN])
    wbf_d = din("w_bf", [D, D])
    wbg_d = din("w_bg", [D, D])
    wout_d = din("w_out", [D, D])
    wup_d = din("w_up", [D, 2 * DFF])
    wdn_d = din("w_down", [DFF, D])
    nrm_d = din("norms", [3, D])
    gnorm_d = din("gdn_norm", [1, DH])
    cwg_d = din("cwg", [128, 24, 4])
    cwf_d = din("cwf", [128, 44, 3])
    hv_d = din("hv", [8, 3])
    cb_d = din("cb", [128, NCB, 128])
    sl_d = din("sel_last", [128, 128])
    sb_d = din("selbig", [128, 96])
    out_d = nc.dram_tensor("out", [NSEQ, T, D], F32, kind="ExternalOutput").ap()
    if debug:
        dbg_fox = nc.dram_tensor("dbg_fox", [128, H, T], BF16, kind="ExternalOutput").ap()
        dbg_gdn = nc.dram_tensor("dbg_gdn", [128, H, T], BF16, kind="ExternalOutput").ap()

    def scratch(name, shape):
        return nc.dram_tensor(name, shape, BF16, kind="Internal").ap()

    wb_in = scratch("wb_in", [D, DIN])
    wb_bf = scratch("wb_bf", [D, D])
    wb_bg = scratch("wb_bg", [D, D])
    wb_out = scratch("wb_out", [D, D])
    wb_up = scratch("wb_up", [D, 2 * DFF])
    wb_dn = scratch("wb_dn", [DFF, D])

    P = Prog(nc)
    S = P.st
    cbb = S("cbb", [128, NCB, 128], BF16)
    mkw = S("mkw", [128, 512], BF16)
    ones_b = S("ones_b", [128, 128], BF16)
    idf8 = S("idf8", [8, 8], F32)
    sl_f = S("sl_f", [128, 128], F32)
    sb_b = S("sb_b", [128, 96], BF16)
    gnorm = S("gnorm", [128, DH], F32)
    cwg = S("cwg_s", [128, 24, 4], F32)
    cwf = S("cwf_s", [128, 44, 3], F32)
    hv = S("hv_s", [8, 8], F32)
    wsm = S("wsm", [128, 8, 24], BF16)
    onesr = S("onesr", [8, 128], F32)
    hnT = S("hnT", [128, 8, T], BF16)
    yfoxT = S("yfoxT", [128, 8, T], BF16)
    ygdnT = S("ygdnT", [128, 8, T], BF16)

    def alloc_pk():
        AR.reset()
        pk_ = {n: AR.alloc("PK_" + n, [128, T], BF16) for n in ("c", "L", "LB")}
        cols_ = AR.alloc("cols", [128, NT, 5, 8], F32)
        return pk_, cols_
    AR = Arena(P, arena_bytes)

    pf = [TT(P.psum("pf%d" % i, [128, 512], F32)[:], "pf%d" % i) for i in range(8)]
    pb = [TTV(pf[6 + i], pf[6 + i][:].bitcast(BF16)) for i in range(2)]
    ident_b = cbb[:, 0, :]

    dW = P.dsem("dW")
    wtt = {}
    for name, src, dst, rows_ in (("in", win_d, wb_in, D), ("bf", wbf_d, wb_bf, D), ("bg", wbg_d, wb_bg, D),
                                  ("out", wout_d, wb_out, D), ("up", wup_d, wb_up, D), ("dn", wdn_d, wb_dn, DFF)):
        t = TT(dst, "wb_" + name)
        wtt[name] = t
        for r0 in range(0, rows_, 128):
            P.dma("pool", t, dst[r0:r0 + 128, :], src[r0:r0 + 128, :], [], dW, key=r0)
    dC = P.dsem("dC")
    cbf = AR.alloc("cbf", [128, NCB, 128], F32)
    sb_f = AR.alloc("sb_f", [128, 96], F32)
    P.dma("sp", cbf, cbf[:], cb_d, [], dC)
    P.dma("sp", sb_f, sb_f[:], sb_d, [], dC)
    P.dma("sp", sl_f, sl_f[:], sl_d, [], dC)
    P.dma("sp", idf8, idf8[:], cb_d[0:8, 0, 0:8], [], dC)
    P.dma("sp", gnorm, gnorm[:], gnorm_d.partition_broadcast(128), [], dC)
    P.dma("sp", cwg, cwg[:], cwg_d, [], dC)
    P.dma("sp", cwf, cwf[:], cwf_d, [], dC)
    P.dma("sp", hv, hv[:, 0:3], hv_d, [], dC)
    P.cp("dve", cbb, cbb[:], cbf[:], [cbf])
    P.cp("dve", sb_b, sb_b[:], sb_f[:], [sb_f])
    P.op("dve", lambda e: e.memset(mkw[:], 0.0), writes=[mkw])
    P.cp("dve", mkw, mkw[:, 0:128], cbb[:, 1, :], [cbb])
    P.op("dve", lambda e: e.memset(ones_b[:], 1.0), writes=[ones_b])
    P.op("dve", lambda e: e.memset(onesr[:], 1.0), writes=[onesr])
    P.tsc("dve", hv, hv[:, 3:4], hv[:, 0:1], -1.0, None, ALU.mult, None, [hv])
    P.act(hv, hv[:, 4:5], hv[:, 1:2], AF.Exp, [hv])
    dS = P.dsem("dS")
    for i, off in enumerate((OFF_FF, OFF_GA, OFF_GB)):
        P.dma("sp", wsm, wsm[:, :, i * 8:(i + 1) * 8], wb_in[:, off:off + 8].rearrange("(c p) n -> p c n", p=128), [wtt["in"]], dS, key=i)

    def selbig(kind, h):
        o = (kind * 8 + h) * 6
        return sb_b[:, o:o + 6]

    dX = [P.dsem("dX%d" % i) for i in range(2)]
    dWh = [P.dsem("dWh%d" % i) for i in range(8)]
    dPK = P.dsem("dPK")
    dOut = P.dsem("dOut")
    dG = P.dsem("dG")
    dTail = [P.dsem("dTl%d" % i) for i in range(12)]

    def wslice(wb, c0, n):
        return wb[:, c0:c0 + n].rearrange("(c p) n -> p c n", p=128)

    def rstd_from_ss(stt_, cin, cout, n, scale):
        P.tsc("dve", stt_, stt_[:, cout:cout + n], stt_[:, cin:cin + n], scale, EPS, ALU.mult, ALU.add, [stt_])
        P.act(stt_, stt_[:, cout:cout + n], stt_[:, cout:cout + n], AF.Ln, [stt_])
        P.act(stt_, stt_[:, cout:cout + n], stt_[:, cout:cout + n], AF.Exp, [stt_], scale=-0.5)

    def rmsnorm_tile(xin, xin_tt, gam_ap, gam_tt, stt_, junk, outb):
        P.act(junk, junk[:], xin, AF.Square, [xin_tt], accum=stt_[:, 0:1], extra_w=[stt_])
        rstd_from_ss(stt_, 0, 1, 1, 1.0 / D)
        P.stt(outb, outb[:], xin, stt_[:, 1:2], gam_ap, ALU.mult, ALU.mult, [xin_tt, stt_, gam_tt])

    xcnt = [0]
    P.barrier()

    for s in range(NSEQ):
        AR.reset()
        gam1 = AR.alloc("gam1", [128, D], F32)
        xt = [AR.alloc("xt%d" % i, [128, D], F32) for i in range(2)]
        sqj = AR.alloc("sqj", [128, D], F32)
        hnb = [AR.alloc("hnb%d" % i, [128, D], BF16) for i in range(2)]
        st4 = [AR.alloc("st4_%d" % i, [128, 8], F32) for i in range(2)]
        P.dma("sp", gam1, gam1[:], nrm_d[0:1, :].partition_broadcast(128), [], dG)
        for tt_ in range(NT):
            i = xcnt[0] % 2
            xcnt[0] += 1
            P.dma("sp", xt[i], xt[i][:], x_d[s, tt_ * 128:(tt_ + 1) * 128, :], [], dX[i])
            rmsnorm_tile(xt[i][:], xt[i], gam1[:], gam1, st4[i], sqj, hnb[i])
            for half in range(2):
                pbt = pb[half]
                for c in range(4):
                    cc = half * 4 + c
                    P.tr(pbt, pbt[:, c * 128:(c + 1) * 128], hnb[i][:, cc * 128:(cc + 1) * 128], ident_b, [hnb[i], cbb], inc=(c == 3), key=c)
                P.act(hnT, hnT[:, half * 4:half * 4 + 4, tt_ * 128:(tt_ + 1) * 128],
                      pbt[:, 0:512].rearrange("p (c n) -> p c n", c=4), AF.Copy, [pbt], key=(tt_, half))
        P.barrier()

        PK, cols = alloc_pk()
        for n in PK:
            P.op("pool", lambda e, n=n: e.memset(PK[n][:], 1.0), writes=[PK[n]], dur=4000.0)
        rows = {n: AR.alloc("row_" + n, [8, T], F32) for n in ("nlf", "ncum", "negg", "nlb", "L", "LB", "r1", "r2")}
        rows["tmp"] = rows["r1"]
        pcs = [AR.alloc("pc%d" % i, [8, T], BF16) for i in range(3)] * 3
        nlf, ncum, negg, nlb, Lr, LBr, tmp = (rows[n] for n in ("nlf", "ncum", "negg", "nlb", "L", "LB", "tmp"))
        Lc = AR.alloc("Lc", [128, 8], F32)
        nlc = AR.alloc("nlc", [128, 8], F32)
        t8 = AR.alloc("t8", [128, 8], F32)
        for b in range(NB):
            cs = slice(b * BW, (b + 1) * BW)
            for i in range(3):
                pst = pf[i]
                for c in range(8):
                    P.mm(pst, pst[0:8, 0:BW], wsm[:, c, i * 8:(i + 1) * 8], hnT[:, c, cs], [wsm, hnT], start=(c == 0), stop=(c == 7), inc=(c == 7))
            P.act(tmp, tmp[:, cs], pf[0][0:8, 0:BW], AF.Exp, [pf[0], hv], bias=hv[:, 3:4], scale=-1.0)
            P.act(nlf, nlf[:, cs], tmp[:, cs], AF.Ln, [tmp], bias=1.0)
            P.act(tmp, tmp[:, cs], pf[1][0:8, 0:BW], AF.Exp, [pf[1], hv], bias=hv[:, 2:3], scale=1.0)
            P.act(negg, negg[:, cs], tmp[:, cs], AF.Ln, [tmp], bias=1.0)
            P.tsc("dve", negg, negg[:, cs], negg[:, cs], hv[:, 4:5], None, ALU.mult, None, [negg, hv])
            P.act(tmp, tmp[:, cs], pf[2][0:8, 0:BW], AF.Exp, [pf[2]], scale=-1.0)
            P.act(nlb, nlb[:, cs], tmp[:, cs], AF.Ln, [tmp], bias=1.0)
        for tt_ in range(NT):
            cs = slice(tt_ * 128, (tt_ + 1) * 128)
            init = 0.0 if tt_ == 0 else ncum[:, tt_ * 128 - 1:tt_ * 128]
            P.op("dve", lambda e, cs=cs, init=init: e.tensor_tensor_scan(out=ncum[:, cs], data0=onesr[:], data1=nlf[:, cs], initial=init, op0=ALU.mult, op1=ALU.add),
                 reads=[onesr, nlf, ncum], writes=[ncum])
            P.op("dve", lambda e, cs=cs: e.tensor_tensor_scan(out=Lr[:, cs], data0=onesr[:], data1=negg[:, cs], initial=0.0, op0=ALU.mult, op1=ALU.add),
                 reads=[onesr, negg], writes=[Lr])
        P.tt("dve", LBr, LBr[:], Lr[:], nlb[:], ALU.add, [Lr, nlb])
        r1, r2 = rows["r1"], rows["r2"]
        for k, (src, pk) in enumerate(((ncum, PK["c"]), (Lr, PK["L"]), (LBr, PK["LB"]))):
            pc = pcs[3 * k:3 * k + 3]
            P.cp("dve", pc[0], pc[0][:], src[:], [src])
            P.tt("dve", r1, r1[:], src[:], pc[0][:], ALU.subtract, [src, pc[0]])
            P.cp("dve", pc[1], pc[1][:], r1[:], [r1])
            P.tt("dve", r2, r2[:], r1[:], pc[1][:], ALU.subtract, [r1, pc[1]])
            P.cp("dve", pc[2], pc[2][:], r2[:], [r2])
            for j in range(3):
                P.dma("sp", pk, pk[32 * j:32 * j + 8, :], pc[j][:], [pc[j]], dPK, key=j)
        for tt_ in range(NT):
            cs = slice(tt_ * 128, (tt_ + 1) * 128)
            pst = pf[3 + tt_ % 2]
            P.tr(pst, pst[:, 0:8], Lr[:, cs], idf8[:], [Lr, idf8], inc=False)
            P.tr(pst, pst[:, 8:16], nlb[:, cs], idf8[:], [nlb, idf8], inc=True)
            P.cp("dve", Lc, Lc[:], pst[:, 0:8], [pst])
            P.cp("dve", nlc, nlc[:], pst[:, 8:16], [pst])
            P.mm(pst, pst[:, 16:24], sl_f[:], Lc[:], [sl_f, Lc])
            P.tt("dve", t8, t8[:], Lc[:], nlc[:], ALU.add, [Lc, nlc])
            P.act(cols, cols[:, tt_, 0, :], t8[:], AF.Exp, [t8], scale=-1.0, key=(tt_, 0))
            P.act(cols, cols[:, tt_, 1, :], nlc[:], AF.Exp, [nlc], scale=-1.0, key=(tt_, 1))
            P.tt("dve", t8, t8[:], Lc[:], pst[:, 16:24], ALU.subtract, [Lc, pst])
            P.act(cols, cols[:, tt_, 2, :], t8[:], AF.Exp, [t8], key=(tt_, 2))
            P.act(cols, cols[:, tt_, 3, :], pst[:, 16:24], AF.Exp, [pst], scale=-1.0, key=(tt_, 3))
            P.act(cols, cols[:, tt_, 4, :], Lc[:], AF.Exp, [Lc], scale=-1.0, key=(tt_, 4))
        P.barrier()

        PK, cols = alloc_pk()
        wh = [[AR.alloc("wh%d_%d" % (i, j), [128, 8, 128], BF16) for j in range(3)] for i in range(2)]
        qT = AR.alloc("qT", [128, T], BF16)
        kT = AR.alloc("kT", [128, T], BF16)
        vtok = AR.alloc("vtok", [128, NT, 128], BF16)
        QA = AR.alloc("QA", [6, T], BF16)
        KA = AR.alloc("KA", [6, T], BF16)
        pT = [AR.alloc("pT%d" % i, [128, BW], BF16) for i in range(3)]
        rsb = AR.alloc("rsb", [128, BW], F32)
        for h in range(H):
            hs = h % 2
            for j, off in enumerate((OFF_FQ, OFF_FK, OFF_FV)):
                P.dma("sp", wh[hs][j], wh[hs][j][:], wslice(wb_in, off + h * 128, 128), [wtt["in"]], dWh[hs * 3 + j])
            for b in range(NB):
                cs = slice(b * BW, (b + 1) * BW)
                for j, dst in ((0, qT), (1, kT)):
                    pst = pf[j]
                    for c in range(8):
                        P.mm(pst, pst[:, 0:BW], wh[hs][j][:, c, :], hnT[:, c, cs], [wh[hs][j], hnT], start=(c == 0), stop=(c == 7), inc=(c == 7))
                    if j == 0:
                        P.act(dst, dst[:, cs], pst[:, 0:BW], AF.Copy, [pst], scale=DH ** -0.5)
                    else:
                        P.cp("dve", dst, dst[:, cs], pst[:, 0:BW], [pst])
                for kind, dst in ((0, QA), (1, KA)):
                    pst = pf[3]
                    P.mm(pst, pst[0:6, 0:BW], selbig(kind, h), PK["c"][:, cs], [sb_b, PK["c"]])
                    P.cp("dve", dst, dst[:, cs], pst[0:6, 0:BW], [pst])
            for t0 in range(0, NT, 4):
                n4 = min(4, NT - t0)
                pst = pf[2]
                for ti in range(n4):
                    tcs = slice((t0 + ti) * 128, (t0 + ti + 1) * 128)
                    for c in range(8):
                        P.mm(pst, pst[:, ti * 128:(ti + 1) * 128], hnT[:, c, tcs], wh[hs][2][:, c, :], [wh[hs][2], hnT], start=(c == 0), stop=(c == 7), inc=(c == 7 and ti == n4 - 1), key=ti)
                P.act(vtok, vtok[:, t0:t0 + n4, :], pst[:, 0:n4 * 128].rearrange("p (t n) -> p t n", t=n4), AF.Copy, [pst], key=t0)
            pcount = 0
            for qb in range(NB):
                ot, sm = pf[4], pf[5]
                nk = QPB * qb + QPB
                for kt in range(nk):
                    j = max(0, kt - QPB * qb)
                    c0 = 128 * j
                    ncol = BW - c0
                    qcs = slice(qb * BW + c0, (qb + 1) * BW)
                    kcs = slice(kt * 128, (kt + 1) * 128)
                    diag = kt >= QPB * qb
                    sps = pf[pcount % 2]
                    ptile = pT[pcount % 3]
                    pcount += 1
                    P.mm(sps, sps[:, c0:BW], kT[:, kcs], qT[:, qcs], [kT, qT], start=True, stop=False, inc=False)
                    P.mm(sps, sps[:, c0:BW], KA[:, kcs], QA[:, qcs], [KA, QA], start=False, stop=(not diag), inc=(not diag))
                    if diag:
                        P.mm(sps, sps[:, c0:BW], ident_b, mkw[:, 0:ncol], [cbb, mkw], start=False, stop=True)
                    P.act(ptile, ptile[:, c0:BW], sps[:, c0:BW], AF.Exp, [sps])
                    last = kt == nk - 1
                    P.mm(ot, ot[:, c0:BW], vtok[:, kt, :], ptile[:, c0:BW], [vtok, ptile], start=(kt == 0), stop=last, inc=False, skip=True)
                    P.mm(sm, sm[:, c0:BW], ones_b[:], ptile[:, c0:BW], [ones_b, ptile], start=(kt == 0), stop=last, inc=True, skip=True)
                P.op("dve", lambda e, sm=sm: e.reciprocal(out=rsb[:, 0:BW], in_=sm[:, 0:BW]), reads=[sm], writes=[rsb])
                P.tt("dve", yfoxT, yfoxT[:, h, qb * BW:(qb + 1) * BW], ot[:, 0:BW], rsb[:, 0:BW], ALU.mult, [ot, rsb], key=(h, qb))
        if debug and s == 0:
            P.dma("sp", None, dbg_fox, yfoxT[:], [yfoxT], dOut, write=False)
        P.barrier()

        for g in range(2):
            gdn_half(P, AR, locals(), s, g)
            P.barrier()
        if debug and s == 0:
            P.dma("sp", None, dbg_gdn, ygdnT[:], [ygdnT], dOut, write=False)

        tail_phase(P, AR, locals(), s)
        P.barrier()

    P.emit()
    P.close()
    return nc
def gdn_half(P, AR, L, s, g):
    T, NT = L["T"], L["NT"]
    hnT, ygdnT, cbb, ident_b = L["hnT"], L["ygdnT"], L["cbb"], L["ident_b"]
    pf, pb, wb_in, wtt, dWh, cwg, gnorm, sb_b = L["pf"], L["pb"], L["wb_in"], L["wtt"], L["dWh"], L["cwg"], L["gnorm"], L["sb_b"]
    selbig, wslice = L["selbig"], L["wslice"]
    G = 4
    PK, cols = L["alloc_pk"]()
    A = AR.alloc
    wg = [A("wg%d" % qi, [128, G, 8, 128], BF16) for qi in range(4)]
    halo = A("halo", [128, 12, 3], F32)
    stage = [A("stage%d" % i, [128, 131], F32) for i in range(2)]
    acc = [A("acc%d" % i, [128, 128], F32) for i in range(2)]
    cT = A("cT", [128, 12, 128], BF16)
    QA1 = A("QA1", [6, G, 128], BF16)
    QA2 = A("QA2", [6, G, 128], BF16)
    KAg = A("KAg", [6, G, 128], BF16)
    tks = [{n: A("tk%d_" % p + n, [128, G, 128], BF16) for n in ("kn", "Rw", "kd", "qs", "qd", "Rv")} for p in range(2)]
    knTs = [A("knT%d" % p, [128, G, 128], BF16) for p in range(2)]
    qsTs = [A("qsT%d" % p, [128, G, 128], BF16) for p in range(2)]
    qdTs = [A("qdT%d" % p, [128, G, 128], BF16) for p in range(2)]
    E1Ts = [A("E1T%d" % p, [128, G, 128], BF16) for p in range(2)]
    E1s = [A("E1_%d" % p, [128, G, 128], BF16) for p in range(2)]
    E2Ts = [A("E2T%d" % p, [128, G, 128], BF16) for p in range(2)]
    NEB, NEBT = 2, 2
    Eb = [A("Eb%d" % i, [128, G, 128], BF16) for i in range(NEB)]
    EbT = [A("EbT%d" % i, [128, G, 128], BF16) for i in range(NEBT)]
    Tm = [A("Tm%d" % i, [128, G, 128], BF16) for i in range(2)]
    TTm = [A("TTm%d" % i, [128, G, 128], BF16) for i in range(2)]
    Y1s = A("Y1s", [128, G, 128], BF16)
    Y2s = A("Y2s", [128, G, 128], BF16)
    negwT = Y1s
    u_bf = Y2s
    S_f = A("S_f", [128, G, 128], F32)
    S_b = A("S_b", [128, G, 128], BF16)
    gzss = [A("gzs%d" % p, [128, G, 128], BF16) for p in range(2)]
    o_f = A("o_f", [128, G, 128], F32)
    on_b = A("on_b", [128, G, 128], BF16)
    junk = A("junk", [128, 128], BF16)
    scs = [A("sc%d" % p, [128, 40], F32) for p in range(2)]
    if SCHED_DEBUG:
        print("gdn arena used", AR.off, "of", AR.n)

    offs = (OFF_GQ, OFF_GK, OFF_GV, OFF_GZ)
    for qi in range(4):
        for i in range(G):
            hh = g * G + i
            P.dma("sp", wg[qi], wg[qi][:, i, :, :], wslice(wb_in, offs[qi] + hh * 128, 128), [wtt["in"]], dWh[6], key=i)
    P.op("dve", lambda e: e.memset(S_f[:], 0.0), writes=[S_f])
    P.op("dve", lambda e: e.memset(S_b[:], 0.0), writes=[S_b])
    hsl = slice(g * G, (g + 1) * G)
    scale = DH ** -0.5

    def v4(t_, i):
        return t_[:, i, :]

    f4 = lambda pst: pst[:].rearrange("p (a b) -> p a b", a=G)

    def early(t):
            cs = slice(t * 128, (t + 1) * 128)
            tk, knT, qsT, qdT, sc = tks[t % 2], knTs[t % 2], qsTs[t % 2], qdTs[t % 2], scs[t % 2]
            E1T, E1, E2T = E1Ts[t % 2], E1s[t % 2], E2Ts[t % 2]
            AT, Am, attnT = E1T, E1, E2T
            gzs = gzss[t % 2]
            k = 0
            for qi in range(3):
                for i in range(G):
                    hh = g * G + i
                    ch = qi * 8 + hh
                    pst = pf[4 + k % 2]
                    stg = stage[k % 2]
                    ac = acc[k % 2]
                    k += 1
                    for c in range(8):
                        P.mm(pst, pst[:, 0:128], wg[qi][:, i, c, :], hnT[:, c, cs], [wg[qi], hnT], start=(c == 0), stop=(c == 7), inc=(c == 7))
                    P.act(stg, stg[:, 3:131], pst[:, 0:128], AF.Copy, [pst], key='x')
                    hi = qi * G + i
                    if t == 0:
                        P.op("pool", lambda e, stg=stg: e.memset(stg[:, 0:3], 0.0), writes=[stg], key='h')
                    else:
                        P.cp("pool", stg, stg[:, 0:3], halo[:, hi, :], [halo], key='h')
                    P.cp("pool", halo, halo[:, hi, :], stg[:, 128:131], [stg], key=hi)
                    P.act(ac, ac[:], pst[:, 0:128], AF.Copy, [pst, cwg], scale=cwg[:, ch, 3:4])
                    for tap in (2, 1, 0):
                        P.stt(ac, ac[:], stg[:, tap:tap + 128], cwg[:, ch, tap:tap + 1], ac[:], ALU.mult, ALU.add, [stg, cwg, ac])
                    P.act(cT, cT[:, hi, :], ac[:], AF.Silu, [ac], key=hi)
            for i in range(G):
                for c in range(8):
                    P.mm(pf[5], pf[5][:, i * 128:(i + 1) * 128], hnT[:, c, cs], wg[3][:, i, c, :], [wg[3], hnT], start=(c == 0), stop=(c == 7), inc=(c == 7 and i == G - 1))
            P.act(gzs, gzs[:], pf[5][:].rearrange("p (a b) -> p a b", a=G), AF.Silu, [pf[5]])
            for dst, pk, kind, pst in ((QA1, PK["LB"], 0, pf[3]), (QA2, PK["L"], 0, pf[4]), (KAg, PK["L"], 1, pf[5])):
                for i in range(G):
                    P.mm(pst, pst[0:6, i * 128:(i + 1) * 128], selbig(kind, g * G + i), pk[:, cs], [sb_b, pk], inc=(i == G - 1))
                P.cp("dve", dst, dst[:], pst[0:6, :].rearrange("p (a b) -> p a b", a=G), [pst])
            for i in range(G):
                P.tr(pb[0], pb[0][:, i * 128:(i + 1) * 128], cT[:, 1 * G + i, :], ident_b, [cT, cbb], inc=False)
            for i in range(G):
                P.tr(pb[0], pb[0][:, 512 + i * 128:512 + (i + 1) * 128], cT[:, 0 * G + i, :], ident_b, [cT, cbb], inc=(i == G - 1))
            for i in range(G):
                P.tr(pb[1], pb[1][:, i * 128:(i + 1) * 128], cT[:, 2 * G + i, :], ident_b, [cT, cbb], inc=(i == G - 1))
            for i in range(G):
                P.act(junk, junk[:], pb[0][:, i * 128:(i + 1) * 128], AF.Square, [pb[0]], accum=sc[:, i:i + 1], extra_w=[sc])
                P.act(junk, junk[:], pb[0][:, 512 + i * 128:512 + (i + 1) * 128], AF.Square, [pb[0]], accum=sc[:, 4 + i:5 + i], extra_w=[sc])
            P.tsc("dve", sc, sc[:, 8:16], sc[:, 0:8], 1.0, EPS, ALU.mult, ALU.add, [sc])
            P.act(sc, sc[:, 8:16], sc[:, 8:16], AF.Ln, [sc])
            P.act(sc, sc[:, 8:16], sc[:, 8:16], AF.Exp, [sc], scale=-0.5)
            P.tt("dve", sc, sc[:, 16:20], sc[:, 8:12], cols[:, t, 0, hsl], ALU.mult, [sc, cols])
            P.tt("dve", sc, sc[:, 20:24], sc[:, 8:12], cols[:, t, 2, hsl], ALU.mult, [sc, cols])
            P.tsc("dve", sc, sc[:, 24:28], sc[:, 12:16], scale, None, ALU.mult, None, [sc])
            P.tt("dve", sc, sc[:, 28:32], sc[:, 24:28], cols[:, t, 4, hsl], ALU.mult, [sc, cols])
            k4 = pb[0][:, 0:512].rearrange("p (a b) -> p a b", a=G)
            q4 = pb[0][:, 512:1024].rearrange("p (a b) -> p a b", a=G)
            v4_ = pb[1][:, 0:512].rearrange("p (a b) -> p a b", a=G)
            for name, src, scl, srct in (("kn", k4, sc[:, 8:12], pb[0]), ("Rw", k4, sc[:, 16:20], pb[0]), ("kd", k4, sc[:, 20:24], pb[0]),
                                         ("qs", q4, sc[:, 24:28], pb[0]), ("qd", q4, sc[:, 28:32], pb[0])):
                P.tt("dve", tk[name], tk[name][:], src, bc_last(scl, 128), ALU.mult, [srct, sc])
            P.tt("dve", tk["Rv"], tk["Rv"][:], v4_, bc_last(cols[:, t, 1, hsl], 128), ALU.mult, [pb[1], cols])
            for i in range(G):
                P.tr(pb[1], pb[1][:, 512 + i * 128:512 + (i + 1) * 128], tk["kn"][:, i, :], ident_b, [tk["kn"], cbb], inc=(i == G - 1))
            P.act(knT, knT[:], pb[1][:, 512:1024].rearrange("p (a b) -> p a b", a=G), AF.Copy, [pb[1]])
            for i in range(G):
                P.tr(pb[0], pb[0][:, i * 128:(i + 1) * 128], tk["qs"][:, i, :], ident_b, [tk["qs"], cbb], inc=False)
            for i in range(G):
                P.tr(pb[0], pb[0][:, 512 + i * 128:512 + (i + 1) * 128], tk["qd"][:, i, :], ident_b, [tk["qd"], cbb], inc=(i == G - 1))
            P.act(qsT, qsT[:], pb[0][:, 0:512].rearrange("p (a b) -> p a b", a=G), AF.Copy, [pb[0]])
            P.act(qdT, qdT[:], pb[0][:, 512:1024].rearrange("p (a b) -> p a b", a=G), AF.Copy, [pb[0]])
            for i in range(G):
                cs_i = slice(i * 128, (i + 1) * 128)
                last = i == G - 1
                P.mm(pf[0], pf[0][:, cs_i], knT[:, i, :], knT[:, i, :], [knT], inc=last)
            for i in range(G):
                cs_i = slice(i * 128, (i + 1) * 128)
                P.mm(pf[1], pf[1][:, cs_i], knT[:, i, :], qsT[:, i, :], [knT, qsT], inc=(i == G - 1))
            for pst, lh, rh, mk in ((pf[2], KAg, QA1, 2), (pf[3], QA1, KAg, 3), (pf[4], KAg, QA2, 1)):
                for i in range(G):
                    cs_i = slice(i * 128, (i + 1) * 128)
                    P.mm(pst, pst[:, cs_i], lh[:, i, :], rh[:, i, :], [lh, rh], start=True, stop=False, inc=False)
                    P.mm(pst, pst[:, cs_i], ident_b, cbb[:, mk, :], [cbb], start=False, stop=True, inc=(i == G - 1))
            for dst, pst in ((E1T, pf[2]), (E1, pf[3]), (E2T, pf[4])):
                P.act(dst, dst[:], pst[:].rearrange("p (a b) -> p a b", a=G), AF.Exp, [pst])
            P.tt("dve", AT, AT[:], f4(pf[0]), E1T[:], ALU.mult, [pf[0], E1T])
            P.tt("dve", Am, Am[:], f4(pf[0]), E1[:], ALU.mult, [pf[0], E1])
            P.tt("dve", attnT, attnT[:], f4(pf[1]), E2T[:], ALU.mult, [pf[1], E2T])
    def rec(t):
            cs = slice(t * 128, (t + 1) * 128)
            tk, knT, qsT, qdT, sc = tks[t % 2], knTs[t % 2], qsTs[t % 2], qdTs[t % 2], scs[t % 2]
            E1T, E1, E2T = E1Ts[t % 2], E1s[t % 2], E2Ts[t % 2]
            AT, Am, attnT = E1T, E1, E2T
            gzs = gzss[t % 2]
            cur = 0
            P.tt("pool", Eb[0], Eb[0][:], Am[:], bc_mid(cbb[:, 4, :], G), ALU.mult, [Am, cbb])
            P.tt("pool", EbT[0], EbT[0][:], AT[:], bc_mid(cbb[:, 11, :], G), ALU.mult, [AT, cbb])
            P.tt("dve", Tm[cur], Tm[cur][:], bc_mid(ident_b, G), Eb[0][:], ALU.subtract, [cbb, Eb[0]])
            P.tt("dve", TTm[cur], TTm[cur][:], bc_mid(ident_b, G), EbT[0][:], ALU.subtract, [cbb, EbT[0]])
            for l in range(1, 7):
                lastl = l == 6
                ebuf = Eb[l % NEB]
                ebtbuf = EbT[l % NEBT]
                P.tt("pool", ebuf, ebuf[:], Am[:], bc_mid(cbb[:, 4 + l, :], G), ALU.mult, [Am, cbb])
                if not lastl:
                    P.tt("pool", ebtbuf, ebtbuf[:], AT[:], bc_mid(cbb[:, 11 + l, :], G), ALU.mult, [AT, cbb])
                for i in range(G):
                    P.mm(pf[0], pf[0][:, i * 128:(i + 1) * 128], ebuf[:, i, :], TTm[cur][:, i, :], [ebuf, TTm[cur]], inc=(i == G - 1))
                if not lastl:
                    for i in range(G):
                        P.mm(pf[1], pf[1][:, i * 128:(i + 1) * 128], ebtbuf[:, i, :], Tm[cur][:, i, :], [ebtbuf, Tm[cur]], inc=(i == G - 1))
                P.act(Y2s, Y2s[:], f4(pf[0]), AF.Copy, [pf[0]])
                if not lastl:
                    P.cp("dve", Y1s, Y1s[:], f4(pf[1]), [pf[1]])
                for i in range(G):
                    P.mm(pf[2], pf[2][:, i * 128:(i + 1) * 128], Tm[cur][:, i, :], Y2s[:, i, :], [Tm[cur], Y2s], inc=(i == G - 1))
                if not lastl:
                    for i in range(G):
                        P.mm(pf[3], pf[3][:, i * 128:(i + 1) * 128], TTm[cur][:, i, :], Y1s[:, i, :], [TTm[cur], Y1s], inc=(i == G - 1))
                P.tt("dve", TTm[1 - cur], TTm[1 - cur][:], TTm[cur][:], f4(pf[2]), ALU.subtract, [TTm[cur], pf[2]])
                if not lastl:
                    P.tt("dve", Tm[1 - cur], Tm[1 - cur][:], Tm[cur][:], f4(pf[3]), ALU.subtract, [Tm[cur], pf[3]])
                cur = 1 - cur
            TT_ = TTm[cur]
            TTfin[t] = TT_
    def late(t):
            cs = slice(t * 128, (t + 1) * 128)
            tk, knT, qsT, qdT, sc = tks[t % 2], knTs[t % 2], qsTs[t % 2], qdTs[t % 2], scs[t % 2]
            E1T, E1, E2T = E1Ts[t % 2], E1s[t % 2], E2Ts[t % 2]
            AT, Am, attnT = E1T, E1, E2T
            gzs = gzss[t % 2]
            TT_ = TTfin[t]
            for i in range(G):
                P.mm(pf[4], pf[4][:, i * 128:(i + 1) * 128], tk["Rw"][:, i, :], TT_[:, i, :], [tk["Rw"], TT_], inc=(i == G - 1))
            P.act(negwT, negwT[:], f4(pf[4]), AF.Copy, [pf[4]], scale=-1.0)
            for i in range(G):
                cs_i = slice(i * 128, (i + 1) * 128)
                P.mm(pf[5], pf[5][:, cs_i], TT_[:, i, :], tk["Rv"][:, i, :], [TT_, tk["Rv"]], start=True, stop=False, inc=False)
                P.mm(pf[5], pf[5][:, cs_i], negwT[:, i, :], S_b[:, i, :], [negwT, S_b], start=False, stop=True, inc=(i == G - 1))
            P.act(u_bf, u_bf[:], f4(pf[5]), AF.Copy, [pf[5]])
            for i in range(G):
                cs_i = slice(i * 128, (i + 1) * 128)
                P.mm(pf[0], pf[0][:, cs_i], qdT[:, i, :], S_b[:, i, :], [qdT, S_b], start=True, stop=False, inc=False)
                P.mm(pf[0], pf[0][:, cs_i], attnT[:, i, :], u_bf[:, i, :], [attnT, u_bf], start=False, stop=True, inc=(i == G - 1))
            for i in range(G):
                cs_i = slice(i * 128, (i + 1) * 128)
                P.mm(pf[1], pf[1][:, cs_i], tk["kd"][:, i, :], u_bf[:, i, :], [tk["kd"], u_bf], inc=(i == G - 1))
            P.tt("dve", S_f, S_f[:], S_f[:], bc_last(cols[:, t, 3, hsl], 128), ALU.mult, [S_f, cols])
            P.tt("dve", S_f, S_f[:], S_f[:], f4(pf[1]), ALU.add, [S_f, pf[1]])
            P.cp("pool", S_b, S_b[:], S_f[:], [S_f])
            for i in range(G):
                P.act(junk, junk[:], pf[0][:, i * 128:(i + 1) * 128], AF.Square, [pf[0]], accum=sc[:, 32 + i:33 + i], extra_w=[sc])
            P.tsc("dve", sc, sc[:, 36:40], sc[:, 32:36], 1.0 / DH, EPS, ALU.mult, ALU.add, [sc])
            P.act(sc, sc[:, 36:40], sc[:, 36:40], AF.Ln, [sc])
            P.act(sc, sc[:, 36:40], sc[:, 36:40], AF.Exp, [sc], scale=-0.5)
            P.tt("dve", o_f, o_f[:], f4(pf[0]), bc_last(sc[:, 36:40], 128), ALU.mult, [pf[0], sc])
            P.tt("dve", o_f, o_f[:], o_f[:], bc_mid(gnorm[:], G), ALU.mult, [o_f, gnorm])
            P.tt("dve", on_b, on_b[:], o_f[:], gzs[:], ALU.mult, [o_f, gzs])
            for i in range(G):
                P.tr(pb[1], pb[1][:, i * 128:(i + 1) * 128], on_b[:, i, :], ident_b, [on_b, cbb], inc=(i == G - 1))
            P.act(ygdnT, ygdnT[:, g * G:(g + 1) * G, cs], pb[1][:, 0:512].rearrange("p (a b) -> p a b", a=G), AF.Copy, [pb[1]], key=(g, t))

    TTfin = {}
    early(0)
    for t in range(NT):
        rec(t)
        if t + 1 < NT:
            early(t + 1)
        late(t)


def tail_phase(P, AR, L, s):
    T = L["T"]
    hnT, yfoxT, ygdnT, cbb, ident_b, cwf = L["hnT"], L["yfoxT"], L["ygdnT"], L["cbb"], L["ident_b"], L["cwf"]
    pf, pb, wtt, dG, dOut = L["pf"], L["pb"], L["wtt"], L["dG"], L["dOut"]
    wb_in, wb_bf, wb_bg, wb_out, wb_up, wb_dn = L["wb_in"], L["wb_bf"], L["wb_bg"], L["wb_out"], L["wb_up"], L["wb_dn"]
    nrm_d, x_d, out_d, wslice, rmsnorm_tile = L["nrm_d"], L["x_d"], L["out_d"], L["wslice"], L["rmsnorm_tile"]
    BWT = min(512, T)
    QT = BWT // 128
    NBT = T // BWT
    AR.reset()
    A = AR.alloc
    gam2 = A("gam2", [128, D], F32)
    gam3 = A("gam3", [128, D], F32)
    h1 = A("h1", [128, QT, D], F32)
    xres = A("xres", [128, D], F32)
    hn2b = A("hn2b", [128, D], BF16)
    hn2T = A("hn2T", [128, 8, BWT], BF16)
    aT = A("aT", [128, NF, BWT], BF16)
    halo = A("fhalo", [128, 44, 2], F32)
    st4 = A("st4t", [128, 8], F32)
    base = AR.off
    NWM = 2
    wm = [[A("wm%d_%d" % (k, i), [128, 8, 128], BF16) for i in range(4)] for k in range(NWM)]
    woc = [A("woc%d" % i, [128, D], BF16) for i in range(3)]
    yT = A("yT", [128, 8, BWT], BF16)
    sg = [A("sg%d" % i, [128, BWT], F32) for i in range(2)]
    end_early = AR.off
    AR.off = base
    NWU, NWD = 4, 4
    wupc = [A("wupc%d" % i, [128, 8, 256], BF16) for i in range(NWU)]
    stage = [A("fstage%d" % i, [128, 2 + BWT], F32) for i in range(2)]
    acc = [A("facc%d" % i, [128, BWT], F32) for i in range(2)]
    sil = A("sil", [128, BWT], F32)
    wdc = [A("wdc%d" % i, [128, D], BF16) for i in range(NWD)]
    ot = A("ot", [128, D], F32)
    AR.off = max(AR.off, end_early)
    sqj = xres
    P.dma("sp", gam2, gam2[:], nrm_d[1:2, :].partition_broadcast(128), [], dG)
    P.dma("sp", gam3, gam3[:], nrm_d[2:3, :].partition_broadcast(128), [], dG)
    cnt = {"wm": 0, "wo": 0, "wu": 0, "wd": 0}
    for b in range(NBT):
        cs = slice(b * BWT, (b + 1) * BWT)
        for dc in range(8):
            srcs = ((wb_in, OFF_GF + dc * 128, hnT), (wb_in, OFF_GG + dc * 128, hnT), (wb_bf, dc * 128, yfoxT), (wb_bg, dc * 128, ygdnT))
            keys = ("in", "in", "bf", "bg")
            wset = wm[cnt["wm"] % NWM]
            cnt["wm"] += 1
            for i, (wb, c0, _) in enumerate(srcs):
                P.dma("sp", wset[i], wset[i][:], wslice(wb, c0, 128), [wtt[keys[i]]], True)
            for i, (_, _, act_) in enumerate(srcs):
                for c in range(8):
                    P.mm(pf[i], pf[i][:, 0:BWT], wset[i][:, c, :], act_[:, c, cs], [wset[i], act_], start=(c == 0), stop=(c == 7))
            P.act(sg[0], sg[0][:], pf[0][:, 0:BWT], AF.Sigmoid, [pf[0]])
            P.act(sg[1], sg[1][:], pf[1][:, 0:BWT], AF.Sigmoid, [pf[1]])
            P.tt("dve", sg[0], sg[0][:], sg[0][:], pf[2][:, 0:BWT], ALU.mult, [sg[0], pf[2]])
            P.tt("dve", sg[1], sg[1][:], sg[1][:], pf[3][:, 0:BWT], ALU.mult, [sg[1], pf[3]])
            P.tt("dve", yT, yT[:, dc, :], sg[0][:], sg[1][:], ALU.add, [sg[0], sg[1]], key=dc)
        for c in range(8):
            w = woc[cnt["wo"] % 3]
            cnt["wo"] += 1
            P.dma("sp", w, w[:], wb_out[c * 128:(c + 1) * 128, :], [wtt["out"]], True)
            for ti in range(QT):
                for hf in range(2):
                    pst = pf[ti * 2 + hf]
                    P.mm(pst, pst[:, 0:512], yT[:, c, ti * 128:(ti + 1) * 128], w[:, hf * 512:(hf + 1) * 512], [yT, w], start=(c == 0), stop=(c == 7))
        for ti in range(QT):
            tok0 = b * BWT + ti * 128
            P.dma("sp", xres, xres[:], x_d[s, tok0:tok0 + 128, :], [], True)
            for hf in range(2):
                pst = pf[ti * 2 + hf]
                P.tt("dve", h1, h1[:, ti, hf * 512:(hf + 1) * 512], xres[:, hf * 512:(hf + 1) * 512], pst[:, 0:512], ALU.add, [xres, pst], key=(ti, hf))
        for ti in range(QT):
            rmsnorm_tile(h1[:, ti, :], h1, gam2[:], gam2, st4, sqj, hn2b)
            for half in range(2):
                pbt = pb[half]
                for c in range(4):
                    cc = half * 4 + c
                    P.tr(pbt, pbt[:, c * 128:(c + 1) * 128], hn2b[:, cc * 128:(cc + 1) * 128], ident_b, [hn2b, cbb], key=c)
                P.act(hn2T, hn2T[:, half * 4:half * 4 + 4, ti * 128:(ti + 1) * 128],
                      pbt[:, 0:512].rearrange("p (c n) -> p c n", c=4), AF.Copy, [pbt], key=(ti, half))
        P.barrier()
        for f in range(NF):
            w = wupc[cnt["wu"] % NWU]
            cnt["wu"] += 1
            P.dma("sp", w, w[:, :, 0:128], wslice(wb_up, f * 128, 128), [wtt["up"]], True, key=0)
            P.dma("sp", w, w[:, :, 128:256], wslice(wb_up, DFF + f * 128, 128), [wtt["up"]], True, key=1)
            for part in range(2):
                pst = pf[part]
                stg = stage[part]
                ac = acc[part]
                ch = part * NF + f
                for c in range(8):
                    P.mm(pst, pst[:, 0:BWT], w[:, c, part * 128:(part + 1) * 128], hn2T[:, c, :], [w, hn2T], start=(c == 0), stop=(c == 7))
                P.act(stg, stg[:, 2:2 + BWT], pst[:, 0:BWT], AF.Copy, [pst], key='x')
                if b == 0:
                    P.op("pool", lambda e, stg=stg: e.memset(stg[:, 0:2], 0.0), writes=[stg], key='h')
                else:
                    P.cp("pool", stg, stg[:, 0:2], halo[:, ch, :], [halo], key='h')
                P.cp("pool", halo, halo[:, ch, :], stg[:, BWT:BWT + 2], [stg], key=ch)
                P.act(ac, ac[:], pst[:, 0:BWT], AF.Copy, [pst, cwf], scale=cwf[:, ch, 2:3])
                for tap in (1, 0):
                    P.stt(ac, ac[:], stg[:, tap:tap + BWT], cwf[:, ch, tap:tap + 1], ac[:], ALU.mult, ALU.add, [stg, cwf, ac])
            P.act(sil, sil[:], acc[0][:], AF.Silu, [acc[0]])
            P.tt("dve", aT, aT[:, f, :], sil[:], acc[1][:], ALU.mult, [sil, acc[1]], key=f)
        for f in range(NF):
            w = wdc[cnt["wd"] % NWD]
            cnt["wd"] += 1
            P.dma("sp", w, w[:], wb_dn[f * 128:(f + 1) * 128, :], [wtt["dn"]], True)
            for ti in range(QT):
                for hf in range(2):
                    pst = pf[ti * 2 + hf]
                    P.mm(pst, pst[:, 0:512], aT[:, f, ti * 128:(ti + 1) * 128], w[:, hf * 512:(hf + 1) * 512], [aT, w], start=(f == 0), stop=(f == NF - 1))
        for ti in range(QT):
            tok0 = b * BWT + ti * 128
            for hf in range(2):
                pst = pf[ti * 2 + hf]
                P.tt("dve", ot, ot[:, hf * 512:(hf + 1) * 512], h1[:, ti, hf * 512:(hf + 1) * 512], pst[:, 0:512], ALU.add, [h1, pst], key=hf)
            rmsnorm_tile(ot[:], ot, gam3[:], gam3, st4, sqj, ot)
            P.dma("sp", None, out_d[s, tok0:tok0 + 128, :], ot[:], [ot], dOut, write=False)
        if b < NBT - 1:
            P.barrier()


def make_in_maps(inp, NSEQ, ncores, same=False):
    cb, sel_last, selbig = host_consts()
    f = lambda a: np.ascontiguousarray(np.asarray(a, dtype=np.float32))
    common = {
        "w_in": f(inp["w_in"][0]), "w_bf": f(inp["w_branch_fox"][0]), "w_bg": f(inp["w_branch_gdn"][0]),
        "w_out": f(inp["w_out"][0]), "w_up": f(inp["w_up"][0]), "w_down": f(inp["w_down"][0]),
        "norms": f(np.stack([np.asarray(inp["norm_mix"])[0], np.asarray(inp["norm_ffn"])[0], np.asarray(inp["norm_final"])], 0)),
        "gdn_norm": f(inp["gdn_norm"]),
        "cwg": f(np.asarray(inp["gdn_conv_w"])[0].T.reshape(24, 128, 4).transpose(1, 0, 2)),
        "cwf": f(np.asarray(inp["ffn_conv_w"])[0].T.reshape(44, 128, 3).transpose(1, 0, 2)),
        "hv": f(np.stack([np.asarray(inp["fox_f_bias"])[0], np.asarray(inp["gdn_a_log"])[0], np.asarray(inp["gdn_dt_bias"])[0]], 1)),
        "cb": cb, "sel_last": sel_last, "selbig": selbig,
    }
    x = np.asarray(inp["x"], np.float32)
    maps = []
    for c in range(ncores):
        m = dict(common)
        m["x"] = f(x[0:NSEQ] if same else x[c * NSEQ:(c + 1) * NSEQ])
        maps.append(m)
    return maps


_NC_CACHE = {}


def kernel(**inputs):
    x = np.asarray(inputs["x"])
    B, T, _ = x.shape
    ncores = 8
    NSEQ = B // ncores
    key = (T, NSEQ)
    if key not in _NC_CACHE:
        _NC_CACHE[key] = build(T, NSEQ)
    nc = _NC_CACHE[key]
    maps = make_in_maps(inputs, NSEQ, ncores)
    res = run_bass_kernel_spmd(nc, maps, core_ids=list(range(ncores)))
    out = np.concatenate([np.asarray(r["out"]) for r in res.results], axis=0)
    return out.astype(np.float32)
```
